# Optimizing a Trainium2 kernel written in Bass

```python
import math
import jax
import jax.numpy as jnp
from jax import lax
import numpy as np

D_MODEL = 1024
BATCH = 32
SEQ = 256
DEPTH = 4
DEC_BATCH = 4
DEC_SEQ = 1024
PAST_LEN = 256

GRID_W = 64
D_MIX = D_MODEL
MIX_W = D_MIX // 4
CHUNK = 128
EPS = 1e-6
SSD_HEAD_DIM = 64
SSD_HEADS = MIX_W // SSD_HEAD_DIM
SSD_GROUPS = 2
SSD_STATE = 64
SSD_CONV = 4
SSD_XBC = MIX_W + 2 * SSD_GROUPS * SSD_STATE
SSD_COLS = MIX_W + SSD_XBC + 2 * SSD_HEADS
RET_HEADS = 4
RET_HEAD_DIM = MIX_W // RET_HEADS
ROPE_BASE = 10000.0
RET_COLS = 4 * MIX_W
S5_GROUP_CH = 16
S5_GROUPS = MIX_W // S5_GROUP_CH
S5_STATE = 64
S5_COLS = MIX_W
LRU_BLOCKS = 4
LRU_BLOCK = MIX_W // LRU_BLOCKS
LRU_CONV = 4
LRU_C = 8.0
LRU_COLS = 2 * MIX_W
IN_COLS = SSD_COLS + RET_COLS + S5_COLS + LRU_COLS
D_FF = ((8 * D_MODEL // 3 + 127) // 128) * 128
FFN_CONV = 3

kernel_name = 'hybrid_diffusion_parallel_ssm_step'


def rmsnorm(x, w):
    xf = x.astype(jnp.float32)
    return xf * lax.rsqrt(jnp.mean(xf * xf, axis=-1, keepdims=True) + EPS) * w.astype(jnp.float32)


def dwconv(x, w, b):
    k = w.shape[0]
    y = lax.conv_general_dilated(x, w[:, None, :].astype(x.dtype), (1,), [(k // 2, k - 1 - k // 2)],
                                 dimension_numbers=('NWC', 'WIO', 'NWC'), feature_group_count=x.shape[-1])
    return y + b


def flip(t):
    return jnp.flip(t, axis=1)


def chunked_decay_scan(q, k, v, log_a, s0):
    b, L, h, n = q.shape
    p = v.shape[-1]
    nc = L // CHUNK
    f32 = jnp.float32
    qc = q.astype(f32).reshape(b, nc, CHUNK, h, n)
    kc = k.astype(f32).reshape(b, nc, CHUNK, h, n)
    vc = v.astype(f32).reshape(b, nc, CHUNK, h, p)
    cum = jnp.cumsum(log_a.astype(f32).reshape(b, nc, CHUNK, h), axis=2)
    lower = jnp.tril(jnp.ones((CHUNK, CHUNK), bool))[None, None, :, :, None]
    seg = cum[:, :, :, None, :] - cum[:, :, None, :, :]
    decay = jnp.where(lower, jnp.exp(jnp.where(lower, seg, 0.0)), 0.0)
    scores = jnp.einsum('bclhn,bcshn->bclsh', qc, kc) * decay
    y = jnp.einsum('bclsh,bcshp->bclhp', scores, vc)
    tail = jnp.exp(cum[:, :, -1:, :] - cum)
    chunk_states = jnp.einsum('bcshn,bcsh,bcshp->bchnp', kc, tail, vc)
    chunk_decay = jnp.exp(cum[:, :, -1, :])

    def step(s, inp):
        cs, cd = inp
        return cd[..., None, None] * s + cs, s

    s_final, s_enter = lax.scan(step, s0.astype(f32),
                                (jnp.moveaxis(chunk_states, 1, 0), jnp.moveaxis(chunk_decay, 1, 0)))
    s_enter = jnp.moveaxis(s_enter, 0, 1)
    y = y + jnp.einsum('bclhn,bchnp,bclh->bclhp', qc, s_enter, jnp.exp(cum))
    return y.reshape(b, L, h, p), s_final


def diag_scan(a, u, s0):
    def comb(e1, e2):
        a1, b1 = e1
        a2, b2 = e2
        return a1 * a2, a2 * b1 + b2
    a_cum, b_cum = lax.associative_scan(comb, (a, u), axis=1)
    h = b_cum + a_cum * s0[:, None]
    return h, h[:, -1]


def complex_diag_scan(a_re, a_im, u_re, u_im, s_re, s_im):
    def comb(e1, e2):
        ar1, ai1, br1, bi1 = e1
        ar2, ai2, br2, bi2 = e2
        return (ar2 * ar1 - ai2 * ai1, ar2 * ai1 + ai2 * ar1,
                ar2 * br1 - ai2 * bi1 + br2, ar2 * bi1 + ai2 * br1 + bi2)
    ar, ai, br, bi = lax.associative_scan(comb, (a_re, a_im, u_re, u_im), axis=1)
    s_re = s_re[:, None]
    s_im = s_im[:, None]
    return br + ar * s_re - ai * s_im, bi + ar * s_im + ai * s_re


def grid_rotary(t):
    L = t.shape[1]
    rows = L // GRID_W
    row = jnp.repeat(jnp.arange(rows, dtype=jnp.float32), GRID_W)
    col = jnp.tile(jnp.arange(GRID_W, dtype=jnp.float32), rows)
    nf = RET_HEAD_DIM // 4
    inv = ROPE_BASE ** (-jnp.arange(nf, dtype=jnp.float32) / nf)
    ang = jnp.stack([row, col], axis=-1)[:, :, None] * inv
    cos = jnp.cos(ang)[None, :, None]
    sin = jnp.sin(ang)[None, :, None]
    tr = t.reshape(t.shape[:3] + (2, 2, nf))
    re, im = tr[..., 0, :], tr[..., 1, :]
    out = jnp.stack([re * cos - im * sin, re * sin + im * cos], axis=-2)
    return out.reshape(t.shape)


def retention_log_decay(offset):
    return jnp.log1p(-2.0 ** (-5.0 - offset - jnp.arange(RET_HEADS, dtype=jnp.float32)))


def ssd_mixer(p, init, conv_w, conv_b, dt_bias, a_log, d_skip, norm_w):
    b, L, _ = p.shape
    z, xbc, dt = jnp.split(p, [MIX_W, MIX_W + SSD_XBC], axis=-1)
    xbc = jax.nn.silu(dwconv(xbc, conv_w, conv_b))
    x, bm, cm = jnp.split(xbc, [MIX_W, MIX_W + SSD_GROUPS * SSD_STATE], axis=-1)
    x = x.reshape(b, L, SSD_HEADS, SSD_HEAD_DIM)
    rep = SSD_HEADS // SSD_GROUPS
    bm = jnp.repeat(bm.reshape(b, L, SSD_GROUPS, SSD_STATE), rep, axis=2)
    cm = jnp.repeat(cm.reshape(b, L, SSD_GROUPS, SSD_STATE), rep, axis=2)
    dt = jax.nn.softplus(dt.reshape(b, L, 2, SSD_HEADS) + dt_bias)
    a = -jnp.exp(a_log.astype(jnp.float32))
    y_f, s_f = chunked_decay_scan(cm, bm, x * dt[:, :, 0, :, None], dt[:, :, 0] * a[0], init[:, 0])
    y_b, s_b = chunked_decay_scan(flip(cm), flip(bm), flip(x * dt[:, :, 1, :, None]),
                                  flip(dt[:, :, 1] * a[1]), init[:, 1])
    y = y_f + flip(y_b) + d_skip[:, None] * x
    y = y.reshape(b, L, MIX_W) * jax.nn.silu(z)
    return rmsnorm(y, norm_w), jnp.stack([s_f, s_b], axis=1)


def retention_mixer(p, init, gn_w, rotary):
    b, L, _ = p.shape
    q, k, v, g = jnp.split(p, 4, axis=-1)
    shp = (b, L, RET_HEADS, RET_HEAD_DIM)
    q, k, v = q.reshape(shp), k.reshape(shp), v.reshape(shp)
    if rotary:
        q, k = grid_rotary(q), grid_rotary(k)
    k = k * RET_HEAD_DIM ** -0.5
    lg_f = jnp.broadcast_to(retention_log_decay(0.0), (b, L, RET_HEADS))
    lg_b = jnp.broadcast_to(retention_log_decay(0.5), (b, L, RET_HEADS))
    y_f, s_f = chunked_decay_scan(q, k, v, lg_f, init[:, 0])
    y_b, s_b = chunked_decay_scan(flip(q), flip(k), flip(v), lg_b, init[:, 1])
    y = y_f + flip(y_b)
    mu = jnp.mean(y, axis=-1, keepdims=True)
    var = jnp.mean(jnp.square(y - mu), axis=-1, keepdims=True)
    y = ((y - mu) * lax.rsqrt(var + EPS)).reshape(b, L, MIX_W) * gn_w
    return jax.nn.silu(g) * y, jnp.stack([s_f, s_b], axis=1)


def s5_mixer(u, init, lam_re, lam_im, log_step, b_re, b_im, c_re, c_im, d_skip, glu_w):
    f32 = jnp.float32
    bsz, L, _ = u.shape
    ug = u.reshape(bsz, L, S5_GROUPS, S5_GROUP_CH)
    b_re, b_im, c_re, c_im = b_re.astype(f32), b_im.astype(f32), c_re.astype(f32), c_im.astype(f32)

    def direction(ud, dr):
        step = jnp.exp(log_step[dr].astype(f32))[:, None]
        lr, li = lam_re[dr].astype(f32), lam_im[dr].astype(f32)
        mag = jnp.exp(lr * step)
        abar_re, abar_im = mag * jnp.cos(li * step), mag * jnp.sin(li * step)
        den = lr * lr + li * li
        nr, ni = abar_re - 1.0, abar_im
        coef_re = ((nr * lr + ni * li) / den)[..., None]
        coef_im = ((ni * lr - nr * li) / den)[..., None]
        bbar_re = coef_re * b_re - coef_im * b_im
        bbar_im = coef_re * b_im + coef_im * b_re
        bu_re = jnp.einsum('blgm,gnm->blgn', ud, bbar_re)
        bu_im = jnp.einsum('blgm,gnm->blgn', ud, bbar_im)
        x_re, x_im = complex_diag_scan(jnp.broadcast_to(abar_re, bu_re.shape), jnp.broadcast_to(abar_im, bu_im.shape),
                                       bu_re, bu_im, init[:, dr, :, :, 0], init[:, dr, :, :, 1])
        y = jnp.einsum('blgn,gmn->blgm', x_re, c_re) - jnp.einsum('blgn,gmn->blgm', x_im, c_im)
        return y, jnp.stack([x_re[:, -1], x_im[:, -1]], axis=-1)

    y_f, s_f = direction(ug, 0)
    y_b, s_b = direction(flip(ug), 1)
    y = (y_f + flip(y_b)).reshape(bsz, L, MIX_W) + d_skip * u
    val, gate = jnp.split(y @ glu_w, 2, axis=-1)
    return val * jax.nn.sigmoid(gate), jnp.stack([s_f, s_b], axis=1)


def lru_mixer(p, init, conv_w, conv_b, w_a, b_a, w_x, b_x, lam):
    xb, gb = jnp.split(p, 2, axis=-1)
    xb = dwconv(xb, conv_w, conv_b)
    bsz, L, _ = xb.shape

    def direction(xd, dr):
        xr = xd.reshape(bsz, L, LRU_BLOCKS, LRU_BLOCK)
        r = jax.nn.sigmoid(jnp.einsum('blhi,hij->blhj', xr, w_a[dr]) + b_a[dr])
        i = jax.nn.sigmoid(jnp.einsum('blhi,hij->blhj', xr, w_x[dr]) + b_x[dr])
        log_a = -LRU_C * r * jax.nn.softplus(-lam[dr].astype(jnp.float32).reshape(LRU_BLOCKS, LRU_BLOCK))
        a = jnp.exp(log_a)
        u = jnp.sqrt(-jnp.expm1(2.0 * log_a)) * (i * xr)
        return diag_scan(a.reshape(bsz, L, MIX_W), u.reshape(bsz, L, MIX_W), init[:, dr])

    h_f, s_f = direction(xb, 0)
    h_b, s_b = direction(flip(xb), 1)
    y = (h_f + flip(h_b)) * jax.nn.gelu(gb)
    return y, jnp.stack([s_f, s_b], axis=1)


def conv_ffn(h, w_up, conv_w, conv_b, w_down):
    g, u = jnp.split(h @ w_up, 2, axis=-1)
    g = dwconv(g, conv_w, conv_b)
    return (jax.nn.silu(g) * u) @ w_down


def trunk_layer(x, cond, init, lp, rotary):
    mod = jax.nn.silu(cond.astype(jnp.float32)) @ lp['mod_w'] + lp['mod_b']
    sh1, sc1, g1, sh2, sc2, g2 = jnp.split(mod[:, None, :], 6, axis=-1)
    h = rmsnorm(x, lp['norm1_w']) * (1.0 + sc1) + sh1
    proj = h @ lp['w_in']
    o1 = SSD_COLS
    o2 = o1 + RET_COLS
    o3 = o2 + S5_COLS
    pa, pb, pc, pd = jnp.split(proj, [o1, o2, o3], axis=-1)
    ya, st_a = ssd_mixer(pa, init[0], lp['ssd_conv_w'], lp['ssd_conv_b'], lp['ssd_dt_bias'],
                         lp['ssd_a_log'], lp['ssd_d'], lp['ssd_norm_w'])
    yb, st_b = retention_mixer(pb, init[1], lp['ret_gn_w'], rotary)
    yc, st_c = s5_mixer(pc, init[2], lp['s5_lam_re'], lp['s5_lam_im'], lp['s5_log_step'], lp['s5_b_re'],
                        lp['s5_b_im'], lp['s5_c_re'], lp['s5_c_im'], lp['s5_d'], lp['s5_glu_w'])
    yd, st_d = lru_mixer(pd, init[3], lp['lru_conv_w'], lp['lru_conv_b'], lp['lru_wa'], lp['lru_ba'],
                         lp['lru_wx'], lp['lru_bx'], lp['lru_lambda'])
    y = jnp.concatenate([ya, yb, yc, yd], axis=-1) @ lp['w_out']
    x = x + g1 * y
    h = rmsnorm(x, lp['norm2_w']) * (1.0 + sc2) + sh2
    x = x + g2 * conv_ffn(h, lp['ffn_w_up'], lp['ffn_conv_w'], lp['ffn_conv_b'], lp['ffn_w_down'])
    return x, (st_a, st_b, st_c, st_d)


def setup_inputs(seed: int = 0) -> dict:
    key = jax.random.key(seed)
    ks = iter(jax.random.split(key, 48))
    f32 = jnp.float32

    def nrm(shape, scale):
        return jax.random.normal(next(ks), shape, f32) * scale

    def unif(shape, lo, hi):
        return jax.random.uniform(next(ks), shape, f32, minval=lo, maxval=hi)

    inp = {}
    inp['x_prompt'] = nrm((BATCH, SEQ, D_MODEL), 1.0)
    inp['x_sample'] = nrm((DEC_BATCH, DEC_SEQ, D_MODEL), 1.0)
    inp['state_ssd'] = nrm((DEC_BATCH, DEPTH, 2, SSD_HEADS, SSD_STATE, SSD_HEAD_DIM), 0.3)
    inp['state_ret'] = nrm((DEC_BATCH, DEPTH, 2, RET_HEADS, RET_HEAD_DIM, RET_HEAD_DIM), 0.3)
    inp['state_s5'] = nrm((DEC_BATCH, DEPTH, 2, S5_GROUPS, S5_STATE, 2), 0.1)
    inp['state_lru'] = nrm((DEC_BATCH, DEPTH, 2, MIX_W), 0.5)
    inp['c'] = nrm((DEC_BATCH, D_MODEL), 1.0)
    inp['c_ctx'] = nrm((D_MODEL,), 1.0)
    inp['norm1_w'] = 1.0 + nrm((DEPTH, D_MODEL), 0.02)
    inp['norm2_w'] = 1.0 + nrm((DEPTH, D_MODEL), 0.02)
    inp['mod_w'] = nrm((DEPTH, D_MODEL, 6 * D_MODEL), 0.5 * D_MODEL ** -0.5)
    inp['mod_b'] = nrm((DEPTH, 6 * D_MODEL), 0.02)
    inp['w_in'] = nrm((DEPTH, D_MODEL, IN_COLS), D_MODEL ** -0.5)
    inp['ssd_conv_w'] = nrm((DEPTH, SSD_CONV, SSD_XBC), SSD_CONV ** -0.5)
    inp['ssd_conv_b'] = nrm((DEPTH, SSD_XBC), 0.02)
    dt0 = jnp.exp(unif((DEPTH, 2, SSD_HEADS), math.log(1e-3), math.log(1e-1)))
    inp['ssd_dt_bias'] = dt0 + jnp.log(-jnp.expm1(-dt0))
    inp['ssd_a_log'] = jnp.log(unif((DEPTH, 2, SSD_HEADS), 1.0, 16.0))
    inp['ssd_d'] = 1.0 + nrm((DEPTH, SSD_HEADS), 0.02)
    inp['ssd_norm_w'] = 1.0 + nrm((DEPTH, MIX_W), 0.02)
    inp['ret_gn_w'] = 1.0 + nrm((DEPTH, MIX_W), 0.02)
    inp['s5_lam_re'] = -0.5 + nrm((DEPTH, 2, S5_GROUPS, S5_STATE), 0.01)
    inp['s5_lam_im'] = math.pi * jnp.arange(S5_STATE, dtype=f32) + nrm((DEPTH, 2, S5_GROUPS, S5_STATE), 0.01)
    inp['s5_log_step'] = unif((DEPTH, 2, S5_GROUPS), math.log(1e-3), math.log(1e-1))
    inp['s5_b_re'] = nrm((DEPTH, S5_GROUPS, S5_STATE, S5_GROUP_CH), (2 * S5_GROUP_CH) ** -0.5)
    inp['s5_b_im'] = nrm((DEPTH, S5_GROUPS, S5_STATE, S5_GROUP_CH), (2 * S5_GROUP_CH) ** -0.5)
    inp['s5_c_re'] = nrm((DEPTH, S5_GROUPS, S5_GROUP_CH, S5_STATE), S5_STATE ** -0.5)
    inp['s5_c_im'] = nrm((DEPTH, S5_GROUPS, S5_GROUP_CH, S5_STATE), S5_STATE ** -0.5)
    inp['s5_d'] = nrm((DEPTH, MIX_W), 1.0)
    inp['s5_glu_w'] = nrm((DEPTH, MIX_W, 2 * MIX_W), MIX_W ** -0.5)
    inp['lru_conv_w'] = nrm((DEPTH, LRU_CONV, MIX_W), LRU_CONV ** -0.5)
    inp['lru_conv_b'] = nrm((DEPTH, MIX_W), 0.02)
    inp['lru_wa'] = nrm((DEPTH, 2, LRU_BLOCKS, LRU_BLOCK, LRU_BLOCK), LRU_BLOCK ** -0.5)
    inp['lru_ba'] = nrm((DEPTH, 2, LRU_BLOCKS, LRU_BLOCK), 0.02)
    inp['lru_wx'] = nrm((DEPTH, 2, LRU_BLOCKS, LRU_BLOCK, LRU_BLOCK), LRU_BLOCK ** -0.5)
    inp['lru_bx'] = nrm((DEPTH, 2, LRU_BLOCKS, LRU_BLOCK), 0.02)
    sig = unif((DEPTH, 2, MIX_W), 0.9, 0.999) ** (1.0 / LRU_C)
    inp['lru_lambda'] = jnp.log(sig) - jnp.log1p(-sig)
    inp['w_out'] = nrm((DEPTH, D_MIX, D_MODEL), D_MIX ** -0.5)
    inp['ffn_w_up'] = nrm((DEPTH, D_MODEL, 2 * D_FF), D_MODEL ** -0.5)
    inp['ffn_conv_w'] = nrm((DEPTH, FFN_CONV, D_FF), FFN_CONV ** -0.5)
    inp['ffn_conv_b'] = nrm((DEPTH, D_FF), 0.02)
    inp['ffn_w_down'] = nrm((DEPTH, D_FF, D_MODEL), D_FF ** -0.5)
    inp['final_norm_w'] = 1.0 + nrm((D_MODEL,), 0.02)
    return inp


def reference(x_prompt, x_sample, state_ssd, state_ret, state_s5, state_lru, c, c_ctx,
              norm1_w, norm2_w, mod_w, mod_b, w_in, ssd_conv_w, ssd_conv_b, ssd_dt_bias, ssd_a_log,
              ssd_d, ssd_norm_w, ret_gn_w, s5_lam_re, s5_lam_im, s5_log_step, s5_b_re, s5_b_im,
              s5_c_re, s5_c_im, s5_d, s5_glu_w, lru_conv_w, lru_conv_b, lru_wa, lru_ba, lru_wx, lru_bx,
              lru_lambda, w_out, ffn_w_up, ffn_conv_w, ffn_conv_b, ffn_w_down, final_norm_w):
    f32 = jnp.float32
    bp = x_prompt.shape[0]
    xp = x_prompt.astype(f32)
    xs = x_sample.astype(f32)
    zero_init = (jnp.zeros((bp, 2, SSD_HEADS, SSD_STATE, SSD_HEAD_DIM), f32),
                 jnp.zeros((bp, 2, RET_HEADS, RET_HEAD_DIM, RET_HEAD_DIM), f32),
                 jnp.zeros((bp, 2, S5_GROUPS, S5_STATE, 2), f32),
                 jnp.zeros((bp, 2, MIX_W), f32))
    new_ssd, new_ret, new_s5, new_lru = [], [], [], []
    for l in range(DEPTH):
        lp = {'norm1_w': norm1_w[l], 'norm2_w': norm2_w[l], 'mod_w': mod_w[l], 'mod_b': mod_b[l],
              'w_in': w_in[l], 'ssd_conv_w': ssd_conv_w[l], 'ssd_conv_b': ssd_conv_b[l],
              'ssd_dt_bias': ssd_dt_bias[l], 'ssd_a_log': ssd_a_log[l], 'ssd_d': ssd_d[l],
              'ssd_norm_w': ssd_norm_w[l], 'ret_gn_w': ret_gn_w[l], 's5_lam_re': s5_lam_re[l],
              's5_lam_im': s5_lam_im[l], 's5_log_step': s5_log_step[l], 's5_b_re': s5_b_re[l],
              's5_b_im': s5_b_im[l], 's5_c_re': s5_c_re[l], 's5_c_im': s5_c_im[l], 's5_d': s5_d[l],
              's5_glu_w': s5_glu_w[l], 'lru_conv_w': lru_conv_w[l], 'lru_conv_b': lru_conv_b[l],
              'lru_wa': lru_wa[l], 'lru_ba': lru_ba[l], 'lru_wx': lru_wx[l], 'lru_bx': lru_bx[l],
              'lru_lambda': lru_lambda[l], 'w_out': w_out[l], 'ffn_w_up': ffn_w_up[l],
              'ffn_conv_w': ffn_conv_w[l], 'ffn_conv_b': ffn_conv_b[l], 'ffn_w_down': ffn_w_down[l]}
        xp, st = trunk_layer(xp, c_ctx[None, :], zero_init, lp, False)
        new_ssd.append(st[0])
        new_ret.append(st[1])
        new_s5.append(st[2])
        new_lru.append(st[3])
        cache = (state_ssd[:, l], state_ret[:, l], state_s5[:, l], state_lru[:, l])
        xs, _ = trunk_layer(xs, c, cache, lp, True)
    y_prompt = rmsnorm(xp, final_norm_w)
    y_sample = rmsnorm(xs, final_norm_w)
    new_state_ssd = jnp.stack(new_ssd, axis=1)
    new_state_ret = jnp.stack(new_ret, axis=1)
    new_state_s5 = jnp.stack(new_s5, axis=1)
    new_state_lru = jnp.stack(new_lru, axis=1)
    return (y_prompt, y_sample, new_state_ssd, new_state_ret, new_state_s5, new_state_lru)
```

```python
import math
from contextlib import ExitStack
import numpy as np
import concourse.bass as bass
import concourse.mybir as mybir
from concourse.bass_utils import run_bass_kernel_spmd

F32 = mybir.dt.float32
BF16 = mybir.dt.bfloat16
I32 = mybir.dt.int32
ALU = mybir.AluOpType
AF = mybir.ActivationFunctionType

ENABLED = {'ssd', 'ret', 's5', 'lru', 'ffn'}
DEPTH = 4
DEBUG_TAPS = False
LAST = {}

L = 4; D = 1024; KT = 8; NT = 1536; NSEG = 6; SEG = 256; NCH = 12; CH = 128
DFF = 2816; NJ = 22
EPS = 1e-6
ENGS = ("pe", "act", "dve", "pool", "sp")
N_DMA_SLOTS = 32


class Prog:
    def __init__(self, nc):
        self.nc = nc
        self.ops = {e: [] for e in ENGS}
        self.cnt = {}
        self.last_w = {}
        self.readers = {}
        self.known = {e: {} for e in ENGS}
        self.dma_rr = 0
        self.dma_rr_q = {}
        self.out_dmas = []

    EXPAND = {"T3": [("T3", 0), ("T3", 1), ("T3", 2)],
              "T4": [("T4", "D", 0), ("T4", "D", 1), ("T4", "Ds"), ("T4", "q"), ("T4", "P"), ("T4", "b", 0), ("T4", "b", 1), ("T4", "b", 2)],
              "T7": [("T7", 0), ("T7", 1), ("T7", 2)]}

    @classmethod
    def _keys(cls, lst):
        out = []
        for k in lst:
            if k is None:
                continue
            if not isinstance(k, (str, tuple)):
                k = k.tensor.name if hasattr(k, 'tensor') else k.name
            if k in cls.EXPAND:
                out.extend(cls.EXPAND[k])
            else:
                out.append(k)
        return out

    def _deps(self, eng, reads, writes):
        need = {}

        def add(sv):
            if sv is None:
                return
            s, v = sv
            if need.get(s, 0) < v:
                need[s] = v
        for k in reads:
            add(self.last_w.get(k))
        for k in writes:
            add(self.last_w.get(k))
            for sv in self.readers.get(k, ()):
                add(sv)
        waits = []
        kn = self.known[eng]
        for s, v in need.items():
            if kn.get(s, 0) >= v:
                continue
            kn[s] = v
            waits.append((s, v))
        return waits

    def _record(self, semkey, val, reads, writes):
        for k in reads:
            self.readers.setdefault(k, []).append((semkey, val))
        for k in writes:
            self.last_w[k] = (semkey, val)
            self.readers[k] = []

    def op(self, eng, name, R=(), W=(), noself=False, **kw):
        fn = (name, kw)
        reads = self._keys(R)
        writes = self._keys(W)
        waits = self._deps(eng, reads, writes)
        if noself:
            waits = [(s, v) for (s, v) in waits if s != eng]
        self.cnt[eng] = self.cnt.get(eng, 0) + 1
        val = self.cnt[eng]
        self.ops[eng].append((fn, waits, (eng, 1)))
        self._record(eng, val, reads, writes)

    def dma(self, queue, R=(), W=(), is_output=False, **kw):
        fn = ("dma_start", kw)
        reads = self._keys(R)
        writes = self._keys(W)
        half = N_DMA_SLOTS // 2
        rr = self.dma_rr_q.get(queue, 0)
        self.dma_rr_q[queue] = (rr + 1) % half
        slot = rr + (half if queue == "pool" else 0)
        sk = ("dma", slot)
        waits = self._deps(queue, reads, writes)
        prev = self.cnt.get(sk, 0)
        if prev and self.known[queue].get(sk, 0) < prev:
            self.known[queue][sk] = prev
            waits.append((sk, prev))
        self.cnt[sk] = prev + 1
        val = prev + 1
        self.ops[queue].append((fn, waits, (sk, 16)))
        self._record(sk, val, reads, writes)
        if is_output:
            self.out_dmas.append((sk, val))

    def run(self):
        nc = self.nc
        with ExitStack() as st:
            sems = {}
            for e in ENGS:
                sems[e] = st.enter_context(nc.semaphore("s_" + e))
            for i in range(N_DMA_SLOTS):
                sems[("dma", i)] = st.enter_context(nc.semaphore("s_dma%d" % i))
            block = st.enter_context(nc.Block())
            mult = lambda s: 16 if isinstance(s, tuple) else 1

            def replay(ename, eh, final=False):
                for fn, waits, (isem, iamt) in self.ops[ename]:
                    for s, v in waits:
                        eh.wait_ge(sems[s], v * mult(s))
                    ins = getattr(eh, fn[0])(**fn[1])
                    ins.then_inc(sems[isem], iamt)
                if final:
                    done = {}
                    for s, v in self.out_dmas:
                        done[s] = max(done.get(s, 0), v)
                    for s, v in done.items():
                        eh.wait_ge(sems[s], v * 16)

            @block.tensor
            def _(e):
                replay("pe", e)

            @block.scalar
            def _(e):
                replay("act", e)

            @block.vector
            def _(e):
                replay("dve", e)

            @block.gpsimd
            def _(e):
                replay("pool", e)

            @block.sync
            def _(e):
                replay("sp", e, final=True)


def _cols(v, nt):
    return np.ascontiguousarray(np.asarray(v, np.float32).reshape(nt, 128).T)


class Pack:
    def __init__(self):
        self.off = {}
        self.n = 0
        self.arrs = []

    def add(self, name, arr):
        a = np.ascontiguousarray(arr, np.float32).reshape(128, -1)
        self.off[name] = (self.n, a.shape[1])
        self.n += a.shape[1]
        self.arrs.append(a)

    def build(self):
        return np.ascontiguousarray(np.concatenate(self.arrs, axis=1))


def _consts():
    pk = Pack()
    i = np.arange(128)
    pk.add('ident', np.eye(128))
    pk.add('maskU', (i[:, None] <= i[None, :]).astype(np.float32))
    pk.add('maskL', (i[:, None] >= i[None, :]).astype(np.float32))
    pk.add('negF', np.where(i[:, None] <= i[None, :], 0.0, -30000.0))
    pk.add('negB', np.where(i[:, None] >= i[None, :], 0.0, -30000.0))
    pk.add('ones', np.ones((128, 128)))
    rp = np.zeros((128, 128), np.float32)
    for hb in (0, 64):
        for d in range(64):
            if (d % 32) < 16:
                rp[hb + d + 16, hb + d] = -1.0
            else:
                rp[hb + d - 16, hb + d] = 1.0
    pk.add('rperm', rp)
    hh = np.arange(4, dtype=np.float32)
    lf = np.log1p(-2.0 ** (-5.0 - 0.0 - hh)).astype(np.float32)
    lb = np.log1p(-2.0 ** (-5.0 - 0.5 - hh)).astype(np.float32)
    la = np.stack([lf, lb], axis=0)[None, :, None, :] * np.ones((128, 1, NCH, 1), np.float32)
    pk.add('retla', la)
    b64 = np.zeros((128, 128), np.float32)
    b64[0:64, 0:64] = 1.0 / 64.0
    b64[64:128, 64:128] = 1.0 / 64.0
    pk.add('blk64', b64)
    tt_ = np.arange(256, dtype=np.float32)
    pk.add('rampf', np.broadcast_to(tt_ + 1.0, (128, 256)))
    pk.add('rampb', np.broadcast_to(256.0 - tt_, (128, 256)))
    return pk


def _rot_tables(sample):
    cos = np.ones((128, 1024), np.float32)
    sin = np.zeros((128, 1024), np.float32)
    if sample:
        t = np.arange(1024)
        row = (t // 64).astype(np.float32)
        col = (t % 64).astype(np.float32)
        nf = 16
        inv = (10000.0 ** (-np.arange(nf, dtype=np.float32) / nf)).astype(np.float32)
        for p in range(128):
            d = p % 64
            half = d // 32
            f = d % 16
            ang = (row if half == 0 else col) * inv[f]
            cos[p] = np.cos(ang)
            sin[p] = np.sin(ang)
    return cos, sin


def _shared_layouts(inp):
    sh = {}
    sh['w_in'] = np.ascontiguousarray(inp['w_in'].reshape(L * 1024, 2568))
    sh['w_out'] = np.ascontiguousarray(inp['w_out'].reshape(L * 1024, 1024))
    sh['w_up'] = np.ascontiguousarray(inp['ffn_w_up'].reshape(L * 1024, 2 * DFF))
    sh['w_down'] = np.ascontiguousarray(inp['ffn_w_down'].reshape(L * DFF, 1024))
    sh['mod_w'] = np.ascontiguousarray(inp['mod_w'].reshape(L * 1024, 6144))
    sh['glu_w'] = np.ascontiguousarray(inp['s5_glu_w'].reshape(L * 256, 512))
    lw = np.zeros((L, 128, 2, 2, 2, 128), np.float32)
    for wi, nm in enumerate(('lru_wa', 'lru_wx')):
        w = inp[nm]
        for ti in range(2):
            for bb in range(2):
                blk = ti * 2 + bb
                lw[:, bb * 64:(bb + 1) * 64, wi, :, ti, bb * 64:(bb + 1) * 64] = np.transpose(w[:, :, blk], (0, 2, 1, 3))
    sh['lruw'] = np.ascontiguousarray(lw.reshape(L * 128, 2 * 2 * 2 * 128))
    bt = np.zeros((L, 128, 2, 8, 128), np.float32)
    ct = np.zeros((L, 128, 2, 8, 128), np.float32)
    for ri, (nb, ncn) in enumerate((('s5_b_re', 's5_c_re'), ('s5_b_im', 's5_c_im'))):
        b = inp[nb]
        c = inp[ncn]
        for g in range(16):
            j = g // 2
            gl = g % 8
            co = (g % 2) * 64
            bt[:, gl * 16:(gl + 1) * 16, ri, j, co:co + 64] = np.transpose(b[:, g], (0, 2, 1))
            ct[:, co:co + 64, ri, j, gl * 16:(gl + 1) * 16] = np.transpose(c[:, g], (0, 2, 1))
    sh['s5bt'] = np.ascontiguousarray(bt.reshape(L * 128, 2 * 8 * 128))
    sh['s5ct'] = np.ascontiguousarray(ct.reshape(L * 128, 2 * 8 * 128))
    return sh


def _head_map(kind, h):
    if kind == 'ssd':
        return (h // 2) * 64, h % 2
    return (h % 2) * 64, h // 2


def _state_layout(st, kind):
    o = np.zeros((L, 128, 2, 2, 64), np.float32)
    for h in range(4):
        base, slot = _head_map(kind, h)
        o[:, base:base + 64, :, slot, :] = np.transpose(st[:, :, h], (0, 2, 1, 3))
    return np.ascontiguousarray(o.reshape(L * 128, 256))


def _core_pack(inp, core):
    sample = core < 4
    pk = Pack()
    condA = inp['c'][core] if sample else inp['c_ctx']
    cond = np.stack([_cols(condA, 8), _cols(inp['c_ctx'], 8)], axis=-1)
    pk.add('cond', cond)
    pk.add('f', np.full((128, 1), 1.0 if sample else 0.0, np.float32))
    pk.add('n1w', np.stack([_cols(inp['norm1_w'][l], 8) for l in range(L)], axis=1))
    pk.add('n2w', np.stack([_cols(inp['norm2_w'][l], 8) for l in range(L)], axis=1))
    pk.add('modb', np.stack([_cols(inp['mod_b'][l], 48) for l in range(L)], axis=1))
    pk.add('fnw', _cols(inp['final_norm_w'], 8))
    pk.add('lcw', np.stack([np.stack([_cols(inp['lru_conv_w'][l, j], 2) for j in range(4)], axis=-1) for l in range(L)], axis=1))
    pk.add('lcb', np.stack([_cols(inp['lru_conv_b'][l], 2) for l in range(L)], axis=1))
    pk.add('llam', np.stack([np.stack([_cols(inp['lru_lambda'][l, d], 2) for d in range(2)], axis=1) for l in range(L)], axis=1))
    pk.add('lba', np.stack([np.stack([_cols(inp['lru_ba'][l, d].reshape(-1), 2) for d in range(2)], axis=1) for l in range(L)], axis=1))
    pk.add('lbx', np.stack([np.stack([_cols(inp['lru_bx'][l, d].reshape(-1), 2) for d in range(2)], axis=1) for l in range(L)], axis=1))
    if sample:
        li = inp['state_lru'][core]
    else:
        li = np.zeros((L, 2, 256), np.float32)
    pk.add('linit', np.stack([np.stack([_cols(li[l, d], 2) for d in range(2)], axis=1) for l in range(L)], axis=1))
    pk.add('scw', np.stack([np.stack([_cols(inp['ssd_conv_w'][l, j], 4) for j in range(4)], axis=-1) for l in range(L)], axis=1))
    pk.add('scb', np.stack([_cols(inp['ssd_conv_b'][l], 4) for l in range(L)], axis=1))
    pk.add('sdtb', np.broadcast_to(inp['ssd_dt_bias'].reshape(1, L, 8), (128, L, 8)))
    pk.add('salog', np.broadcast_to(inp['ssd_a_log'].reshape(1, L, 8), (128, L, 8)))
    pk.add('sdcol', np.stack([_cols(np.repeat(inp['ssd_d'][l], 64), 2) for l in range(L)], axis=1))
    pk.add('snw', np.stack([_cols(inp['ssd_norm_w'][l], 2) for l in range(L)], axis=1))
    pk.add('rgn', np.stack([_cols(inp['ret_gn_w'][l], 2) for l in range(L)], axis=1))
    pk.add('fcw', np.stack([np.stack([_cols(inp['ffn_conv_w'][l, j], NJ) for j in range(3)], axis=-1) for l in range(L)], axis=1))
    pk.add('fcb', np.stack([_cols(inp['ffn_conv_b'][l], NJ) for l in range(L)], axis=1))
    pk.add('lre', np.stack([np.stack([_cols(inp['s5_lam_re'][l, d].reshape(-1), 8) for d in range(2)], axis=1) for l in range(L)], axis=1))
    pk.add('lim', np.stack([np.stack([_cols(inp['s5_lam_im'][l, d].reshape(-1), 8) for d in range(2)], axis=1) for l in range(L)], axis=1))
    pk.add('lstep', np.stack([np.stack([_cols(np.repeat(inp['s5_log_step'][l, d], 64), 8) for d in range(2)], axis=1) for l in range(L)], axis=1))
    pk.add('s5d', np.stack([_cols(inp['s5_d'][l], 2) for l in range(L)], axis=1))
    if sample:
        s5i = inp['state_s5'][core]
    else:
        s5i = np.zeros((L, 2, 16, 64, 2), np.float32)
    pk.add('s5init', np.stack([np.stack([np.stack([_cols(s5i[l, d, :, :, ri].reshape(-1), 8) for ri in range(2)], axis=-1)
                                         for d in range(2)], axis=1) for l in range(L)], axis=1))
    return pk


def _assign(core):
    if core < 4:
        return ('s', core), [2 * core, 2 * core + 1]
    base = 8 + (core - 4) * 6
    return ('p', [base, base + 1, base + 2, base + 3]), [base + 4, base + 5]


def build(pko, cso, enabled, depth):
    nc = bass.Bass("TRN2", target_bir_lowering=False)

    def din(name, shape):
        return nc.dram_tensor(name, list(shape), F32, kind="ExternalInput").ap()

    def dout(name, shape):
        return nc.dram_tensor(name, list(shape), F32, kind="ExternalOutput").ap()

    xin = din("xin", [NT, D])
    pk_d = din("pk", [128, pko.n])
    cs_d = din("cst", [128, cso.n])
    cos_d = din("cosT", [128, 1024])
    sin_d = din("sinT", [128, 1024])
    w_in = din("w_in", [L * 1024, 2568])
    w_out = din("w_out", [L * 1024, 1024])
    w_up = din("w_up", [L * 1024, 2 * DFF])
    w_down = din("w_down", [L * DFF, 1024])
    mod_w = din("mod_w", [L * 1024, 6144])
    glu_w = din("glu_w", [L * 256, 512])
    lruw = din("lruw", [L * 128, 1024])
    s5bt = din("s5bt", [L * 128, 2048])
    s5ct = din("s5ct", [L * 128, 2048])
    ssdinit = din("ssdinit", [L * 128, 256])
    retinit = din("retinit", [L * 128, 256])
    y_out = dout("y", [NT, D])
    o_ssd = dout("o_ssd", [NSEG, L, 2, 4, 64, 64])
    o_ret = dout("o_ret", [NSEG, L, 2, 4, 64, 64])
    o_s5 = dout("o_s5", [NSEG, L, 2, 1024, 2])
    o_lru = dout("o_lru", [NSEG, L, 2, 256])

    with ExitStack() as st:
        def sb(name, shape, dt=F32):
            return st.enter_context(nc.sbuf_tensor(name, list(shape), dt))

        def psum(name, shape, dt=F32):
            return st.enter_context(nc.psum_tensor(name, list(shape), dt))

        P = Prog(nc)

        def V(name, R, W, **kw):
            P.op("dve", name, R, W, **kw)

        def A(name, R, W, **kw):
            P.op("act", name, R, W, **kw)

        def ACT(out, in_, func, R, W, **kw):
            P.op("act", "activation", R, W, out=out, in_=in_, func=func, **kw)

        def MM(out, lhsT, rhs, R, W, start=True, stop=True, noself=False):
            P.op("pe", "matmul", R, W, noself=noself, out=out, lhsT=lhsT, rhs=rhs, start=start, stop=stop)

        x = sb("x", [128, KT, NT])
        h = sb("h", [128, KT, NT], BF16)
        ymix = sb("ymix", [128, KT, NT], BF16)
        ws = [sb("ws%d" % i, [128, 8, 512], BF16) for i in range(3)]
        mws = [sb("mws%d" % i, [128, 8, 128], BF16) for i in range(4)]
        cpad = sb("cpad", [128, NSEG, 260])
        pk = sb("pks", [128, pko.n])
        cst = sb("csts", [128, cso.n])
        T = [sb("T%d" % i, [128, NT]) for i in range(8)]
        cs = sb("csil", [128, KT, 2], BF16)
        modsb = sb("modsb", [128, 48, 2])
        modA = sb("modA", [128, 2, KT, 2])
        onesb = sb("onesb", [128, 128], BF16)
        identb = sb("identb", [128, 128], BF16)
        rpermb = sb("rpermb", [128, 128], BF16)
        small = sb("small", [128, 512])
        rs = sb("rs", [128, 256])
        fin_lru = sb("fin_lru", [128, 2, 2, NSEG])
        pb = [psum("pb%d" % i, [128, 512]) for i in range(6)]
        pmod = psum("pmod", [128, 512])
        pbb = psum("pbb", [128, 1024], BF16)
        pbi = [0]
        pb_lim = [6]

        def PB():
            pbi[0] = (pbi[0] + 1) % pb_lim[0]
            return pb[pbi[0]]

        def pkv(name, *dims):
            o, n = pko.off[name]
            v = pk[:, o:o + n]
            if len(dims) == 2:
                v = v.rearrange("p (a b) -> p a b", a=dims[0], b=dims[1])
            elif len(dims) == 3:
                v = v.rearrange("p (a b c) -> p a b c", a=dims[0], b=dims[1], c=dims[2])
            elif len(dims) == 4:
                v = v.rearrange("p (a b c d) -> p a b c d", a=dims[0], b=dims[1], c=dims[2], d=dims[3])
            return v

        def csv(name):
            o, n = cso.off[name]
            return cst[:, o:o + n]

        fcol = pkv('f')
        ALLY = [('ymix', i) for i in range(8)]

        P.dma("sp", W=[pk], out=pk[:], in_=pk_d)
        P.dma("sp", W=[cst], out=cst[:], in_=cs_d)
        V("memset", [], [onesb], ap=onesb[:], constant=1.0 / 1024.0)
        V("tensor_copy", [cst], [identb], out=identb[:], in_=csv('ident'))
        V("tensor_copy", [cst], [rpermb], out=rpermb[:], in_=csv('rperm'))
        V("memset", [], [cpad], ap=cpad[:], constant=0.0)
        V("memset", [], ALLY, ap=ymix[:], constant=0.0)
        V("memset", [], [small], ap=small[:], constant=0.0)
        V("memset", [small], [small], ap=small[:, 0:1], constant=EPS)
        ACT(cs[:], pkv('cond', 8, 2), AF.Silu, [pk], [cs])

        for c in range(NCH):
            xs_ = T[c % 2]
            P.dma("sp", W=[xs_], out=xs_[:, 0:1024], in_=xin[c * 128:(c + 1) * 128, :])
            for half in range(2):
                ps = PB()
                for q in range(4):
                    kt = half * 4 + q
                    P.op("pe", "transpose", [xs_, cst], [ps], out=ps[:, q * 128:(q + 1) * 128], in_=xs_[:, kt * 128:(kt + 1) * 128], identity=csv('ident'))
                A("copy", [ps], [('x', c // 4)], out=x[:, half * 4:half * 4 + 4, c * 128:(c + 1) * 128], in_=ps[:].rearrange("p (a b) -> p a b", a=4, b=128))

        wsi = [0]

        def load_w(dram2d, row0, nk, col0, ncols):
            s = ws[wsi[0] % 3]
            wsi[0] += 1
            src = dram2d[row0:row0 + nk * 128, col0:col0 + ncols].rearrange("(kt p) c -> p kt c", p=128)
            P.dma("pool", W=[s], out=s[:, 0:nk, 0:ncols], in_=src)
            return s

        def mm_acc(ps_ap, pairs, R, W):
            n = len(pairs)
            for i, (lt, rh) in enumerate(pairs):
                MM(ps_ap, lt, rh, R, W if i in (0, n - 1) else [], start=(i == 0), stop=(i == n - 1), noself=(i > 0))

        def dense_fm(wslot, c0, b, nk=8, rhs=None, rkeys=None):
            ps = PB()
            if rhs is None:
                rhs = h
                rkeys = [('h', b)]
            pairs = [(wslot[:, kt, c0:c0 + 128], rhs[:, kt, b * 512:(b + 1) * 512]) for kt in range(nk)]
            mm_acc(ps[:], pairs, R=[wslot] + rkeys, W=[ps])
            return ps

        def mod_issue(l, sl):
            m = mws[sl % 4]
            src = mod_w[l * 1024:(l + 1) * 1024, sl * 128:(sl + 1) * 128].rearrange("(kt p) c -> p kt c", p=128)
            P.dma("pool", W=[m], out=m[:], in_=src)

        def mod_mm(l, sl):
            m = mws[sl % 4]
            pairs = [(m[:, kt, :], cs[:, kt, :]) for kt in range(KT)]
            mm_acc(pmod[:, sl * 2:sl * 2 + 2], pairs, R=[m, cs], W=[pmod])

        def mod_finish(l):
            mb = pkv('modb', L, 48)
            mp3 = pmod[:, 0:96].rearrange("p (a b) -> p a b", a=48, b=2)
            for c in range(2):
                V("tensor_tensor", [pmod, pk], [modsb], out=modsb[:, :, c], in0=mp3[:, :, c], in1=mb[:, l, :], op=ALU.add)
            for which, (nm, sco) in enumerate((('n1w', 8), ('n2w', 32))):
                nw = pkv(nm, L, 8)
                V("tensor_scalar", [modsb], [modA], out=modA[:, which], in0=modsb[:, sco:sco + 8, :], scalar1=1.0, scalar2=None, op0=ALU.add)
                V("tensor_tensor", [modA, pk], [modA], out=modA[:, which], in0=modA[:, which], in1=nw[:, l, :].unsqueeze(2).to_broadcast([128, 8, 2]), op=ALU.mult)

        def modulation_all(l):
            for sl in range(48):
                mod_issue(l, sl)
                if sl >= 2:
                    mod_mm(l, sl - 2)
            mod_mm(l, 46)
            mod_mm(l, 47)

        sqb = T[7][:, 0:1024].bitcast(BF16).rearrange("p (a b) -> p a b", a=8, b=256)

        rs_alt = [(rs[:], rs), (small[:, 256:512], ('small', 'rs'))]

        def rstd_block(nb):
            b = nb // 2
            tsl = slice(nb * 256, (nb + 1) * 256)
            rs_ap, rs_k = rs_alt[nb % 2]
            ACT(sqb, x[:, :, tsl], AF.Square, [('x', b)], [T[7]])
            ss = PB()
            mm_acc(ss[:, 0:256], [(onesb[:], sqb[:, kt, :]) for kt in range(KT)], R=[T[7], onesb], W=[ss])
            ACT(rs_ap, ss[:, 0:256], AF.Ln, [ss, small], [rs_k], bias=small[:, 0:1], scale=1.0)
            ACT(rs_ap, rs_ap, AF.Exp, [rs_k], [rs_k], scale=-0.5)
            return rs_ap, rs_k

        def norm_mod(l, which):
            sho = 0 if which == 0 else 24
            for nb in range(6):
                c = 0 if nb < 4 else 1
                b = nb // 2
                tsl = slice(nb * 256, (nb + 1) * 256)
                rs_ap, rs_k = rstd_block(nb)
                for kt in range(KT):
                    tmp = T[5 + kt % 2][:, 0:256]
                    tk = T[5 + kt % 2]
                    V("tensor_tensor", [('x', b), rs_k], [tk], out=tmp, in0=x[:, kt, tsl], in1=rs_ap, op=ALU.mult)
                    V("tensor_scalar", [tk, modA, modsb], [('h', b)], out=h[:, kt, tsl], in0=tmp, scalar1=modA[:, which, kt, c:c + 1], scalar2=modsb[:, sho + kt, c:c + 1], op0=ALU.mult, op1=ALU.add)

        def pad_fix(npad_l, npad_r):
            if npad_l:
                V("tensor_scalar", [cpad, pk], [cpad], out=cpad[:, 1:4, 2 - npad_l:2], in0=cpad[:, 0:3, 258 - npad_l:258], scalar1=fcol, scalar2=None, op0=ALU.mult)
            if npad_r:
                V("tensor_scalar", [cpad, pk], [cpad], out=cpad[:, 0:3, 258:258 + npad_r], in0=cpad[:, 1:4, 2:2 + npad_r], scalar1=fcol, scalar2=None, op0=ALU.mult)

        def conv_from_cpad(out3, wcols, bcol, ktaps, okeys):
            o0 = 2 - ktaps // 2
            ACT(out3, cpad[:, :, o0:o0 + 256], AF.Identity, [cpad, pk], okeys, scale=wcols[0], bias=bcol)
            for j in range(1, ktaps):
                V("scalar_tensor_tensor", [cpad, pk] + okeys, okeys, out=out3, in0=cpad[:, :, o0 + j:o0 + j + 256], scalar=wcols[j], in1=out3, op0=ALU.mult, op1=ALU.add)

        def evac_to_cpad(ps, b):
            A("copy", [ps], [cpad], out=cpad[:, 2 * b:2 * b + 2, 2:258], in_=ps[:].rearrange("p (a b) -> p a b", a=2, b=256))

        def v3(t):
            return t[:].rearrange("p (a b) -> p a b", a=NSEG, b=256)

        def lru_mixer(l):
            wsl = load_w(w_in, l * 1024, 8, 2056, 512)
            t5b = T[5][:].bitcast(BF16)
            lw = t5b[:, 0:1024].rearrange("p (w d t o) -> p w d t o", w=2, d=2, t=2, o=128)
            xcb = t5b[:, 1024:1024 + NT]
            P.dma("pool", W=[T[5]], out=t5b[:, 0:1024], in_=lruw[l * 128:(l + 1) * 128, :])
            lcw = pkv('lcw', L, 2, 4); lcb = pkv('lcb', L, 2); llam = pkv('llam', L, 2, 2)
            lba = pkv('lba', L, 2, 2); lbx = pkv('lbx', L, 2, 2); linit = pkv('linit', L, 2, 2)
            cpv = small[:, 8:12]
            ACT(cpv, llam[:, l].rearrange("p a b -> p (a b)"), AF.Exp, [pk], [small], scale=-1.0)
            ACT(cpv, cpv, AF.Ln, [small], [small], bias=1.0, scale=1.0)
            V("tensor_scalar", [small], [small], out=cpv, in0=cpv, scalar1=-8.0, scalar2=None, op0=ALU.mult)
            for ti in range(2):
                for b in range(3):
                    ps = dense_fm(wsl, ti * 128, b)
                    evac_to_cpad(ps, b)
                pad_fix(2, 1)
                xc = T[0]
                conv_from_cpad(v3(xc), [lcw[:, l, ti, j:j + 1] for j in range(4)], lcb[:, l, ti:ti + 1], 4, [xc])
                gg = T[1]
                for b in range(3):
                    ps = dense_fm(wsl, 256 + ti * 128, b)
                    bs = slice(b * 512, (b + 1) * 512)
                    t2 = T[2][:, 0:512]
                    ACT(t2, ps[:], AF.Square, [ps], [T[2]])
                    V("tensor_scalar", [T[2]], [T[2]], out=t2, in0=t2, scalar1=0.044715, scalar2=1.0, op0=ALU.mult, op1=ALU.add)
                    V("tensor_tensor", [T[2], ps], [T[2]], out=t2, in0=t2, in1=ps[:], op=ALU.mult)
                    ACT(t2, t2, AF.Sigmoid, [T[2]], [T[2]], scale=1.5957691216057308)
                    V("tensor_tensor", [T[2], ps], [gg], out=gg[:, bs], in0=t2, in1=ps[:], op=ALU.mult)
                hacc = T[2]
                A("copy", [xc], [T[5]], out=xcb, in_=xc[:])
                for d in range(2):
                    av = T[3]; uv = T[4]
                    for b in range(3):
                        bs = slice(b * 512, (b + 1) * 512)
                        pa = PB()
                        MM(pa[:], lw[:, 0, d, ti, :], xcb[:, bs], [T[5]], [pa])
                        px = PB()
                        MM(px[:], lw[:, 1, d, ti, :], xcb[:, bs], [T[5]], [px])
                        ACT(av[:, bs], pa[:], AF.Sigmoid, [pa, pk], [av], bias=lba[:, l, d, ti:ti + 1], scale=1.0)
                        ACT(uv[:, bs], px[:], AF.Sigmoid, [px, pk], [uv], bias=lbx[:, l, d, ti:ti + 1], scale=1.0)
                    cpc = small[:, 8 + d * 2 + ti:8 + d * 2 + ti + 1]
                    ACT(av[:], av[:], AF.Exp, [av, small], [av], scale=cpc)
                    V("tensor_tensor", [uv, xc], [uv], out=uv[:], in0=uv[:], in1=xc[:], op=ALU.mult)
                    m2 = T[6]
                    ACT(m2[:], av[:], AF.Square, [av], [m2])
                    ACT(m2[:], m2[:], AF.Sqrt, [m2], [m2], scale=-1.0, bias=1.0)
                    V("tensor_tensor", [uv, m2], [uv], out=uv[:], in0=uv[:], in1=m2[:], op=ALU.mult)
                    hd = hacc if d == 0 else T[6]
                    icol = small[:, 16:17]
                    order = list(range(NSEG)) if d == 0 else [3, 2, 1, 0, 5, 4]
                    for sg in order:
                        first = (sg == 0 and d == 0) or (sg == 3 and d == 1)
                        if sg >= 4:
                            init = 0.0
                            rk = []
                        elif first:
                            init = linit[:, l, d, ti:ti + 1]
                            rk = [pk]
                        else:
                            prev = sg - 1 if d == 0 else sg + 1
                            pcol = hd[:, prev * 256 + 255:prev * 256 + 256] if d == 0 else hd[:, prev * 256:prev * 256 + 1]
                            V("tensor_scalar", [hd, pk], [small], out=icol, in0=pcol, scalar1=fcol, scalar2=None, op0=ALU.mult)
                            init = icol
                            rk = [small]
                        if d == 0:
                            sl_ = slice(sg * 256, (sg + 1) * 256)
                        else:
                            sl_ = slice(sg * 256 + 255, (sg * 256 - 1) if sg > 0 else None, -1)
                        V("tensor_tensor_scan", [av, uv] + rk, [hd], out=hd[:, sl_], data0=av[:, sl_], data1=uv[:, sl_], initial=init, op0=ALU.mult, op1=ALU.add)
                    fc = 255 if d == 0 else 0
                    A("copy", [hd], [fin_lru], out=fin_lru[:, ti, d, :], in_=v3(hd)[:, :, fc])
                V("tensor_tensor", [hacc, T[6]], [hacc], out=hacc[:], in0=hacc[:], in1=T[6][:], op=ALU.add)
                V("tensor_tensor", [hacc, gg], [('ymix', 6 + ti)], out=ymix[:, 6 + ti, :], in0=hacc[:], in1=gg[:], op=ALU.mult)
                for d in range(2):
                    dst = o_lru[:, l, d, ti * 128:(ti + 1) * 128].rearrange("s p -> p s")
                    P.dma("sp", R=[fin_lru], is_output=True, out=dst, in_=fin_lru[:, ti, d, :], allow_slow_non_contiguous=True)

        def out_proj(l):
            for half in range(2):
                wsl = load_w(w_out, l * 1024, 8, half * 512, 512)
                for q in range(4):
                    dt_ = half * 4 + q
                    for b in range(3):
                        c = 0 if b < 2 else 1
                        ps = dense_fm(wsl, q * 128, b, rhs=ymix, rkeys=ALLY)
                        bs = slice(b * 512, (b + 1) * 512)
                        V("scalar_tensor_tensor", [ps, modsb, ('x', b)], [('x', b)], out=x[:, dt_, bs], in0=ps[:], scalar=modsb[:, 16 + dt_, c:c + 1], in1=x[:, dt_, bs], op0=ALU.mult, op1=ALU.add)

        def ffn(l):
            fcw = pkv('fcw', L, NJ, 3); fcb = pkv('fcb', L, NJ)
            aff = ymix
            nxt = l + 1 if l + 1 < depth else None
            pending = None

            def u_part(wu_, co_, gc_, jj_):
                for b in range(3):
                    ps = dense_fm(wu_, co_, b)
                    bs = slice(b * 512, (b + 1) * 512)
                    V("tensor_tensor", [gc_, ps], [('ymix', jj_)], out=aff[:, jj_, bs], in0=gc_[:, bs], in1=ps[:], op=ALU.mult)

            for (j0, nj) in ((0, 6), (6, 6), (12, 5), (17, 5)):
                for jj in range(nj):
                    j = j0 + jj
                    if nxt is not None:
                        mod_issue(nxt, 2 * j)
                        mod_issue(nxt, 2 * j + 1)
                    if jj % 4 == 0:
                        ncl = min(4, nj - jj) * 128
                        wg = load_w(w_up, l * 1024, 8, j * 128, ncl)
                        wu = load_w(w_up, l * 1024, 8, DFF + j * 128, ncl)
                    co = (jj % 4) * 128
                    for b in range(3):
                        ps = dense_fm(wg, co, b)
                        evac_to_cpad(ps, b)
                    pad_fix(1, 1)
                    gc = T[jj % 2]
                    conv_from_cpad(v3(gc), [fcw[:, l, j, k:k + 1] for k in range(3)], fcb[:, l, j:j + 1], 3, [gc])
                    ACT(gc[:], gc[:], AF.Silu, [gc], [gc])
                    if pending is not None:
                        u_part(*pending)
                    pending = (wu, co, gc, jj)
                    if nxt is not None and j >= 1:
                        mod_mm(nxt, 2 * (j - 1))
                        mod_mm(nxt, 2 * (j - 1) + 1)
                u_part(*pending)
                pending = None
                for half in range(2):
                    wd = load_w(w_down, l * DFF + j0 * 128, nj, half * 512, 512)
                    for q in range(4):
                        dt_ = half * 4 + q
                        for b in range(3):
                            c = 0 if b < 2 else 1
                            ps = dense_fm(wd, q * 128, b, nk=nj, rhs=aff, rkeys=[('ymix', i) for i in range(nj)])
                            bs = slice(b * 512, (b + 1) * 512)
                            V("scalar_tensor_tensor", [ps, modsb, ('x', b)], [('x', b)], out=x[:, dt_, bs], in0=ps[:], scalar=modsb[:, 40 + dt_, c:c + 1], in1=x[:, dt_, bs], op0=ALU.mult, op1=ALU.add)

        def mod_tail(l):
            mod_mm(l, 42)
            mod_mm(l, 43)
            for sl in range(44, 48):
                mod_issue(l, sl)
            for sl in range(44, 48):
                mod_mm(l, sl)

        def final_out():
            fnw = pkv('fnw')
            for nb in range(6):
                b = nb // 2
                tsl = slice(nb * 256, (nb + 1) * 256)
                rs_ap, rs_k = rstd_block(nb)
                for kt in range(KT):
                    V("scalar_tensor_tensor", [('x', b), rs_k, pk], [T[kt // 4]], out=T[kt // 4][:, (kt % 4) * 256:(kt % 4) * 256 + 256], in0=x[:, kt, tsl], scalar=fnw[:, kt:kt + 1], in1=rs_ap, op0=ALU.mult, op1=ALU.mult)
                for cc in range(2):
                    ot = T[2 + cc]
                    for half in range(2):
                        ps = PB()
                        for q in range(4):
                            kt = half * 4 + q
                            src = T[kt // 4][:, (kt % 4) * 256 + cc * 128:(kt % 4) * 256 + cc * 128 + 128]
                            P.op("pe", "transpose", [T[kt // 4], cst], [ps], out=ps[:, q * 128:(q + 1) * 128], in_=src, identity=csv('ident'))
                        A("copy", [ps], [ot], out=ot[:, half * 512:(half + 1) * 512], in_=ps[:])
                    r0 = nb * 256 + cc * 128
                    P.dma("sp", R=[ot], is_output=True, out=y_out[r0:r0 + 128, :], in_=ot[:, 0:1024])

        stt = sb("stt", [128, 8, 96])
        Sm = sb("Sm", [128, 2, 2, 64])
        Sinit = sb("Sinit", [128, 2, 2, 64])
        kwb = sb("kwb", [128, 2, 4, 64], BF16)
        cdH = sb("cdH", [128, 2, NCH, 2])
        blk64b = sb("blk64b", [128, 128], BF16)
        V("tensor_copy", [cst], [blk64b], out=blk64b[:], in_=csv('blk64'))

        def st4(i):
            return stt[:, i, :].rearrange("p (d c h) -> p d c h", d=2, c=NCH, h=4)

        def bfv(t, *dims):
            v = t[:].bitcast(BF16)
            if len(dims) == 2:
                return v.rearrange("p (a b) -> p a b", a=dims[0], b=dims[1])
            if len(dims) == 3:
                return v.rearrange("p (a b c) -> p a b c", a=dims[0], b=dims[1], c=dims[2])
            if len(dims) == 4:
                return v.rearrange("p (a b c d) -> p a b c d", a=dims[0], b=dims[1], c=dims[2], d=dims[3])
            return v

        def attn_mixer(l, kind):
            ssd = (kind == 'ssd')
            v_tok = bfv(T[0], NCH, 4, 64)
            k_tok = bfv(T[1], NCH, 4, 64) if not ssd else bfv(T[1], NCH, 4, 64)[:, :, 0:2, :]
            Sent = bfv(T[2], NCH, 2, 2, 64)
            rhsla = T[3][:, 0:512].rearrange("p (h l) -> p h l", h=4, l=128)
            tmpD = T[3][:, 512:1024].rearrange("p (h l) -> p h l", h=4, l=128)
            eCR = T[3][:, 1024:1536].rearrange("p (h l) -> p h l", h=4, l=128)
            t4b = T[4][:].bitcast(BF16)
            Dm = t4b[:, 0:1024].rearrange("p (d h l) -> p d h l", d=2, h=4, l=128)
            Dsum = t4b[:, 1024:1536].rearrange("p (h l) -> p h l", h=4, l=128)
            qdz = t4b[:, 1536:2560].rearrange("p (d h l) -> p d h l", d=2, h=4, l=128)
            Pm = t4b[:, 2560:3072].rearrange("p (h l) -> p h l", h=4, l=128)
            la = st4(0); lndt = st4(1); Cp = st4(2); eTot = st4(3); tailw = st4(4); dtv = st4(5); nCp = st4(6)
            finS = cpad[:, :, 2:258].rearrange("p s (d t q) -> p s d t q", d=2, t=2, q=64)
            o_st = o_ssd if ssd else o_ret
            init_d = ssdinit if ssd else retinit
            P.dma("sp", W=[Sinit], out=Sinit[:].rearrange("p d t q -> p (d t q)"), in_=init_d[l * 128:(l + 1) * 128, :])

            def hmap(hh_):
                return _head_map(kind, hh_)

            if ssd:
                sz = bfv(T[7], 2, NT)
                xs = bfv(T[6], 2, NT)
                BC = bfv(T[5], 2, NT)
                kf = BC[:, 0:1, :]
                qf = BC[:, 1:2, :]
                scw = pkv('scw', L, 4, 4); scb = pkv('scb', L, 4)
                w1 = load_w(w_in, l * 1024, 8, 0, 512)
                w2 = load_w(w_in, l * 1024, 8, 512, 264)
                for ti in range(2):
                    for b in range(3):
                        ps = dense_fm(w1, ti * 128, b)
                        ACT(sz[:, ti, b * 512:(b + 1) * 512], ps[:], AF.Silu, [ps], [T[7]])
                for ci in range(4):
                    wsl, co = (w1, 256 + ci * 128) if ci < 2 else (w2, (ci - 2) * 128)
                    for b in range(3):
                        ps = dense_fm(wsl, co, b)
                        evac_to_cpad(ps, b)
                    pad_fix(2, 1)
                    conv_from_cpad(v3(T[3]), [scw[:, l, ci, j:j + 1] for j in range(4)], scb[:, l, ci:ci + 1], 4, [T[3]])
                    dst = xs[:, ci, :] if ci < 2 else BC[:, ci - 2, :]
                    ACT(dst, T[3][:], AF.Silu, [T[3]], [T[6] if ci < 2 else T[5]])
                pdt = PB()
                for c in range(NCH):
                    mm_acc(pdt[:, c * 8:(c + 1) * 8], [(h[:, kt, c * 128:(c + 1) * 128], w2[:, kt, 256:264]) for kt in range(KT)], R=[w2, ('h', c // 4)], W=[pdt])
                pdt4 = pdt[:, 0:96].rearrange("p (c d h) -> p d c h", c=NCH, d=2, h=4)
                sdtb = pkv('sdtb', L, 2, 4); salog = pkv('salog', L, 2, 4)
                V("tensor_tensor", [pdt, pk], [stt], out=dtv, in0=pdt4, in1=sdtb[:, l].unsqueeze(2).to_broadcast([128, 2, NCH, 4]), op=ALU.add)
                ACT(stt[:, 5, :], stt[:, 5, :], AF.Exp, [stt], [stt])
                ACT(stt[:, 5, :], stt[:, 5, :], AF.Ln, [stt], [stt], bias=1.0, scale=1.0)
                ACT(stt[:, 1, :], stt[:, 5, :], AF.Ln, [stt], [stt])
                an = small[:, 24:32]
                ACT(an, salog[:, l].rearrange("p a b -> p (a b)"), AF.Exp, [pk], [small])
                V("tensor_scalar", [small], [small], out=an, in0=an, scalar1=-1.0, scalar2=None, op0=ALU.mult)
                V("tensor_tensor", [stt, small], [stt], out=la, in0=dtv, in1=an.rearrange("p (d h) -> p d h", d=2, h=4).unsqueeze(2).to_broadcast([128, 2, NCH, 4]), op=ALU.mult)
            else:
                sg = bfv(T[7], 2, NT)
                qf = bfv(T[5], 2, NT)
                kf = bfv(T[6], 2, NT)
                wA = load_w(w_in, l * 1024, 8, 776, 512)
                wB = load_w(w_in, l * 1024, 8, 776 + 512, 512)
                cpf = cpad[:].rearrange("p s c -> p (s c)")
                cosb = cpf[:, 0:512].bitcast(BF16)
                sinb = cpf[:, 512:1024].bitcast(BF16)
                P.dma("pool", W=[cpad], out=cosb, in_=cos_d)
                P.dma("pool", W=[cpad], out=sinb, in_=sin_d)
                rq = T[3][:, 0:256].bitcast(BF16)
                t1 = T[3][:, 512:1024]
                t2 = T[3][:, 1024:1536]
                for qi in range(4):
                    dstt, dkey = (qf, T[5]) if qi < 2 else (kf, T[6])
                    sc = 1.0 if qi < 2 else 0.125
                    for b in range(3):
                        ps = dense_fm(wA, qi * 128, b)
                        bs = slice(b * 512, (b + 1) * 512)
                        if b == 2:
                            ACT(dstt[:, qi % 2, bs], ps[:], AF.Identity, [ps], [dkey], scale=sc)
                        else:
                            A("copy", [ps], [T[3]], out=rq, in_=ps[:])
                            pp = PB()
                            MM(pp[:], rpermb[:], rq, [rpermb, T[3]], [pp])
                            V("tensor_tensor", [ps, cpad, T[3]], [T[3]], out=t1, in0=ps[:], in1=cosb[:, bs], op=ALU.mult)
                            V("tensor_tensor", [pp, cpad, T[3]], [T[3]], out=t2, in0=pp[:], in1=sinb[:, bs], op=ALU.mult)
                            V("tensor_tensor", [T[3]], [T[3]], out=t1, in0=t1, in1=t2, op=ALU.add)
                            ACT(dstt[:, qi % 2, bs], t1, AF.Identity, [T[3]], [dkey], scale=sc)
                V("memset", [cpad], [cpad], ap=cpad[:], constant=0.0)
                for c in range(NCH):
                    pv = PB()
                    mm_acc(pv[:, 0:256], [(h[:, kt, c * 128:(c + 1) * 128], wB[:, kt, 0:256]) for kt in range(KT)], R=[wB, ('h', c // 4)], W=[pv])
                    A("copy", [pv], [T[0]], out=v_tok[:, c].rearrange("p a b -> p (a b)"), in_=pv[:, 0:256])
                for ti in range(2):
                    for b in range(3):
                        ps = dense_fm(wB, 256 + ti * 128, b)
                        ACT(sg[:, ti, b * 512:(b + 1) * 512], ps[:], AF.Silu, [ps], [T[7]])
                V("tensor_copy", [cst], [stt], out=stt[:, 0, :], in_=csv('retla'))
                V("memset", [stt], [stt], ap=stt[:, 1, :], constant=0.0)

            for c in range(NCH):
                cs_ = slice(c * 128, (c + 1) * 128)
                if ssd:
                    for ti in range(2):
                        P.op("pe", "transpose", [T[6], identb], [pbb], out=pbb[:, ti * 128:(ti + 1) * 128], in_=xs[:, ti, cs_], identity=identb[:])
                    P.op("pe", "transpose", [T[5], identb], [pbb], out=pbb[:, 256:384], in_=kf[:, 0, cs_], identity=identb[:])
                    A("copy", [pbb], [T[0]], out=v_tok[:, c].rearrange("p a b -> p (a b)"), in_=pbb[:, 0:256])
                    A("copy", [pbb], [T[1]], out=k_tok[:, c].rearrange("p a b -> p (a b)"), in_=pbb[:, 256:384])
                else:
                    for ti in range(2):
                        P.op("pe", "transpose", [T[6], identb], [pbb], out=pbb[:, ti * 128:(ti + 1) * 128], in_=kf[:, ti, cs_], identity=identb[:])
                    A("copy", [pbb], [T[1]], out=k_tok[:, c].rearrange("p a b -> p (a b)"), in_=pbb[:, 0:256])

            pc1 = PB()
            MM(pc1[:, 0:48], csv('maskU'), stt[:, 0, 0:48], [cst, stt], [pc1])
            MM(pc1[:, 48:96], csv('maskL'), stt[:, 0, 48:96], [cst, stt], [pc1])
            V("tensor_tensor", [pc1, stt], [stt], out=stt[:, 2, :], in0=pc1[:, 0:96], in1=stt[:, 1, :], op=ALU.subtract)
            V("tensor_scalar", [stt], [stt], out=stt[:, 6, :], in0=stt[:, 2, :], scalar1=-1.0, scalar2=None, op0=ALU.mult)
            pc2 = PB()
            MM(pc2[:, 0:96], csv('ones'), stt[:, 0, :], [cst, stt], [pc2])
            ACT(stt[:, 3, :], pc2[:, 0:96], AF.Exp, [pc2], [stt])
            V("tensor_tensor", [pc2, stt], [stt], out=stt[:, 4, :], in0=pc2[:, 0:96], in1=stt[:, 2, :], op=ALU.subtract)
            ACT(stt[:, 4, :], stt[:, 4, :], AF.Exp, [stt], [stt])
            for base in (0, 64):
                for slot in range(2):
                    hh_ = (base // 64) * 2 + slot if ssd else slot * 2 + base // 64
                    V("tensor_copy", [stt], [cdH], out=cdH[base:base + 64, :, :, slot], in_=eTot[base:base + 64, :, :, hh_])

            csS = [T[3][:].rearrange("p (c t q) -> p c t q", c=NCH, t=2, q=64), T[4][:].rearrange("p (c t q) -> p c t q", c=NCH, t=2, q=64)]
            csK = [T[3], T[4]]
            for c in range(NCH):
                for d in range(2):
                    if ssd:
                        V("tensor_tensor", [T[1], stt], [('kwb', d)], out=kwb[:, d].rearrange("p (g e) n -> p g e n", g=2, e=2),
                          in0=k_tok[:, c].unsqueeze(2).to_broadcast([128, 2, 2, 64]),
                          in1=tailw[:, d, c, :].rearrange("p (g e) -> p g e", g=2, e=2).unsqueeze(3).to_broadcast([128, 2, 2, 64]), op=ALU.mult)
                    else:
                        V("tensor_tensor", [T[1], stt], [('kwb', d)], out=kwb[:, d], in0=k_tok[:, c], in1=tailw[:, d, c, :].unsqueeze(2).to_broadcast([128, 4, 64]), op=ALU.mult)
                    pcs = PB()
                    for hh_ in range(4):
                        base, slot = hmap(hh_)
                        MM(pcs[base:base + 64, slot * 64:(slot + 1) * 64], kwb[:, d, hh_, :], v_tok[:, c, hh_, :], [('kwb', d), T[0]], [pcs])
                    A("copy", [pcs], [csK[d]], out=csS[d][:, c], in_=pcs[:, 0:128].rearrange("p (t q) -> p t q", t=2, q=64))
            V("tensor_copy", [Sinit], [('Sm', 0)], out=Sm[:, 0], in_=Sinit[:, 0])
            V("memset", [], [('Sm', 1)], ap=Sm[:, 1], constant=0.0)
            for i in range(NCH):
                for d in range(2):
                    c = i if d == 0 else NCH - 1 - i
                    sk = ('Sm', d)
                    V("tensor_copy", [sk], [T[2]], out=Sent[:, c, d], in_=Sm[:, d])
                    for slot in range(2):
                        V("scalar_tensor_tensor", [sk, cdH, csK[d]], [sk], out=Sm[:, d, slot], in0=Sm[:, d, slot], scalar=cdH[:, d, c, slot:slot + 1], in1=csS[d][:, c, slot], op0=ALU.mult, op1=ALU.add)
                    seg_end = (c % 2 == 1) if d == 0 else (c % 2 == 0)
                    if seg_end:
                        sg_ = c // 2
                        V("tensor_copy", [sk], [cpad], out=finS[:, sg_, d], in_=Sm[:, d])
                        if d == 0:
                            if sg_ < 3:
                                V("tensor_scalar", [sk, pk], [sk], out=Sm[:, d], in0=Sm[:, d], scalar1=fcol, scalar2=None, op0=ALU.mult)
                            elif sg_ < 5:
                                V("memset", [sk], [sk], ap=Sm[:, d], constant=0.0)
                        else:
                            if sg_ == 5:
                                V("memset", [sk], [sk], ap=Sm[:, d], constant=0.0)
                            elif sg_ == 4:
                                V("tensor_copy", [Sinit, sk], [sk], out=Sm[:, 1], in_=Sinit[:, 1])
                            elif sg_ > 0:
                                V("tensor_scalar", [sk, pk], [sk], out=Sm[:, d], in0=Sm[:, d], scalar1=fcol, scalar2=None, op0=ALU.mult)
            V("memset", [], [T[4]], ap=T[4][:], constant=0.0)
            for hh_ in range(4):
                base, slot = hmap(hh_)
                for d in range(2):
                    dst = o_st[:, l, d, hh_].rearrange("s n q -> n s q")
                    P.dma("sp", R=[cpad], is_output=True, out=dst, in_=finS[base:base + 64, :, d, slot, :])

            ybase = 0 if ssd else 2
            V("memset", [], [mws[0]], ap=mws[0][:], constant=0.0)
            qdz_b = [qdz, mws[0][:].rearrange("p a b -> p (a b)").rearrange("p (d h l) -> p d h l", d=2, h=4, l=128)]
            qdz_k = [('T4', 'q'), mws[0]]
            Dsum_b = [Dsum, mws[1][:, 0:4, :]]
            Dsum_k = [('T4', 'Ds'), mws[1]]

            rl = [T[3][:, 0:512], T[3][:, 512:1024]]
            rlk = [('T3', 0), ('T3', 1)]
            ec = [T[3][:, 1024:1536], mws[2][:].rearrange("p a b -> p (a b)").bitcast(F32)]
            eck_ = [('T3', 2), mws[2]]
            psc_b = [pb[0], pb[1]]
            pcb = [pb[2], pb[3]]
            pyb = pb[4]

            def h3(ap):
                return ap.rearrange("p (h l) -> p h l", h=4, l=128)

            def stA(c):
                cs_ = slice(c * 128, (c + 1) * 128)
                psc = psc_b[c % 2]
                if ssd:
                    for g in range(2):
                        MM(psc[:, g * 128:(g + 1) * 128], kf[g * 64:(g + 1) * 64, 0, cs_], qf[g * 64:(g + 1) * 64, 0, cs_], [T[5]], [psc])
                else:
                    for hh_ in range(4):
                        base, slot = hmap(hh_)
                        MM(psc[:, hh_ * 128:(hh_ + 1) * 128], kf[base:base + 64, slot, cs_], qf[base:base + 64, slot, cs_], [T[5], T[6]], [psc])
                for d in range(2):
                    msk = csv('maskU') if d == 0 else csv('maskL')
                    V("tensor_tensor", [cst, stt], [rlk[d]], out=h3(rl[d]), in0=msk.unsqueeze(1).to_broadcast([128, 4, 128]), in1=la[:, d, c, :].unsqueeze(2).to_broadcast([128, 4, 128]), op=ALU.mult)
                for d in range(2):
                    MM(pcb[d][:], csv('ones'), rl[d], [cst, rlk[d]], [pcb[d]])

            def stC(c):
                cs_ = slice(c * 128, (c + 1) * 128)
                bi = c % 2
                for d in range(2):
                    neg = csv('negF') if d == 0 else csv('negB')
                    pc3 = h3(pcb[d][:])
                    V("tensor_tensor", [pcb[d], cst], [rlk[d]], out=h3(rl[d]), in0=pc3, in1=neg.unsqueeze(1).to_broadcast([128, 4, 128]), op=ALU.add)
                    for hh_ in range(4):
                        ACT(Dm[:, d, hh_, :], h3(rl[d])[:, hh_, :], AF.Exp, [rlk[d], stt], [('T4', 'D', d)], bias=nCp[:, d, c, hh_:hh_ + 1], scale=1.0)
                    ACT(h3(ec[d]), pc3, AF.Exp, [pcb[d]], [eck_[d]])
                for d in range(2):
                    eCR_ = h3(ec[d])
                    for base in (0, 64):
                        if ssd:
                            h0 = (base // 64) * 2
                            V("tensor_tensor", [eck_[d], T[5]], [qdz_k[bi]], out=qdz_b[bi][base:base + 64, d, h0:h0 + 2, :],
                              in0=qf[base:base + 64, 0, cs_].unsqueeze(1).to_broadcast([64, 2, 128]), in1=eCR_[base:base + 64, h0:h0 + 2, :], op=ALU.mult)
                        else:
                            o_ = base // 64
                            V("tensor_tensor", [eck_[d], T[5]], [qdz_k[bi]], out=qdz_b[bi][base:base + 64, d, o_::2, :],
                              in0=qf[base:base + 64, :, cs_], in1=eCR_[base:base + 64, o_::2, :], op=ALU.mult)
                V("tensor_tensor", [('T4', 'D', 0), ('T4', 'D', 1)], [Dsum_k[bi]], out=Dsum_b[bi], in0=Dm[:, 0], in1=Dm[:, 1], op=ALU.add)

            def stB(c):
                bi = c % 2
                psc = psc_b[bi]
                if ssd:
                    V("tensor_tensor", [psc, Dsum_k[bi]], [('T4', 'P')], out=Pm.rearrange("p (g e) l -> p g e l", g=2, e=2),
                      in0=psc[:, 0:256].rearrange("p (g l) -> p g l", g=2, l=128).unsqueeze(2).to_broadcast([128, 2, 2, 128]),
                      in1=Dsum_b[bi].rearrange("p (g e) l -> p g e l", g=2, e=2), op=ALU.mult)
                else:
                    V("tensor_tensor", [psc, Dsum_k[bi]], [('T4', 'P')], out=Pm, in0=h3(psc[:]), in1=Dsum_b[bi], op=ALU.mult)
                py = pyb
                for hh_ in range(4):
                    base, slot = hmap(hh_)
                    oap = py[(hh_ % 2) * 64:(hh_ % 2) * 64 + 64, (hh_ // 2) * 128:(hh_ // 2) * 128 + 128]
                    MM(oap, v_tok[:, c, hh_, :], Pm[:, hh_, :], [T[0], ('T4', 'P')], [py], start=True, stop=False)
                    MM(oap, Sent[:, c, 0, slot, :], qdz_b[bi][:, 0, hh_, :], [T[2], qdz_k[bi]], [], start=False, stop=False, noself=True)
                    MM(oap, Sent[:, c, 1, slot, :], qdz_b[bi][:, 1, hh_, :], [T[2], qdz_k[bi]], [py], start=False, stop=True, noself=True)

            def stD(c):
                cs_ = slice(c * 128, (c + 1) * 128)
                py = pyb
                if ssd:
                    sdcol = pkv('sdcol', L, 2)
                    for ti in range(2):
                        V("scalar_tensor_tensor", [T[6], pk, py], [('ymix', ti)], out=ymix[:, ti, cs_], in0=xs[:, ti, cs_], scalar=sdcol[:, l, ti:ti + 1], in1=py[:, ti * 128:(ti + 1) * 128], op0=ALU.mult, op1=ALU.add)
                else:
                    A("copy", [py], [('ymix', 2), ('ymix', 3)], out=ymix[:, 2:4, cs_], in_=py[:, 0:256].rearrange("p (t l) -> p t l", t=2, l=128))

            for c in range(NCH + 1):
                if c < NCH:
                    stA(c)
                if c >= 1:
                    stB(c - 1)
                if c < NCH:
                    stC(c)
                if c >= 1:
                    stD(c - 1)

            if ssd:
                snw = pkv('snw', L, 2)
                for ti in range(2):
                    V("tensor_tensor", [('ymix', ti), T[7]], [('ymix', ti)], out=ymix[:, ti, :], in0=ymix[:, ti, :], in1=sz[:, ti, :], op=ALU.mult)
                sq2 = T[3][:, 0:512].bitcast(BF16).rearrange("p (t l) -> p t l", t=2, l=512)
                rs2 = T[3][:, 512:1024]
                for b in range(3):
                    bs = slice(b * 512, (b + 1) * 512)
                    ACT(sq2, ymix[:, 0:2, bs], AF.Square, [('ymix', 0), ('ymix', 1)], [T[3]])
                    ss = PB()
                    mm_acc(ss[:], [(onesb[:], sq2[:, ti, :]) for ti in range(2)], R=[T[3], onesb], W=[ss])
                    ACT(rs2, ss[:], AF.Ln, [ss, small], [T[3]], bias=small[:, 0:1], scale=4.0)
                    ACT(rs2, rs2, AF.Exp, [T[3]], [T[3]], scale=-0.5)
                    for ti in range(2):
                        V("scalar_tensor_tensor", [('ymix', ti), pk, T[3]], [('ymix', ti)], out=ymix[:, ti, bs], in0=ymix[:, ti, bs], scalar=snw[:, l, ti:ti + 1], in1=rs2, op0=ALU.mult, op1=ALU.mult)
            else:
                rgn = pkv('rgn', L, 2)
                yc = T[3][:, 0:512]
                sqr = T[3][:, 512:768].bitcast(BF16)
                rs2 = T[3][:, 1024:1536]
                for ti in range(2):
                    for b in range(3):
                        bs = slice(b * 512, (b + 1) * 512)
                        pm = PB()
                        MM(pm[:], blk64b[:], ymix[:, 2 + ti, bs], [blk64b, ('ymix', 2 + ti)], [pm])
                        V("tensor_tensor", [('ymix', 2 + ti), pm], [T[3]], out=yc, in0=ymix[:, 2 + ti, bs], in1=pm[:], op=ALU.subtract)
                        ACT(sqr, yc, AF.Square, [T[3]], [T[3]])
                        pv2 = PB()
                        MM(pv2[:], blk64b[:], sqr, [blk64b, T[3]], [pv2])
                        ACT(rs2, pv2[:], AF.Ln, [pv2, small], [T[3]], bias=small[:, 0:1], scale=1.0)
                        ACT(rs2, rs2, AF.Exp, [T[3]], [T[3]], scale=-0.5)
                        V("tensor_tensor", [T[3]], [T[3]], out=yc, in0=yc, in1=rs2, op=ALU.mult)
                        V("scalar_tensor_tensor", [T[3], pk, T[7]], [('ymix', 2 + ti)], out=ymix[:, 2 + ti, bs], in0=yc, scalar=rgn[:, l, ti:ti + 1], in1=sg[:, ti, bs], op0=ALU.mult, op1=ALU.mult)

        fin_s5 = sb("fin_s5", [128, NSEG, 2, 8, 2])
        s5p = sb("s5p", [128, 2, 8, 12])

        def s5_mixer(l):
            TWO_PI = 2.0 * math.pi
            ub = bfv(T[0], 2, NT)
            wre = T[1]; wim = T[2]
            E1r = T[3][:, 0:256]; E1i = T[3][:, 256:512]; E2r = T[3][:, 512:768]; E2i = T[3][:, 768:1024]
            ang = T[3][:, 1024:1280]; kbuf = T[3][:, 1280:1536].bitcast(I32)
            t1 = T[4][:, 0:512]; t2 = T[4][:, 512:1024]
            xb = bfv(T[5], 2, NT)
            acc = [T[6], T[7]]
            lre = pkv('lre', L, 2, 8); lim = pkv('lim', L, 2, 8); lstep = pkv('lstep', L, 2, 8)
            s5d = pkv('s5d', L, 2); s5init = pkv('s5init', L, 2, 8, 2)
            wu_ = load_w(w_in, l * 1024, 8, 1800, 256)
            wbc = ws[wsi[0] % 3]
            wsi[0] += 1
            BT = wbc[:, 0:4, :].rearrange("p a b -> p (a b)").rearrange("p (r j o) -> p r j o", r=2, j=8, o=128)
            CT = wbc[:, 4:8, :].rearrange("p a b -> p (a b)").rearrange("p (r j o) -> p r j o", r=2, j=8, o=128)
            P.dma("pool", W=[wbc], out=wbc[:, 0:4, :].rearrange("p a b -> p (a b)"), in_=s5bt[l * 128:(l + 1) * 128, :])
            P.dma("pool", W=[wbc], out=wbc[:, 4:8, :].rearrange("p a b -> p (a b)"), in_=s5ct[l * 128:(l + 1) * 128, :])
            V("tensor_scalar", [wbc], [wbc], out=CT[:, 1], in0=CT[:, 1], scalar1=-1.0, scalar2=None, op0=ALU.mult)
            for ti in range(2):
                for b in range(3):
                    ps = dense_fm(wu_, ti * 128, b)
                    A("copy", [ps], [T[0]], out=ub[:, ti, b * 512:(b + 1) * 512], in_=ps[:])
            def sp(k):
                return s5p[:, :, :, k]
            ACT(sp(0), lstep[:, l], AF.Exp, [pk], [s5p])
            V("tensor_tensor", [s5p, pk], [s5p], out=sp(1), in0=lre[:, l], in1=sp(0), op=ALU.mult)
            ACT(sp(1), sp(1), AF.Exp, [s5p], [s5p])
            V("tensor_tensor", [s5p, pk], [s5p], out=sp(2), in0=lim[:, l], in1=sp(0), op=ALU.mult)
            def sincos(dst_sin, dst_cos, src, scr_f, scr_i, keyR, keyW):
                for (dst, off) in ((dst_sin, 0.0), (dst_cos, math.pi / 2.0)):
                    V("tensor_scalar", keyR, keyW, out=scr_i, in0=src, scalar1=off, scalar2=1.0 / TWO_PI, op0=ALU.add, op1=ALU.mult)
                    V("tensor_copy", keyW, keyW, out=scr_f, in_=scr_i)
                    V("tensor_scalar", keyW, keyW, out=scr_f, in0=scr_f, scalar1=-TWO_PI, scalar2=off, op0=ALU.mult, op1=ALU.add)
                    V("tensor_tensor", keyR + keyW, keyW, out=scr_f, in0=scr_f, in1=src, op=ALU.add)
                    V("tensor_scalar", keyW, keyW, out=scr_f, in0=scr_f, scalar1=3.141592, scalar2=-3.141592, op0=ALU.min, op1=ALU.max)
                    ACT(dst, scr_f, AF.Sin, keyW, keyW)
            pscr_i = small[:, 64:80].bitcast(I32).rearrange("p (d j) -> p d j", d=2, j=8)
            pscr_f = small[:, 80:96].rearrange("p (d j) -> p d j", d=2, j=8)
            sincos(sp(4), sp(3), sp(2), pscr_f, pscr_i, [s5p, small], [s5p, small])
            V("tensor_tensor", [s5p], [s5p], out=sp(3), in0=sp(3), in1=sp(1), op=ALU.mult)
            V("tensor_tensor", [s5p], [s5p], out=sp(4), in0=sp(4), in1=sp(1), op=ALU.mult)
            V("tensor_tensor", [s5p, pk], [s5p], out=sp(8), in0=lre[:, l], in1=lre[:, l], op=ALU.mult)
            V("tensor_tensor", [s5p, pk], [s5p], out=sp(9), in0=lim[:, l], in1=lim[:, l], op=ALU.mult)
            V("tensor_tensor", [s5p], [s5p], out=sp(8), in0=sp(8), in1=sp(9), op=ALU.add)
            V("reciprocal", [s5p], [s5p], out=sp(8), in_=sp(8))
            V("tensor_scalar", [s5p], [s5p], out=sp(9), in0=sp(3), scalar1=-1.0, scalar2=None, op0=ALU.add)
            V("tensor_tensor", [s5p, pk], [s5p], out=sp(10), in0=sp(9), in1=lre[:, l], op=ALU.mult)
            V("tensor_tensor", [s5p, pk], [s5p], out=sp(11), in0=sp(4), in1=lim[:, l], op=ALU.mult)
            V("tensor_tensor", [s5p], [s5p], out=sp(5), in0=sp(10), in1=sp(11), op=ALU.add)
            V("tensor_tensor", [s5p], [s5p], out=sp(5), in0=sp(5), in1=sp(8), op=ALU.mult)
            V("tensor_tensor", [s5p, pk], [s5p], out=sp(10), in0=sp(4), in1=lre[:, l], op=ALU.mult)
            V("tensor_tensor", [s5p, pk], [s5p], out=sp(11), in0=sp(9), in1=lim[:, l], op=ALU.mult)
            V("tensor_tensor", [s5p], [s5p], out=sp(6), in0=sp(10), in1=sp(11), op=ALU.subtract)
            V("tensor_tensor", [s5p], [s5p], out=sp(6), in0=sp(6), in1=sp(8), op=ALU.mult)
            V("tensor_scalar", [s5p], [s5p], out=sp(7), in0=sp(5), scalar1=-1.0, scalar2=None, op0=ALU.mult)
            wsets = [(T[1], T[2]), (T[6], T[7])]
            E1r = T[3][:, 0:256]; E1i = T[3][:, 256:512]
            E2sets = [(T[3][:, 512:768], T[3][:, 768:1024], ('T3', 1), T[3][:, 512:1024]),
                      (T[3][:, 1024:1280], T[3][:, 1280:1536], ('T3', 2), T[3][:, 1024:1536])]
            rt1 = mws[0][:].rearrange("p a b -> p (a b)").bitcast(F32)
            rt2 = mws[1][:].rearrange("p a b -> p (a b)").bitcast(F32)
            kb_i = mws[2][:].rearrange("p a b -> p (a b)").bitcast(I32)
            kf_ = mws[3][:].rearrange("p a b -> p (a b)").bitcast(F32)
            angs2 = stt[:, 0:6, :].rearrange("p a b -> p (a b)")[:, 0:512]
            pacc = [pb[3], pb[4], pb[5]]
            pb_lim[0] = 3
            items = []
            for ot in range(2):
                for d in range(2):
                    for j in range(4 * ot, 4 * ot + 4):
                        items.append((ot, d, j, len(items)))

            def ctx(it):
                ot, d, j, idx = it
                wre, wim = wsets[idx % 2]
                E2r, E2i, e2k, E2both = E2sets[idx % 2]
                ec0 = 40 + 4 * (idx % 2)
                return dict(ot=ot, d=d, j=j, idx=idx, wre=wre, wim=wim, E2r=E2r, E2i=E2i, e2k=e2k, E2both=E2both,
                            e2c=small[:, ec0:ec0 + 4], eck=('small', 'e2c', idx % 2), ecol=255 if d == 0 else 0,
                            ramp=csv('rampf') if d == 0 else csv('rampb'))

            def pcol_(c, k):
                return s5p[:, c['d'], c['j'], k:k + 1]

            def tables(c):
                ramp, E2r, E2i, e2k, e2c, eck, ecol = c['ramp'], c['E2r'], c['E2i'], c['e2k'], c['e2c'], c['eck'], c['ecol']
                ACT(angs2[:, 256:512], ramp, AF.Identity, [cst, s5p], [stt], scale=pcol_(c, 2))
                ACT(angs2[:, 0:256], ramp, AF.Identity, [cst, s5p], [stt], scale=pcol_(c, 2), bias=math.pi / 2.0)
                V("tensor_scalar", [stt], [mws[2]], out=kb_i, in0=angs2, scalar1=1.0 / TWO_PI, scalar2=None, op0=ALU.mult)
                V("tensor_copy", [mws[2]], [mws[3]], out=kf_, in_=kb_i)
                V("scalar_tensor_tensor", [mws[3], stt], [mws[3]], out=kf_, in0=kf_, scalar=-TWO_PI, in1=angs2, op0=ALU.mult, op1=ALU.add)
                V("tensor_scalar", [mws[3]], [mws[3]], out=kf_, in0=kf_, scalar1=3.141592, scalar2=-3.141592, op0=ALU.min, op1=ALU.max)
                ACT(c['E2both'], kf_, AF.Sin, [mws[3]], [e2k])
                ACT(e2c[:, 0:1], E2r[:, ecol:ecol + 1], AF.Identity, [e2k, pk], [eck], scale=fcol)
                ACT(e2c[:, 1:2], E2i[:, ecol:ecol + 1], AF.Identity, [e2k, pk], [eck], scale=fcol)
                ACT(e2c[:, 2:3], e2c[:, 1:2], AF.Identity, [eck], [eck], scale=-1.0)
                ACT(e2c[:, 3:4], E2i[:, ecol:ecol + 1], AF.Identity, [e2k], [eck], scale=-1.0)
                ACT(E1r, E2r, AF.Identity, [e2k, s5p], [('T3', 0)], scale=pcol_(c, 5))
                ACT(E1i, E2r, AF.Identity, [e2k, s5p], [('T3', 0)], scale=pcol_(c, 6))

            def tables_b(c):
                E2i, e2k = c['E2i'], c['e2k']
                V("scalar_tensor_tensor", [e2k, ('T3', 0), s5p], [('T3', 0)], out=E1r, in0=E2i, scalar=pcol_(c, 6), in1=E1r, op0=ALU.mult, op1=ALU.add)
                V("scalar_tensor_tensor", [e2k, ('T3', 0), s5p], [('T3', 0)], out=E1i, in0=E2i, scalar=pcol_(c, 7), in1=E1i, op0=ALU.mult, op1=ALU.add)

            def rotate(c):
                j, wre, wim = c['j'], c['wre'], c['wim']
                kt_u = j // 4
                E1r3 = E1r.unsqueeze(1).to_broadcast([128, 2, 256]); E1i3 = E1i.unsqueeze(1).to_broadcast([128, 2, 256])
                for b_ in range(3):
                    bs = slice(b_ * 512, (b_ + 1) * 512)
                    pr = PB()
                    MM(pr[:], BT[:, 0, j, :], ub[:, kt_u, bs], [wbc, T[0]], [pr])
                    pi_ = PB()
                    MM(pi_[:], BT[:, 1, j, :], ub[:, kt_u, bs], [wbc, T[0]], [pi_])
                    pr3 = pr[:].rearrange("p (s t) -> p s t", s=2, t=256); pi3 = pi_[:].rearrange("p (s t) -> p s t", s=2, t=256)
                    t13 = rt1.rearrange("p (s t) -> p s t", s=2, t=256); t23 = rt2.rearrange("p (s t) -> p s t", s=2, t=256)
                    wre3 = wre[:, bs].rearrange("p (s t) -> p s t", s=2, t=256); wim3 = wim[:, bs].rearrange("p (s t) -> p s t", s=2, t=256)
                    V("tensor_tensor", [pr, ('T3', 0)], [mws[0]], out=t13, in0=pr3, in1=E1r3, op=ALU.mult)
                    V("tensor_tensor", [pi_, ('T3', 0)], [mws[1]], out=t23, in0=pi3, in1=E1i3, op=ALU.mult)
                    V("tensor_tensor", [mws[0], mws[1]], [wre], out=wre3, in0=t13, in1=t23, op=ALU.subtract)
                    V("tensor_tensor", [pr, ('T3', 0), mws[0]], [mws[0]], out=t13, in0=pr3, in1=E1i3, op=ALU.mult)
                    V("tensor_tensor", [pi_, ('T3', 0), mws[1]], [mws[1]], out=t23, in0=pi3, in1=E1r3, op=ALU.mult)
                    V("tensor_tensor", [mws[0], mws[1]], [wim], out=wim3, in0=t13, in1=t23, op=ALU.add)

            def scans(c):
                d, j, wre, wim, e2c, eck, ecol, e2k, E2r, E2i = c['d'], c['j'], c['wre'], c['wim'], c['e2c'], c['eck'], c['ecol'], c['e2k'], c['E2r'], c['E2i']
                rho3 = pcol_(c, 1).to_broadcast([128, 256])
                order = list(range(NSEG)) if d == 0 else [3, 2, 1, 0, 5, 4]
                ic = small[:, 32:34]
                for sg in order:
                    first = (sg == 0 and d == 0) or (sg == 3 and d == 1)
                    if d == 0:
                        sl_ = slice(sg * 256, (sg + 1) * 256)
                    else:
                        sl_ = slice(sg * 256 + 255, (sg * 256 - 1) if sg > 0 else None, -1)
                    if sg >= 4:
                        ini = (0.0, 0.0); rk = []
                    elif first:
                        ini = (s5init[:, l, d, j, 0:1], s5init[:, l, d, j, 1:2]); rk = [pk]
                    else:
                        prev = sg - 1 if d == 0 else sg + 1
                        pcol = prev * 256 + ecol
                        V("tensor_scalar", [wre, eck], [('small', 'ic')], out=ic[:, 0:1], in0=wre[:, pcol:pcol + 1], scalar1=e2c[:, 0:1], scalar2=None, op0=ALU.mult)
                        V("scalar_tensor_tensor", [wim, eck, ('small', 'ic')], [('small', 'ic')], out=ic[:, 0:1], in0=wim[:, pcol:pcol + 1], scalar=e2c[:, 2:3], in1=ic[:, 0:1], op0=ALU.mult, op1=ALU.add)
                        V("tensor_scalar", [wre, eck], [('small', 'ic')], out=ic[:, 1:2], in0=wre[:, pcol:pcol + 1], scalar1=e2c[:, 1:2], scalar2=None, op0=ALU.mult)
                        V("scalar_tensor_tensor", [wim, eck, ('small', 'ic')], [('small', 'ic')], out=ic[:, 1:2], in0=wim[:, pcol:pcol + 1], scalar=e2c[:, 0:1], in1=ic[:, 1:2], op0=ALU.mult, op1=ALU.add)
                        ini = (ic[:, 0:1], ic[:, 1:2]); rk = [('small', 'ic')]
                    V("tensor_tensor_scan", [wre, s5p] + rk, [wre], out=wre[:, sl_], data0=rho3, data1=wre[:, sl_], initial=ini[0], op0=ALU.mult, op1=ALU.add)
                    V("tensor_tensor_scan", [wim, s5p] + rk, [wim], out=wim[:, sl_], data0=rho3, data1=wim[:, sl_], initial=ini[1], op0=ALU.mult, op1=ALU.add)
                f6 = small[:, 48:60]
                wre_e = v3(wre)[:, :, ecol]; wim_e = v3(wim)[:, :, ecol]
                V("tensor_scalar", [wre, e2k], [('small', 'f6')], out=f6[:, 0:6], in0=wre_e, scalar1=E2r[:, ecol:ecol + 1], scalar2=None, op0=ALU.mult)
                V("scalar_tensor_tensor", [wim, eck, ('small', 'f6')], [fin_s5], out=fin_s5[:, :, d, j, 0], in0=wim_e, scalar=e2c[:, 3:4], in1=f6[:, 0:6], op0=ALU.mult, op1=ALU.add)
                V("tensor_scalar", [wre, e2k], [('small', 'f6')], out=f6[:, 6:12], in0=wre_e, scalar1=E2i[:, ecol:ecol + 1], scalar2=None, op0=ALU.mult)
                V("scalar_tensor_tensor", [wim, e2k, ('small', 'f6')], [fin_s5], out=fin_s5[:, :, d, j, 1], in0=wim_e, scalar=E2r[:, ecol:ecol + 1], in1=f6[:, 6:12], op0=ALU.mult, op1=ALU.add)

            def unrot(c, eng, sp_, ta2, tb2, tkeys):
                wre, wim, e2k, E2r, E2i = c['wre'], c['wim'], c['e2k'], c['E2r'], c['E2i']
                sgs = slice(2 * sp_, 2 * sp_ + 2)
                E2r6 = E2r.unsqueeze(1).to_broadcast([128, 2, 256]); E2i6 = E2i.unsqueeze(1).to_broadcast([128, 2, 256])
                ta = ta2.rearrange("p (s t) -> p s t", s=2, t=256); tb = tb2.rearrange("p (s t) -> p s t", s=2, t=256)
                wr3 = v3(wre)[:, sgs, :]; wi3 = v3(wim)[:, sgs, :]
                xr3 = xb[:, 0, :].rearrange("p (s t) -> p s t", s=NSEG, t=256)[:, sgs, :]
                xi3 = xb[:, 1, :].rearrange("p (s t) -> p s t", s=NSEG, t=256)[:, sgs, :]
                extra = [fin_s5] if eng == "pool" else []
                P.op(eng, "tensor_tensor", [wre, e2k] + extra, [tkeys[0]], out=ta, in0=wr3, in1=E2r6, op=ALU.mult)
                P.op(eng, "tensor_tensor", [wim, e2k], [tkeys[1]], out=tb, in0=wi3, in1=E2i6, op=ALU.mult)
                P.op(eng, "tensor_tensor", [tkeys[0], tkeys[1]], [('S5X', 0, sp_)], out=xr3, in0=ta, in1=tb, op=ALU.subtract)
                P.op(eng, "tensor_tensor", [wre, e2k, tkeys[0]], [tkeys[0]], out=ta, in0=wr3, in1=E2i6, op=ALU.mult)
                P.op(eng, "tensor_tensor", [wim, e2k, tkeys[1]], [tkeys[1]], out=tb, in0=wi3, in1=E2r6, op=ALU.mult)
                P.op(eng, "tensor_tensor", [tkeys[0], tkeys[1]], [('S5X', 1, sp_)], out=xi3, in0=ta, in1=tb, op=ALU.add)

            def cmat(c, n_in_ot):
                j = c['j']
                for b_ in range(3):
                    bs = slice(b_ * 512, (b_ + 1) * 512)
                    MM(pacc[b_][:], CT[:, 0, j, :], xb[:, 0, bs], [wbc, ('S5X', 0, b_)], [pacc[b_]], start=(n_in_ot == 0), stop=False)
                    MM(pacc[b_][:], CT[:, 1, j, :], xb[:, 1, bs], [wbc, ('S5X', 1, b_)], [pacc[b_]], start=False, stop=(n_in_ot == 7))

            cs_ = [ctx(it) for it in items]
            tables(cs_[0])
            tables_b(cs_[0])
            rotate(cs_[0])
            for i, c in enumerate(cs_):
                if i + 1 < len(cs_):
                    tables(cs_[i + 1])
                scans(c)
                unrot(c, "pool", 0, T[4][:, 0:512], T[4][:, 512:1024], [('T4', 'D', 0), ('T4', 'D', 1)])
                unrot(c, "pool", 1, T[4][:, 0:512], T[4][:, 512:1024], [('T4', 'D', 0), ('T4', 'D', 1)])
                if i + 1 < len(cs_):
                    tables_b(cs_[i + 1])
                    rotate(cs_[i + 1])
                unrot(c, "dve", 2, rt1, rt2, [mws[0], mws[1]])
                cmat(c, i % 8)
                if i % 8 == 7:
                    ot = c['ot']
                    for b_ in range(3):
                        bs = slice(b_ * 512, (b_ + 1) * 512)
                        V("scalar_tensor_tensor", [T[0], pk, pacc[b_]], [('ymix', 4 + ot)], out=ymix[:, 4 + ot, bs], in0=ub[:, ot, bs], scalar=s5d[:, l, ot:ot + 1], in1=pacc[b_][:], op0=ALU.mult, op1=ALU.add)
            pb_lim[0] = 6
            for sg in range(NSEG):
                for d in range(2):
                    dst = o_s5[sg, l, d].rearrange("(j p) r -> p j r", p=128)
                    P.dma("sp", R=[fin_s5], is_output=True, out=dst, in_=fin_s5[:, sg, d], allow_slow_non_contiguous=True)
            wg_ = load_w(glu_w, l * 256, 2, 0, 512)
            yk = [('ymix', 4), ('ymix', 5)]
            gl = T[5]
            for b_ in range(3):
                bs = slice(b_ * 512, (b_ + 1) * 512)
                pgs = []; pvs = []
                for ti in range(2):
                    pgs.append(dense_fm(wg_, 256 + ti * 128, b_, nk=2, rhs=ymix[:, 4:6, :], rkeys=yk))
                    pvs.append(dense_fm(wg_, ti * 128, b_, nk=2, rhs=ymix[:, 4:6, :], rkeys=yk))
                for ti in range(2):
                    sgm = gl[:, ti * 512:(ti + 1) * 512]
                    ACT(sgm, pgs[ti][:], AF.Sigmoid, [pgs[ti]], [gl])
                    V("tensor_tensor", [pvs[ti], gl], [('ymix', 4 + ti)], out=ymix[:, 4 + ti, bs], in0=sgm, in1=pvs[ti][:], op=ALU.mult)

        modulation_all(0)
        for l in range(depth):
            mod_finish(l)
            norm_mod(l, 0)
            if 'ssd' in enabled:
                attn_mixer(l, 'ssd')
            if 'ret' in enabled:
                attn_mixer(l, 'ret')
            if 's5' in enabled:
                s5_mixer(l)
            if 'lru' in enabled:
                lru_mixer(l)
            out_proj(l)
            if 'ffn' not in enabled and l + 1 < depth:
                modulation_all(l + 1)
            if 'ffn' in enabled:
                norm_mod(l, 1)
                ffn(l)
                if l + 1 < depth:
                    mod_tail(l + 1)
                if len(enabled & {'ssd', 'ret', 's5', 'lru'}) < 4 and not DEBUG_TAPS:
                    V("memset", [], ALLY, ap=ymix[:], constant=0.0)
        final_out()
        if DEBUG_TAPS:
            d_mod = dout("dbg_mod", [128, 96])
            P.dma("sp", R=[modsb], is_output=True, out=d_mod, in_=modsb[:].rearrange("p a b -> p (a b)"))
            d_h = dout("dbg_h", [128, KT * NT])
            P.dma("pool", R=[('h', 0), ('h', 1), ('h', 2)], is_output=True, out=d_h, in_=h[:].rearrange("p a b -> p (a b)"))
            d_x = dout("dbg_x", [128, KT * NT])
            P.dma("sp", R=[('x', 0), ('x', 1), ('x', 2)], is_output=True, out=d_x, in_=x[:].rearrange("p a b -> p (a b)"))
            d_y = dout("dbg_ymix", [128, KT * NT])
            P.dma("pool", R=ALLY, is_output=True, out=d_y, in_=ymix[:].rearrange("p a b -> p (a b)"))
        P.run()
    return nc


_CACHE = {}


def kernel(**inp):
    inp = {k: np.asarray(v) for k, v in inp.items()}
    enabled = frozenset(ENABLED)
    depth = DEPTH
    cso = _consts()
    cst = cso.build()
    shared = _shared_layouts(inp)
    in_maps = []
    pko = None
    for core in range(8):
        pko = _core_pack(inp, core)
        (ka, ia), ib = _assign(core)
        if ka == 's':
            xa = inp['x_sample'][ia]
            ssd_i = _state_layout(inp['state_ssd'][ia], 'ssd')
            ret_i = _state_layout(inp['state_ret'][ia], 'ret')
        else:
            xa = inp['x_prompt'][ia].reshape(1024, D)
            ssd_i = np.zeros((L * 128, 256), np.float32)
            ret_i = np.zeros((L * 128, 256), np.float32)
        xb = inp['x_prompt'][ib].reshape(512, D)
        cos, sin = _rot_tables(ka == 's')
        m = dict(shared)
        m.update(xin=np.ascontiguousarray(np.concatenate([xa, xb], axis=0), dtype=np.float32), pk=pko.build(), cst=cst,
                 cosT=cos, sinT=sin, ssdinit=ssd_i, retinit=ret_i)
        in_maps.append(m)
    key = (enabled, depth, DEBUG_TAPS)
    if key not in _CACHE:
        _CACHE[key] = build(pko, cso, enabled, depth)
    nc = _CACHE[key]
    res = run_bass_kernel_spmd(nc, in_maps, core_ids=list(range(8)))
    outs = res.results
    LAST['outs'] = outs
    y_p = np.zeros((32, 256, D), np.float32)
    y_s = np.zeros((4, 1024, D), np.float32)
    n_ssd = np.zeros((32, L, 2, 4, 64, 64), np.float32)
    n_ret = np.zeros((32, L, 2, 4, 64, 64), np.float32)
    n_s5 = np.zeros((32, L, 2, 16, 64, 2), np.float32)
    n_lru = np.zeros((32, L, 2, 256), np.float32)
    for core in range(8):
        r = outs[core]
        (ka, ia), ib = _assign(core)
        y = r['y']
        segs = []
        if ka == 's':
            y_s[ia] = y[0:1024]
        else:
            for k, bidx in enumerate(ia):
                segs.append((k, bidx))
        for k, bidx in enumerate(ib):
            segs.append((4 + k, bidx))
        for sgi, bidx in segs:
            y_p[bidx] = y[sgi * 256:(sgi + 1) * 256]
            n_ssd[bidx] = r['o_ssd'][sgi]
            n_ret[bidx] = r['o_ret'][sgi]
            n_s5[bidx] = r['o_s5'][sgi].reshape(L, 2, 16, 64, 2)
            n_lru[bidx] = r['o_lru'][sgi]
    return (y_p, y_s, n_ssd, n_ret, n_s5, n_lru)
```

```python
import math
from contextlib import ExitStack
import numpy as np
import concourse.bass as bass
import concourse.mybir as mybir
from concourse.bass_utils import run_bass_kernel_spmd

F32 = mybir.dt.float32
BF16 = mybir.dt.bfloat16
I32 = mybir.dt.int32
ALU = mybir.AluOpType
AF = mybir.ActivationFunctionType

ENABLED = {'ssd', 'ret', 's5', 'lru', 'ffn'}
DEPTH = 4
DEBUG_TAPS = False
LAST = {}

L = 4; D = 1024; KT = 8; NT = 1536; NSEG = 6; SEG = 256; NCH = 12; CH = 128
DFF = 2816; NJ = 22
EPS = 1e-6
ENGS = ("pe", "act", "dve", "pool", "sp")
N_DMA_SLOTS = 32


class Prog:
    def __init__(self, nc):
        self.nc = nc
        self.ops = {e: [] for e in ENGS}
        self.cnt = {}
        self.last_w = {}
        self.readers = {}
        self.known = {e: {} for e in ENGS}
        self.dma_rr = 0
        self.dma_rr_q = {}
        self.out_dmas = []

    EXPAND = {"T3": [("T3", 0), ("T3", 1), ("T3", 2)],
              "T4": [("T4", "D", 0), ("T4", "D", 1), ("T4", "Ds"), ("T4", "q"), ("T4", "P"), ("T4", "b", 0), ("T4", "b", 1), ("T4", "b", 2)],
              "T7": [("T7", 0), ("T7", 1), ("T7", 2)]}

    @classmethod
    def _keys(cls, lst):
        out = []
        for k in lst:
            if k is None:
                continue
            if not isinstance(k, (str, tuple)):
                k = k.tensor.name if hasattr(k, 'tensor') else k.name
            if k in cls.EXPAND:
                out.extend(cls.EXPAND[k])
            else:
                out.append(k)
        return out

    def _deps(self, eng, reads, writes):
        need = {}

        def add(sv):
            if sv is None:
                return
            s, v = sv
            if need.get(s, 0) < v:
                need[s] = v
        for k in reads:
            add(self.last_w.get(k))
        for k in writes:
            add(self.last_w.get(k))
            for sv in self.readers.get(k, ()):
                add(sv)
        waits = []
        kn = self.known[eng]
        for s, v in need.items():
            if kn.get(s, 0) >= v:
                continue
            kn[s] = v
            waits.append((s, v))
        return waits

    def _record(self, semkey, val, reads, writes):
        for k in reads:
            self.readers.setdefault(k, []).append((semkey, val))
        for k in writes:
            self.last_w[k] = (semkey, val)
            self.readers[k] = []

    def op(self, eng, name, R=(), W=(), noself=False, **kw):
        fn = (name, kw)
        reads = self._keys(R)
        writes = self._keys(W)
        waits = self._deps(eng, reads, writes)
        if noself:
            waits = [(s, v) for (s, v) in waits if s != eng]
        self.cnt[eng] = self.cnt.get(eng, 0) + 1
        val = self.cnt[eng]
        self.ops[eng].append((fn, waits, (eng, 1)))
        self._record(eng, val, reads, writes)

    def dma(self, queue, R=(), W=(), is_output=False, **kw):
        fn = ("dma_start", kw)
        reads = self._keys(R)
        writes = self._keys(W)
        half = N_DMA_SLOTS // 2
        rr = self.dma_rr_q.get(queue, 0)
        self.dma_rr_q[queue] = (rr + 1) % half
        slot = rr + (half if queue == "pool" else 0)
        sk = ("dma", slot)
        waits = self._deps(queue, reads, writes)
        prev = self.cnt.get(sk, 0)
        if prev and self.known[queue].get(sk, 0) < prev:
            self.known[queue][sk] = prev
            waits.append((sk, prev))
        self.cnt[sk] = prev + 1
        val = prev + 1
        self.ops[queue].append((fn, waits, (sk, 16)))
        self._record(sk, val, reads, writes)
        if is_output:
            self.out_dmas.append((sk, val))

    def run(self):
        nc = self.nc
        with ExitStack() as st:
            sems = {}
            for e in ENGS:
                sems[e] = st.enter_context(nc.semaphore("s_" + e))
            for i in range(N_DMA_SLOTS):
                sems[("dma", i)] = st.enter_context(nc.semaphore("s_dma%d" % i))
            block = st.enter_context(nc.Block())
            mult = lambda s: 16 if isinstance(s, tuple) else 1

            def replay(ename, eh, final=False):
                for fn, waits, (isem, iamt) in self.ops[ename]:
                    for s, v in waits:
                        eh.wait_ge(sems[s], v * mult(s))
                    ins = getattr(eh, fn[0])(**fn[1])
                    ins.then_inc(sems[isem], iamt)
                if final:
                    done = {}
                    for s, v in self.out_dmas:
                        done[s] = max(done.get(s, 0), v)
                    for s, v in done.items():
                        eh.wait_ge(sems[s], v * 16)

            @block.tensor
            def _(e):
                replay("pe", e)

            @block.scalar
            def _(e):
                replay("act", e)

            @block.vector
            def _(e):
                replay("dve", e)

            @block.gpsimd
            def _(e):
                replay("pool", e)

            @block.sync
            def _(e):
                replay("sp", e, final=True)


def _cols(v, nt):
    return np.ascontiguousarray(np.asarray(v, np.float32).reshape(nt, 128).T)


class Pack:
    def __init__(self):
        self.off = {}
        self.n = 0
        self.arrs = []

    def add(self, name, arr):
        a = np.ascontiguousarray(arr, np.float32).reshape(128, -1)
        self.off[name] = (self.n, a.shape[1])
        self.n += a.shape[1]
        self.arrs.append(a)

    def build(self):
        return np.ascontiguousarray(np.concatenate(self.arrs, axis=1))


def _consts():
    pk = Pack()
    i = np.arange(128)
    pk.add('ident', np.eye(128))
    pk.add('maskU', (i[:, None] <= i[None, :]).astype(np.float32))
    pk.add('maskL', (i[:, None] >= i[None, :]).astype(np.float32))
    pk.add('negF', np.where(i[:, None] <= i[None, :], 0.0, -30000.0))
    pk.add('negB', np.where(i[:, None] >= i[None, :], 0.0, -30000.0))
    pk.add('ones', np.ones((128, 128)))
    rp = np.zeros((128, 128), np.float32)
    for hb in (0, 64):
        for d in range(64):
            if (d % 32) < 16:
                rp[hb + d + 16, hb + d] = -1.0
            else:
                rp[hb + d - 16, hb + d] = 1.0
    pk.add('rperm', rp)
    hh = np.arange(4, dtype=np.float32)
    lf = np.log1p(-2.0 ** (-5.0 - 0.0 - hh)).astype(np.float32)
    lb = np.log1p(-2.0 ** (-5.0 - 0.5 - hh)).astype(np.float32)
    la = np.stack([lf, lb], axis=0)[None, :, None, :] * np.ones((128, 1, NCH, 1), np.float32)
    pk.add('retla', la)
    b64 = np.zeros((128, 128), np.float32)
    b64[0:64, 0:64] = 1.0 / 64.0
    b64[64:128, 64:128] = 1.0 / 64.0
    pk.add('blk64', b64)
    tt_ = np.arange(256, dtype=np.float32)
    pk.add('rampf', np.broadcast_to(tt_ + 1.0, (128, 256)))
    pk.add('rampb', np.broadcast_to(256.0 - tt_, (128, 256)))
    return pk


def _rot_tables(sample):
    cos = np.ones((128, 1024), np.float32)
    sin = np.zeros((128, 1024), np.float32)
    if sample:
        t = np.arange(1024)
        row = (t // 64).astype(np.float32)
        col = (t % 64).astype(np.float32)
        nf = 16
        inv = (10000.0 ** (-np.arange(nf, dtype=np.float32) / nf)).astype(np.float32)
        for p in range(128):
            d = p % 64
            half = d // 32
            f = d % 16
            ang = (row if half == 0 else col) * inv[f]
            cos[p] = np.cos(ang)
            sin[p] = np.sin(ang)
    return cos, sin


def _shared_layouts(inp):
    sh = {}
    sh['w_in'] = np.ascontiguousarray(inp['w_in'].reshape(L * 1024, 2568))
    sh['w_out'] = np.ascontiguousarray(inp['w_out'].reshape(L * 1024, 1024))
    sh['w_up'] = np.ascontiguousarray(inp['ffn_w_up'].reshape(L * 1024, 2 * DFF))
    sh['w_down'] = np.ascontiguousarray(inp['ffn_w_down'].reshape(L * DFF, 1024))
    sh['mod_w'] = np.ascontiguousarray(inp['mod_w'].reshape(L * 1024, 6144))
    sh['glu_w'] = np.ascontiguousarray(inp['s5_glu_w'].reshape(L * 256, 512))
    lw = np.zeros((L, 128, 2, 2, 2, 128), np.float32)
    for wi, nm in enumerate(('lru_wa', 'lru_wx')):
        w = inp[nm]
        for ti in range(2):
            for bb in range(2):
                blk = ti * 2 + bb
                lw[:, bb * 64:(bb + 1) * 64, wi, :, ti, bb * 64:(bb + 1) * 64] = np.transpose(w[:, :, blk], (0, 2, 1, 3))
    sh['lruw'] = np.ascontiguousarray(lw.reshape(L * 128, 2 * 2 * 2 * 128))
    bt = np.zeros((L, 128, 2, 8, 128), np.float32)
    ct = np.zeros((L, 128, 2, 8, 128), np.float32)
    for ri, (nb, ncn) in enumerate((('s5_b_re', 's5_c_re'), ('s5_b_im', 's5_c_im'))):
        b = inp[nb]
        c = inp[ncn]
        for g in range(16):
            j = g // 2
            gl = g % 8
            co = (g % 2) * 64
            bt[:, gl * 16:(gl + 1) * 16, ri, j, co:co + 64] = np.transpose(b[:, g], (0, 2, 1))
            ct[:, co:co + 64, ri, j, gl * 16:(gl + 1) * 16] = np.transpose(c[:, g], (0, 2, 1))
    sh['s5bt'] = np.ascontiguousarray(bt.reshape(L * 128, 2 * 8 * 128))
    sh['s5ct'] = np.ascontiguousarray(ct.reshape(L * 128, 2 * 8 * 128))
    return sh


def _head_map(kind, h):
    if kind == 'ssd':
        return (h // 2) * 64, h % 2
    return (h % 2) * 64, h // 2


def _state_layout(st, kind):
    o = np.zeros((L, 128, 2, 2, 64), np.float32)
    for h in range(4):
        base, slot = _head_map(kind, h)
        o[:, base:base + 64, :, slot, :] = np.transpose(st[:, :, h], (0, 2, 1, 3))
    return np.ascontiguousarray(o.reshape(L * 128, 256))


def _core_pack(inp, core):
    sample = core < 4
    pk = Pack()
    condA = inp['c'][core] if sample else inp['c_ctx']
    cond = np.stack([_cols(condA, 8), _cols(inp['c_ctx'], 8)], axis=-1)
    pk.add('cond', cond)
    pk.add('f', np.full((128, 1), 1.0 if sample else 0.0, np.float32))
    pk.add('n1w', np.stack([_cols(inp['norm1_w'][l], 8) for l in range(L)], axis=1))
    pk.add('n2w', np.stack([_cols(inp['norm2_w'][l], 8) for l in range(L)], axis=1))
    pk.add('modb', np.stack([_cols(inp['mod_b'][l], 48) for l in range(L)], axis=1))
    pk.add('fnw', _cols(inp['final_norm_w'], 8))
    pk.add('lcw', np.stack([np.stack([_cols(inp['lru_conv_w'][l, j], 2) for j in range(4)], axis=-1) for l in range(L)], axis=1))
    pk.add('lcb', np.stack([_cols(inp['lru_conv_b'][l], 2) for l in range(L)], axis=1))
    pk.add('llam', np.stack([np.stack([_cols(inp['lru_lambda'][l, d], 2) for d in range(2)], axis=1) for l in range(L)], axis=1))
    pk.add('lba', np.stack([np.stack([_cols(inp['lru_ba'][l, d].reshape(-1), 2) for d in range(2)], axis=1) for l in range(L)], axis=1))
    pk.add('lbx', np.stack([np.stack([_cols(inp['lru_bx'][l, d].reshape(-1), 2) for d in range(2)], axis=1) for l in range(L)], axis=1))
    if sample:
        li = inp['state_lru'][core]
    else:
        li = np.zeros((L, 2, 256), np.float32)
    pk.add('linit', np.stack([np.stack([_cols(li[l, d], 2) for d in range(2)], axis=1) for l in range(L)], axis=1))
    pk.add('scw', np.stack([np.stack([_cols(inp['ssd_conv_w'][l, j], 4) for j in range(4)], axis=-1) for l in range(L)], axis=1))
    pk.add('scb', np.stack([_cols(inp['ssd_conv_b'][l], 4) for l in range(L)], axis=1))
    pk.add('sdtb', np.broadcast_to(inp['ssd_dt_bias'].reshape(1, L, 8), (128, L, 8)))
    pk.add('salog', np.broadcast_to(inp['ssd_a_log'].reshape(1, L, 8), (128, L, 8)))
    pk.add('sdcol', np.stack([_cols(np.repeat(inp['ssd_d'][l], 64), 2) for l in range(L)], axis=1))
    pk.add('snw', np.stack([_cols(inp['ssd_norm_w'][l], 2) for l in range(L)], axis=1))
    pk.add('rgn', np.stack([_cols(inp['ret_gn_w'][l], 2) for l in range(L)], axis=1))
    pk.add('fcw', np.stack([np.stack([_cols(inp['ffn_conv_w'][l, j], NJ) for j in range(3)], axis=-1) for l in range(L)], axis=1))
    pk.add('fcb', np.stack([_cols(inp['ffn_conv_b'][l], NJ) for l in range(L)], axis=1))
    pk.add('lre', np.stack([np.stack([_cols(inp['s5_lam_re'][l, d].reshape(-1), 8) for d in range(2)], axis=1) for l in range(L)], axis=1))
    pk.add('lim', np.stack([np.stack([_cols(inp['s5_lam_im'][l, d].reshape(-1), 8) for d in range(2)], axis=1) for l in range(L)], axis=1))
    pk.add('lstep', np.stack([np.stack([_cols(np.repeat(inp['s5_log_step'][l, d], 64), 8) for d in range(2)], axis=1) for l in range(L)], axis=1))
    pk.add('s5d', np.stack([_cols(inp['s5_d'][l], 2) for l in range(L)], axis=1))
    if sample:
        s5i = inp['state_s5'][core]
    else:
        s5i = np.zeros((L, 2, 16, 64, 2), np.float32)
    pk.add('s5init', np.stack([np.stack([np.stack([_cols(s5i[l, d, :, :, ri].reshape(-1), 8) for ri in range(2)], axis=-1)
                                         for d in range(2)], axis=1) for l in range(L)], axis=1))
    return pk


def _assign(core):
    if core < 4:
        return ('s', core), [2 * core, 2 * core + 1]
    base = 8 + (core - 4) * 6
    return ('p', [base, base + 1, base + 2, base + 3]), [base + 4, base + 5]


def build(pko, cso, enabled, depth):
    nc = bass.Bass("TRN2", target_bir_lowering=False)

    def din(name, shape):
        return nc.dram_tensor(name, list(shape), F32, kind="ExternalInput").ap()

    def dout(name, shape):
        return nc.dram_tensor(name, list(shape), F32, kind="ExternalOutput").ap()

    xin = din("xin", [NT, D])
    pk_d = din("pk", [128, pko.n])
    cs_d = din("cst", [128, cso.n])
    cos_d = din("cosT", [128, 1024])
    sin_d = din("sinT", [128, 1024])
    w_in = din("w_in", [L * 1024, 2568])
    w_out = din("w_out", [L * 1024, 1024])
    w_up = din("w_up", [L * 1024, 2 * DFF])
    w_down = din("w_down", [L * DFF, 1024])
    mod_w = din("mod_w", [L * 1024, 6144])
    glu_w = din("glu_w", [L * 256, 512])
    lruw = din("lruw", [L * 128, 1024])
    s5bt = din("s5bt", [L * 128, 2048])
    s5ct = din("s5ct", [L * 128, 2048])
    ssdinit = din("ssdinit", [L * 128, 256])
    retinit = din("retinit", [L * 128, 256])
    y_out = dout("y", [NT, D])
    o_ssd = dout("o_ssd", [NSEG, L, 2, 4, 64, 64])
    o_ret = dout("o_ret", [NSEG, L, 2, 4, 64, 64])
    o_s5 = dout("o_s5", [NSEG, L, 2, 1024, 2])
    o_lru = dout("o_lru", [NSEG, L, 2, 256])

    with ExitStack() as st:
        def sb(name, shape, dt=F32):
            return st.enter_context(nc.sbuf_tensor(name, list(shape), dt))

        def psum(name, shape, dt=F32):
            return st.enter_context(nc.psum_tensor(name, list(shape), dt))

        P = Prog(nc)

        def V(name, R, W, **kw):
            P.op("dve", name, R, W, **kw)

        def A(name, R, W, **kw):
            P.op("act", name, R, W, **kw)

        def ACT(out, in_, func, R, W, **kw):
            P.op("act", "activation", R, W, out=out, in_=in_, func=func, **kw)

        def MM(out, lhsT, rhs, R, W, start=True, stop=True, noself=False):
            P.op("pe", "matmul", R, W, noself=noself, out=out, lhsT=lhsT, rhs=rhs, start=start, stop=stop)

        x = sb("x", [128, KT, NT])
        h = sb("h", [128, KT, NT], BF16)
        ymix = sb("ymix", [128, KT, NT], BF16)
        ws = [sb("ws%d" % i, [128, 8, 512], BF16) for i in range(3)]
        mws = [sb("mws%d" % i, [128, 8, 128], BF16) for i in range(4)]
        cpad = sb("cpad", [128, NSEG, 260])
        pk = sb("pks", [128, pko.n])
        cst = sb("csts", [128, cso.n])
        T = [sb("T%d" % i, [128, NT]) for i in range(8)]
        cs = sb("csil", [128, KT, 2], BF16)
        modsb = sb("modsb", [128, 48, 2])
        modA = sb("modA", [128, 2, KT, 2])
        onesb = sb("onesb", [128, 128], BF16)
        identb = sb("identb", [128, 128], BF16)
        rpermb = sb("rpermb", [128, 128], BF16)
        small = sb("small", [128, 512])
        rs = sb("rs", [128, 256])
        fin_lru = sb("fin_lru", [128, 2, 2, NSEG])
        pb = [psum("pb%d" % i, [128, 512]) for i in range(6)]
        pmod = psum("pmod", [128, 512])
        pbb = psum("pbb", [128, 1024], BF16)
        pbi = [0]
        pb_lim = [6]

        def PB():
            pbi[0] = (pbi[0] + 1) % pb_lim[0]
            return pb[pbi[0]]

        def pkv(name, *dims):
            o, n = pko.off[name]
            v = pk[:, o:o + n]
            if len(dims) == 2:
                v = v.rearrange("p (a b) -> p a b", a=dims[0], b=dims[1])
            elif len(dims) == 3:
                v = v.rearrange("p (a b c) -> p a b c", a=dims[0], b=dims[1], c=dims[2])
            elif len(dims) == 4:
                v = v.rearrange("p (a b c d) -> p a b c d", a=dims[0], b=dims[1], c=dims[2], d=dims[3])
            return v

        def csv(name):
            o, n = cso.off[name]
            return cst[:, o:o + n]

        fcol = pkv('f')
        ALLY = [('ymix', i) for i in range(8)]

        P.dma("sp", W=[pk], out=pk[:], in_=pk_d)
        P.dma("sp", W=[cst], out=cst[:], in_=cs_d)
        V("memset", [], [onesb], ap=onesb[:], constant=1.0 / 1024.0)
        V("tensor_copy", [cst], [identb], out=identb[:], in_=csv('ident'))
        V("tensor_copy", [cst], [rpermb], out=rpermb[:], in_=csv('rperm'))
        V("memset", [], [cpad], ap=cpad[:], constant=0.0)
        V("memset", [], ALLY, ap=ymix[:], constant=0.0)
        V("memset", [], [small], ap=small[:], constant=0.0)
        V("memset", [small], [small], ap=small[:, 0:1], constant=EPS)
        ACT(cs[:], pkv('cond', 8, 2), AF.Silu, [pk], [cs])

        for c in range(NCH):
            xs_ = T[c % 2]
            P.dma("sp", W=[xs_], out=xs_[:, 0:1024], in_=xin[c * 128:(c + 1) * 128, :])
            for half in range(2):
                ps = PB()
                for q in range(4):
                    kt = half * 4 + q
                    P.op("pe", "transpose", [xs_, cst], [ps], out=ps[:, q * 128:(q + 1) * 128], in_=xs_[:, kt * 128:(kt + 1) * 128], identity=csv('ident'))
                A("copy", [ps], [('x', c // 4)], out=x[:, half * 4:half * 4 + 4, c * 128:(c + 1) * 128], in_=ps[:].rearrange("p (a b) -> p a b", a=4, b=128))

        wsi = [0]

        def load_w(dram2d, row0, nk, col0, ncols):
            s = ws[wsi[0] % 3]
            wsi[0] += 1
            src = dram2d[row0:row0 + nk * 128, col0:col0 + ncols].rearrange("(kt p) c -> p kt c", p=128)
            P.dma("pool", W=[s], out=s[:, 0:nk, 0:ncols], in_=src)
            return s

        def mm_acc(ps_ap, pairs, R, W):
            n = len(pairs)
            for i, (lt, rh) in enumerate(pairs):
                MM(ps_ap, lt, rh, R, W if i in (0, n - 1) else [], start=(i == 0), stop=(i == n - 1), noself=(i > 0))

        def dense_fm(wslot, c0, b, nk=8, rhs=None, rkeys=None):
            ps = PB()
            if rhs is None:
                rhs = h
                rkeys = [('h', b)]
            pairs = [(wslot[:, kt, c0:c0 + 128], rhs[:, kt, b * 512:(b + 1) * 512]) for kt in range(nk)]
            mm_acc(ps[:], pairs, R=[wslot] + rkeys, W=[ps])
            return ps

        def mod_issue(l, sl):
            m = mws[sl % 4]
            src = mod_w[l * 1024:(l + 1) * 1024, sl * 128:(sl + 1) * 128].rearrange("(kt p) c -> p kt c", p=128)
            P.dma("pool", W=[m], out=m[:], in_=src)

        def mod_mm(l, sl):
            m = mws[sl % 4]
            pairs = [(m[:, kt, :], cs[:, kt, :]) for kt in range(KT)]
            mm_acc(pmod[:, sl * 2:sl * 2 + 2], pairs, R=[m, cs], W=[pmod])

        def mod_finish(l):
            mb = pkv('modb', L, 48)
            mp3 = pmod[:, 0:96].rearrange("p (a b) -> p a b", a=48, b=2)
            for c in range(2):
                V("tensor_tensor", [pmod, pk], [modsb], out=modsb[:, :, c], in0=mp3[:, :, c], in1=mb[:, l, :], op=ALU.add)
            for which, (nm, sco) in enumerate((('n1w', 8), ('n2w', 32))):
                nw = pkv(nm, L, 8)
                V("tensor_scalar", [modsb], [modA], out=modA[:, which], in0=modsb[:, sco:sco + 8, :], scalar1=1.0, scalar2=None, op0=ALU.add)
                V("tensor_tensor", [modA, pk], [modA], out=modA[:, which], in0=modA[:, which], in1=nw[:, l, :].unsqueeze(2).to_broadcast([128, 8, 2]), op=ALU.mult)

        def modulation_all(l):
            for sl in range(48):
                mod_issue(l, sl)
                if sl >= 2:
                    mod_mm(l, sl - 2)
            mod_mm(l, 46)
            mod_mm(l, 47)

        sqb = T[7][:, 0:1024].bitcast(BF16).rearrange("p (a b) -> p a b", a=8, b=256)

        rs_alt = [(rs[:], rs), (small[:, 256:512], ('small', 'rs'))]

        def rstd_block(nb):
            b = nb // 2
            tsl = slice(nb * 256, (nb + 1) * 256)
            rs_ap, rs_k = rs_alt[nb % 2]
            ACT(sqb, x[:, :, tsl], AF.Square, [('x', b)], [T[7]])
            ss = PB()
            mm_acc(ss[:, 0:256], [(onesb[:], sqb[:, kt, :]) for kt in range(KT)], R=[T[7], onesb], W=[ss])
            ACT(rs_ap, ss[:, 0:256], AF.Ln, [ss, small], [rs_k], bias=small[:, 0:1], scale=1.0)
            ACT(rs_ap, rs_ap, AF.Exp, [rs_k], [rs_k], scale=-0.5)
            return rs_ap, rs_k

        def norm_mod(l, which):
            sho = 0 if which == 0 else 24
            for nb in range(6):
                c = 0 if nb < 4 else 1
                b = nb // 2
                tsl = slice(nb * 256, (nb + 1) * 256)
                rs_ap, rs_k = rstd_block(nb)
                for kt in range(KT):
                    tmp = T[5 + kt % 2][:, 0:256]
                    tk = T[5 + kt % 2]
                    V("tensor_tensor", [('x', b), rs_k], [tk], out=tmp, in0=x[:, kt, tsl], in1=rs_ap, op=ALU.mult)
                    V("tensor_scalar", [tk, modA, modsb], [('h', b)], out=h[:, kt, tsl], in0=tmp, scalar1=modA[:, which, kt, c:c + 1], scalar2=modsb[:, sho + kt, c:c + 1], op0=ALU.mult, op1=ALU.add)

        def pad_fix(npad_l, npad_r):
            if npad_l:
                ACT(cpad[:, 1:4, 2 - npad_l:2], cpad[:, 0:3, 258 - npad_l:258], AF.Identity, [cpad, pk], [cpad], scale=fcol)
            if npad_r:
                ACT(cpad[:, 0:3, 258:258 + npad_r], cpad[:, 1:4, 2:2 + npad_r], AF.Identity, [cpad, pk], [cpad], scale=fcol)

        def conv_from_cpad(out3, wcols, bcol, ktaps, okeys):
            o0 = 2 - ktaps // 2
            ACT(out3, cpad[:, :, o0:o0 + 256], AF.Identity, [cpad, pk], okeys, scale=wcols[0], bias=bcol)
            for j in range(1, ktaps):
                V("scalar_tensor_tensor", [cpad, pk] + okeys, okeys, out=out3, in0=cpad[:, :, o0 + j:o0 + j + 256], scalar=wcols[j], in1=out3, op0=ALU.mult, op1=ALU.add)

        def evac_to_cpad(ps, b):
            A("copy", [ps], [cpad], out=cpad[:, 2 * b:2 * b + 2, 2:258], in_=ps[:].rearrange("p (a b) -> p a b", a=2, b=256))

        def v3(t):
            return t[:].rearrange("p (a b) -> p a b", a=NSEG, b=256)

        def lru_mixer(l):
            wsl = load_w(w_in, l * 1024, 8, 2056, 512)
            t5b = T[5][:].bitcast(BF16)
            lw = t5b[:, 0:1024].rearrange("p (w d t o) -> p w d t o", w=2, d=2, t=2, o=128)
            xcb = t5b[:, 1024:1024 + NT]
            P.dma("pool", W=[T[5]], out=t5b[:, 0:1024], in_=lruw[l * 128:(l + 1) * 128, :])
            lcw = pkv('lcw', L, 2, 4); lcb = pkv('lcb', L, 2); llam = pkv('llam', L, 2, 2)
            lba = pkv('lba', L, 2, 2); lbx = pkv('lbx', L, 2, 2); linit = pkv('linit', L, 2, 2)
            cpv = small[:, 8:12]
            ACT(cpv, llam[:, l].rearrange("p a b -> p (a b)"), AF.Exp, [pk], [small], scale=-1.0)
            ACT(cpv, cpv, AF.Ln, [small], [small], bias=1.0, scale=1.0)
            V("tensor_scalar", [small], [small], out=cpv, in0=cpv, scalar1=-8.0, scalar2=None, op0=ALU.mult)
            for ti in range(2):
                for b in range(3):
                    ps = dense_fm(wsl, ti * 128, b)
                    evac_to_cpad(ps, b)
                pad_fix(2, 1)
                xc = T[0]
                conv_from_cpad(v3(xc), [lcw[:, l, ti, j:j + 1] for j in range(4)], lcb[:, l, ti:ti + 1], 4, [xc])
                gg = T[1]
                for b in range(3):
                    ps = dense_fm(wsl, 256 + ti * 128, b)
                    bs = slice(b * 512, (b + 1) * 512)
                    t2 = T[2][:, 0:512]
                    ACT(t2, ps[:], AF.Square, [ps], [T[2]])
                    V("tensor_scalar", [T[2]], [T[2]], out=t2, in0=t2, scalar1=0.044715, scalar2=1.0, op0=ALU.mult, op1=ALU.add)
                    V("tensor_tensor", [T[2], ps], [T[2]], out=t2, in0=t2, in1=ps[:], op=ALU.mult)
                    ACT(t2, t2, AF.Sigmoid, [T[2]], [T[2]], scale=1.5957691216057308)
                    V("tensor_tensor", [T[2], ps], [gg], out=gg[:, bs], in0=t2, in1=ps[:], op=ALU.mult)
                hacc = T[2]
                A("copy", [xc], [T[5]], out=xcb, in_=xc[:])
                for d in range(2):
                    av = T[3]; uv = T[4]
                    for b in range(3):
                        bs = slice(b * 512, (b + 1) * 512)
                        pa = PB()
                        MM(pa[:], lw[:, 0, d, ti, :], xcb[:, bs], [T[5]], [pa])
                        px = PB()
                        MM(px[:], lw[:, 1, d, ti, :], xcb[:, bs], [T[5]], [px])
                        ACT(av[:, bs], pa[:], AF.Sigmoid, [pa, pk], [av], bias=lba[:, l, d, ti:ti + 1], scale=1.0)
                        ACT(uv[:, bs], px[:], AF.Sigmoid, [px, pk], [uv], bias=lbx[:, l, d, ti:ti + 1], scale=1.0)
                    cpc = small[:, 8 + d * 2 + ti:8 + d * 2 + ti + 1]
                    ACT(av[:], av[:], AF.Exp, [av, small], [av], scale=cpc)
                    V("tensor_tensor", [uv, xc], [uv], out=uv[:], in0=uv[:], in1=xc[:], op=ALU.mult)
                    m2 = T[6]
                    ACT(m2[:], av[:], AF.Square, [av], [m2])
                    ACT(m2[:], m2[:], AF.Sqrt, [m2], [m2], scale=-1.0, bias=1.0)
                    V("tensor_tensor", [uv, m2], [uv], out=uv[:], in0=uv[:], in1=m2[:], op=ALU.mult)
                    hd = hacc if d == 0 else T[6]
                    icol = small[:, 16:17]
                    order = list(range(NSEG)) if d == 0 else [3, 2, 1, 0, 5, 4]
                    for sg in order:
                        first = (sg == 0 and d == 0) or (sg == 3 and d == 1)
                        if sg >= 4:
                            init = 0.0
                            rk = []
                        elif first:
                            init = linit[:, l, d, ti:ti + 1]
                            rk = [pk]
                        else:
                            prev = sg - 1 if d == 0 else sg + 1
                            pcol = hd[:, prev * 256 + 255:prev * 256 + 256] if d == 0 else hd[:, prev * 256:prev * 256 + 1]
                            V("tensor_scalar", [hd, pk], [small], out=icol, in0=pcol, scalar1=fcol, scalar2=None, op0=ALU.mult)
                            init = icol
                            rk = [small]
                        if d == 0:
                            sl_ = slice(sg * 256, (sg + 1) * 256)
                        else:
                            sl_ = slice(sg * 256 + 255, (sg * 256 - 1) if sg > 0 else None, -1)
                        V("tensor_tensor_scan", [av, uv] + rk, [hd], out=hd[:, sl_], data0=av[:, sl_], data1=uv[:, sl_], initial=init, op0=ALU.mult, op1=ALU.add)
                    fc = 255 if d == 0 else 0
                    A("copy", [hd], [fin_lru], out=fin_lru[:, ti, d, :], in_=v3(hd)[:, :, fc])
                V("tensor_tensor", [hacc, T[6]], [hacc], out=hacc[:], in0=hacc[:], in1=T[6][:], op=ALU.add)
                V("tensor_tensor", [hacc, gg], [('ymix', 6 + ti)], out=ymix[:, 6 + ti, :], in0=hacc[:], in1=gg[:], op=ALU.mult)
                for d in range(2):
                    dst = o_lru[:, l, d, ti * 128:(ti + 1) * 128].rearrange("s p -> p s")
                    P.dma("sp", R=[fin_lru], is_output=True, out=dst, in_=fin_lru[:, ti, d, :], allow_slow_non_contiguous=True)

        def out_proj(l):
            for half in range(2):
                wsl = load_w(w_out, l * 1024, 8, half * 512, 512)
                for q in range(4):
                    dt_ = half * 4 + q
                    for b in range(3):
                        c = 0 if b < 2 else 1
                        ps = dense_fm(wsl, q * 128, b, rhs=ymix, rkeys=ALLY)
                        bs = slice(b * 512, (b + 1) * 512)
                        V("scalar_tensor_tensor", [ps, modsb, ('x', b)], [('x', b)], out=x[:, dt_, bs], in0=ps[:], scalar=modsb[:, 16 + dt_, c:c + 1], in1=x[:, dt_, bs], op0=ALU.mult, op1=ALU.add)

        def ffn(l):
            fcw = pkv('fcw', L, NJ, 3); fcb = pkv('fcb', L, NJ)
            aff = ymix
            nxt = l + 1 if l + 1 < depth else None
            pending = None

            def u_part(wu_, co_, gc_, jj_):
                for b in range(3):
                    ps = dense_fm(wu_, co_, b)
                    bs = slice(b * 512, (b + 1) * 512)
                    V("tensor_tensor", [gc_, ps], [('ymix', jj_)], out=aff[:, jj_, bs], in0=gc_[:, bs], in1=ps[:], op=ALU.mult)

            for (j0, nj) in ((0, 6), (6, 6), (12, 5), (17, 5)):
                for jj in range(nj):
                    j = j0 + jj
                    if nxt is not None:
                        mod_issue(nxt, 2 * j)
                        mod_issue(nxt, 2 * j + 1)
                    if jj % 4 == 0:
                        ncl = min(4, nj - jj) * 128
                        wg = load_w(w_up, l * 1024, 8, j * 128, ncl)
                        wu = load_w(w_up, l * 1024, 8, DFF + j * 128, ncl)
                    co = (jj % 4) * 128
                    for b in range(3):
                        ps = dense_fm(wg, co, b)
                        evac_to_cpad(ps, b)
                    pad_fix(1, 1)
                    gc = T[jj % 2]
                    conv_from_cpad(v3(gc), [fcw[:, l, j, k:k + 1] for k in range(3)], fcb[:, l, j:j + 1], 3, [gc])
                    ACT(gc[:], gc[:], AF.Silu, [gc], [gc])
                    if pending is not None:
                        u_part(*pending)
                    pending = (wu, co, gc, jj)
                    if nxt is not None and j >= 1:
                        mod_mm(nxt, 2 * (j - 1))
                        mod_mm(nxt, 2 * (j - 1) + 1)
                u_part(*pending)
                pending = None
                for half in range(2):
                    wd = load_w(w_down, l * DFF + j0 * 128, nj, half * 512, 512)
                    for q in range(4):
                        dt_ = half * 4 + q
                        for b in range(3):
                            c = 0 if b < 2 else 1
                            ps = dense_fm(wd, q * 128, b, nk=nj, rhs=aff, rkeys=[('ymix', i) for i in range(nj)])
                            bs = slice(b * 512, (b + 1) * 512)
                            V("scalar_tensor_tensor", [ps, modsb, ('x', b)], [('x', b)], out=x[:, dt_, bs], in0=ps[:], scalar=modsb[:, 40 + dt_, c:c + 1], in1=x[:, dt_, bs], op0=ALU.mult, op1=ALU.add)

        def mod_tail(l):
            mod_mm(l, 42)
            mod_mm(l, 43)
            for sl in range(44, 48):
                mod_issue(l, sl)
            for sl in range(44, 48):
                mod_mm(l, sl)

        def final_out():
            fnw = pkv('fnw')
            for nb in range(6):
                b = nb // 2
                tsl = slice(nb * 256, (nb + 1) * 256)
                rs_ap, rs_k = rstd_block(nb)
                for kt in range(KT):
                    V("scalar_tensor_tensor", [('x', b), rs_k, pk], [T[kt // 4]], out=T[kt // 4][:, (kt % 4) * 256:(kt % 4) * 256 + 256], in0=x[:, kt, tsl], scalar=fnw[:, kt:kt + 1], in1=rs_ap, op0=ALU.mult, op1=ALU.mult)
                for cc in range(2):
                    ot = T[2 + cc]
                    for half in range(2):
                        ps = PB()
                        for q in range(4):
                            kt = half * 4 + q
                            src = T[kt // 4][:, (kt % 4) * 256 + cc * 128:(kt % 4) * 256 + cc * 128 + 128]
                            P.op("pe", "transpose", [T[kt // 4], cst], [ps], out=ps[:, q * 128:(q + 1) * 128], in_=src, identity=csv('ident'))
                        A("copy", [ps], [ot], out=ot[:, half * 512:(half + 1) * 512], in_=ps[:])
                    r0 = nb * 256 + cc * 128
                    P.dma("sp", R=[ot], is_output=True, out=y_out[r0:r0 + 128, :], in_=ot[:, 0:1024])

        stt = sb("stt", [128, 8, 96])
        Sm = sb("Sm", [128, 2, 2, 64])
        Sinit = sb("Sinit", [128, 2, 2, 64])
        kwb = sb("kwb", [128, 2, 4, 64], BF16)
        cdH = sb("cdH", [128, 2, NCH, 2])
        blk64b = sb("blk64b", [128, 128], BF16)
        V("tensor_copy", [cst], [blk64b], out=blk64b[:], in_=csv('blk64'))

        def st4(i):
            return stt[:, i, :].rearrange("p (d c h) -> p d c h", d=2, c=NCH, h=4)

        def bfv(t, *dims):
            v = t[:].bitcast(BF16)
            if len(dims) == 2:
                return v.rearrange("p (a b) -> p a b", a=dims[0], b=dims[1])
            if len(dims) == 3:
                return v.rearrange("p (a b c) -> p a b c", a=dims[0], b=dims[1], c=dims[2])
            if len(dims) == 4:
                return v.rearrange("p (a b c d) -> p a b c d", a=dims[0], b=dims[1], c=dims[2], d=dims[3])
            return v

        def attn_mixer(l, kind):
            ssd = (kind == 'ssd')
            v_tok = bfv(T[0], NCH, 4, 64)
            k_tok = bfv(T[1], NCH, 4, 64) if not ssd else bfv(T[1], NCH, 4, 64)[:, :, 0:2, :]
            Sent = bfv(T[2], NCH, 2, 2, 64)
            rhsla = T[3][:, 0:512].rearrange("p (h l) -> p h l", h=4, l=128)
            tmpD = T[3][:, 512:1024].rearrange("p (h l) -> p h l", h=4, l=128)
            eCR = T[3][:, 1024:1536].rearrange("p (h l) -> p h l", h=4, l=128)
            t4b = T[4][:].bitcast(BF16)
            Dm = t4b[:, 0:1024].rearrange("p (d h l) -> p d h l", d=2, h=4, l=128)
            Dsum = t4b[:, 1024:1536].rearrange("p (h l) -> p h l", h=4, l=128)
            qdz = t4b[:, 1536:2560].rearrange("p (d h l) -> p d h l", d=2, h=4, l=128)
            Pm = t4b[:, 2560:3072].rearrange("p (h l) -> p h l", h=4, l=128)
            la = st4(0); lndt = st4(1); Cp = st4(2); eTot = st4(3); tailw = st4(4); dtv = st4(5)
            finS = cpad[:, :, 2:258].rearrange("p s (d t q) -> p s d t q", d=2, t=2, q=64)
            o_st = o_ssd if ssd else o_ret
            init_d = ssdinit if ssd else retinit
            P.dma("sp", W=[Sinit], out=Sinit[:].rearrange("p d t q -> p (d t q)"), in_=init_d[l * 128:(l + 1) * 128, :])

            def hmap(hh_):
                return _head_map(kind, hh_)

            if ssd:
                sz = bfv(T[7], 2, NT)
                xs = bfv(T[6], 2, NT)
                BC = bfv(T[5], 2, NT)
                kf = BC[:, 0:1, :]
                qf = BC[:, 1:2, :]
                scw = pkv('scw', L, 4, 4); scb = pkv('scb', L, 4)
                w1 = load_w(w_in, l * 1024, 8, 0, 512)
                w2 = load_w(w_in, l * 1024, 8, 512, 264)
                for ti in range(2):
                    for b in range(3):
                        ps = dense_fm(w1, ti * 128, b)
                        ACT(sz[:, ti, b * 512:(b + 1) * 512], ps[:], AF.Silu, [ps], [T[7]])
                for ci in range(4):
                    wsl, co = (w1, 256 + ci * 128) if ci < 2 else (w2, (ci - 2) * 128)
                    for b in range(3):
                        ps = dense_fm(wsl, co, b)
                        evac_to_cpad(ps, b)
                    pad_fix(2, 1)
                    conv_from_cpad(v3(T[3]), [scw[:, l, ci, j:j + 1] for j in range(4)], scb[:, l, ci:ci + 1], 4, [T[3]])
                    dst = xs[:, ci, :] if ci < 2 else BC[:, ci - 2, :]
                    ACT(dst, T[3][:], AF.Silu, [T[3]], [T[6] if ci < 2 else T[5]])
                pdt = PB()
                for c in range(NCH):
                    mm_acc(pdt[:, c * 8:(c + 1) * 8], [(h[:, kt, c * 128:(c + 1) * 128], w2[:, kt, 256:264]) for kt in range(KT)], R=[w2, ('h', c // 4)], W=[pdt])
                pdt4 = pdt[:, 0:96].rearrange("p (c d h) -> p d c h", c=NCH, d=2, h=4)
                sdtb = pkv('sdtb', L, 2, 4); salog = pkv('salog', L, 2, 4)
                V("tensor_tensor", [pdt, pk], [stt], out=dtv, in0=pdt4, in1=sdtb[:, l].unsqueeze(2).to_broadcast([128, 2, NCH, 4]), op=ALU.add)
                ACT(stt[:, 5, :], stt[:, 5, :], AF.Exp, [stt], [stt])
                ACT(stt[:, 5, :], stt[:, 5, :], AF.Ln, [stt], [stt], bias=1.0, scale=1.0)
                ACT(stt[:, 1, :], stt[:, 5, :], AF.Ln, [stt], [stt])
                an = small[:, 24:32]
                ACT(an, salog[:, l].rearrange("p a b -> p (a b)"), AF.Exp, [pk], [small])
                V("tensor_scalar", [small], [small], out=an, in0=an, scalar1=-1.0, scalar2=None, op0=ALU.mult)
                V("tensor_tensor", [stt, small], [stt], out=la, in0=dtv, in1=an.rearrange("p (d h) -> p d h", d=2, h=4).unsqueeze(2).to_broadcast([128, 2, NCH, 4]), op=ALU.mult)
            else:
                sg = bfv(T[7], 2, NT)
                qf = bfv(T[5], 2, NT)
                kf = bfv(T[6], 2, NT)
                wA = load_w(w_in, l * 1024, 8, 776, 512)
                wB = load_w(w_in, l * 1024, 8, 776 + 512, 512)
                cpf = cpad[:].rearrange("p s c -> p (s c)")
                cosb = cpf[:, 0:512].bitcast(BF16)
                sinb = cpf[:, 512:1024].bitcast(BF16)
                P.dma("pool", W=[cpad], out=cosb, in_=cos_d)
                P.dma("pool", W=[cpad], out=sinb, in_=sin_d)
                rq = T[3][:, 0:256].bitcast(BF16)
                t1 = T[3][:, 512:1024]
                t2 = T[3][:, 1024:1536]
                for qi in range(4):
                    dstt, dkey = (qf, T[5]) if qi < 2 else (kf, T[6])
                    sc = 1.0 if qi < 2 else 0.125
                    for b in range(3):
                        ps = dense_fm(wA, qi * 128, b)
                        bs = slice(b * 512, (b + 1) * 512)
                        if b == 2:
                            ACT(dstt[:, qi % 2, bs], ps[:], AF.Identity, [ps], [dkey], scale=sc)
                        else:
                            A("copy", [ps], [T[3]], out=rq, in_=ps[:])
                            pp = PB()
                            MM(pp[:], rpermb[:], rq, [rpermb, T[3]], [pp])
                            V("tensor_tensor", [ps, cpad, T[3]], [T[3]], out=t1, in0=ps[:], in1=cosb[:, bs], op=ALU.mult)
                            V("tensor_tensor", [pp, cpad, T[3]], [T[3]], out=t2, in0=pp[:], in1=sinb[:, bs], op=ALU.mult)
                            V("tensor_tensor", [T[3]], [T[3]], out=t1, in0=t1, in1=t2, op=ALU.add)
                            ACT(dstt[:, qi % 2, bs], t1, AF.Identity, [T[3]], [dkey], scale=sc)
                V("memset", [cpad], [cpad], ap=cpad[:], constant=0.0)
                for c in range(NCH):
                    pv = PB()
                    mm_acc(pv[:, 0:256], [(h[:, kt, c * 128:(c + 1) * 128], wB[:, kt, 0:256]) for kt in range(KT)], R=[wB, ('h', c // 4)], W=[pv])
                    A("copy", [pv], [T[0]], out=v_tok[:, c].rearrange("p a b -> p (a b)"), in_=pv[:, 0:256])
                for ti in range(2):
                    for b in range(3):
                        ps = dense_fm(wB, 256 + ti * 128, b)
                        ACT(sg[:, ti, b * 512:(b + 1) * 512], ps[:], AF.Silu, [ps], [T[7]])
                V("tensor_copy", [cst], [stt], out=stt[:, 0, :], in_=csv('retla'))
                V("memset", [stt], [stt], ap=stt[:, 1, :], constant=0.0)

            for c in range(NCH):
                cs_ = slice(c * 128, (c + 1) * 128)
                if ssd:
                    for ti in range(2):
                        P.op("pe", "transpose", [T[6], identb], [pbb], out=pbb[:, ti * 128:(ti + 1) * 128], in_=xs[:, ti, cs_], identity=identb[:])
                    P.op("pe", "transpose", [T[5], identb], [pbb], out=pbb[:, 256:384], in_=kf[:, 0, cs_], identity=identb[:])
                    A("copy", [pbb], [T[0]], out=v_tok[:, c].rearrange("p a b -> p (a b)"), in_=pbb[:, 0:256])
                    A("copy", [pbb], [T[1]], out=k_tok[:, c].rearrange("p a b -> p (a b)"), in_=pbb[:, 256:384])
                else:
                    for ti in range(2):
                        P.op("pe", "transpose", [T[6], identb], [pbb], out=pbb[:, ti * 128:(ti + 1) * 128], in_=kf[:, ti, cs_], identity=identb[:])
                    A("copy", [pbb], [T[1]], out=k_tok[:, c].rearrange("p a b -> p (a b)"), in_=pbb[:, 0:256])

            pc1 = PB()
            MM(pc1[:, 0:48], csv('maskU'), stt[:, 0, 0:48], [cst, stt], [pc1])
            MM(pc1[:, 48:96], csv('maskL'), stt[:, 0, 48:96], [cst, stt], [pc1])
            V("tensor_tensor", [pc1, stt], [stt], out=stt[:, 2, :], in0=pc1[:, 0:96], in1=stt[:, 1, :], op=ALU.subtract)
            pc2 = PB()
            MM(pc2[:, 0:96], csv('ones'), stt[:, 0, :], [cst, stt], [pc2])
            ACT(stt[:, 3, :], pc2[:, 0:96], AF.Exp, [pc2], [stt])
            V("tensor_tensor", [pc2, stt], [stt], out=stt[:, 4, :], in0=pc2[:, 0:96], in1=stt[:, 2, :], op=ALU.subtract)
            ACT(stt[:, 4, :], stt[:, 4, :], AF.Exp, [stt], [stt])
            for base in (0, 64):
                for slot in range(2):
                    hh_ = (base // 64) * 2 + slot if ssd else slot * 2 + base // 64
                    V("tensor_copy", [stt], [cdH], out=cdH[base:base + 64, :, :, slot], in_=eTot[base:base + 64, :, :, hh_])

            csS = [T[3][:].rearrange("p (c t q) -> p c t q", c=NCH, t=2, q=64), T[4][:].rearrange("p (c t q) -> p c t q", c=NCH, t=2, q=64)]
            csK = [T[3], T[4]]
            for c in range(NCH):
                for d in range(2):
                    if ssd:
                        V("tensor_tensor", [T[1], stt], [('kwb', d)], out=kwb[:, d].rearrange("p (g e) n -> p g e n", g=2, e=2),
                          in0=k_tok[:, c].unsqueeze(2).to_broadcast([128, 2, 2, 64]),
                          in1=tailw[:, d, c, :].rearrange("p (g e) -> p g e", g=2, e=2).unsqueeze(3).to_broadcast([128, 2, 2, 64]), op=ALU.mult)
                    else:
                        V("tensor_tensor", [T[1], stt], [('kwb', d)], out=kwb[:, d], in0=k_tok[:, c], in1=tailw[:, d, c, :].unsqueeze(2).to_broadcast([128, 4, 64]), op=ALU.mult)
                    pcs = PB()
                    for hh_ in range(4):
                        base, slot = hmap(hh_)
                        MM(pcs[base:base + 64, slot * 64:(slot + 1) * 64], kwb[:, d, hh_, :], v_tok[:, c, hh_, :], [('kwb', d), T[0]], [pcs])
                    A("copy", [pcs], [csK[d]], out=csS[d][:, c], in_=pcs[:, 0:128].rearrange("p (t q) -> p t q", t=2, q=64))
            V("tensor_copy", [Sinit], [('Sm', 0)], out=Sm[:, 0], in_=Sinit[:, 0])
            V("memset", [], [('Sm', 1)], ap=Sm[:, 1], constant=0.0)
            for i in range(NCH):
                for d in range(2):
                    c = i if d == 0 else NCH - 1 - i
                    sk = ('Sm', d)
                    V("tensor_copy", [sk], [T[2]], out=Sent[:, c, d], in_=Sm[:, d])
                    for slot in range(2):
                        V("scalar_tensor_tensor", [sk, cdH, csK[d]], [sk], out=Sm[:, d, slot], in0=Sm[:, d, slot], scalar=cdH[:, d, c, slot:slot + 1], in1=csS[d][:, c, slot], op0=ALU.mult, op1=ALU.add)
                    seg_end = (c % 2 == 1) if d == 0 else (c % 2 == 0)
                    if seg_end:
                        sg_ = c // 2
                        V("tensor_copy", [sk], [cpad], out=finS[:, sg_, d], in_=Sm[:, d])
                        if d == 0:
                            if sg_ < 3:
                                V("tensor_scalar", [sk, pk], [sk], out=Sm[:, d], in0=Sm[:, d], scalar1=fcol, scalar2=None, op0=ALU.mult)
                            elif sg_ < 5:
                                V("memset", [sk], [sk], ap=Sm[:, d], constant=0.0)
                        else:
                            if sg_ == 5:
                                V("memset", [sk], [sk], ap=Sm[:, d], constant=0.0)
                            elif sg_ == 4:
                                V("tensor_copy", [Sinit, sk], [sk], out=Sm[:, 1], in_=Sinit[:, 1])
                            elif sg_ > 0:
                                V("tensor_scalar", [sk, pk], [sk], out=Sm[:, d], in0=Sm[:, d], scalar1=fcol, scalar2=None, op0=ALU.mult)
            V("memset", [], [T[4]], ap=T[4][:], constant=0.0)
            for hh_ in range(4):
                base, slot = hmap(hh_)
                for d in range(2):
                    dst = o_st[:, l, d, hh_].rearrange("s n q -> n s q")
                    P.dma("sp", R=[cpad], is_output=True, out=dst, in_=finS[base:base + 64, :, d, slot, :])

            ybase = 0 if ssd else 2
            V("memset", [], [mws[0]], ap=mws[0][:], constant=0.0)
            qdz_b = [qdz, mws[0][:].rearrange("p a b -> p (a b)").rearrange("p (d h l) -> p d h l", d=2, h=4, l=128)]
            qdz_k = [('T4', 'q'), mws[0]]
            Dsum_b = [Dsum, mws[1][:, 0:4, :]]
            Dsum_k = [('T4', 'Ds'), mws[1]]

            rl = [T[3][:, 0:512], T[3][:, 512:1024]]
            rlk = [('T3', 0), ('T3', 1)]
            ec = [T[3][:, 1024:1536], mws[2][:].rearrange("p a b -> p (a b)").bitcast(F32)]
            eck_ = [('T3', 2), mws[2]]
            psc_b = [pb[0], pb[1]]
            pcb = [pb[2], pb[3]]
            pyb = pb[4]

            def h3(ap):
                return ap.rearrange("p (h l) -> p h l", h=4, l=128)

            def stA(c):
                cs_ = slice(c * 128, (c + 1) * 128)
                psc = psc_b[c % 2]
                if ssd:
                    for g in range(2):
                        MM(psc[:, g * 128:(g + 1) * 128], kf[g * 64:(g + 1) * 64, 0, cs_], qf[g * 64:(g + 1) * 64, 0, cs_], [T[5]], [psc])
                else:
                    for hh_ in range(4):
                        base, slot = hmap(hh_)
                        MM(psc[:, hh_ * 128:(hh_ + 1) * 128], kf[base:base + 64, slot, cs_], qf[base:base + 64, slot, cs_], [T[5], T[6]], [psc])
                for d in range(2):
                    msk = csv('maskU') if d == 0 else csv('maskL')
                    V("tensor_tensor", [cst, stt], [rlk[d]], out=h3(rl[d]), in0=msk.unsqueeze(1).to_broadcast([128, 4, 128]), in1=la[:, d, c, :].unsqueeze(2).to_broadcast([128, 4, 128]), op=ALU.mult)
                for d in range(2):
                    MM(pcb[d][:], csv('ones'), rl[d], [cst, rlk[d]], [pcb[d]])

            def stC(c):
                cs_ = slice(c * 128, (c + 1) * 128)
                bi = c % 2
                for d in range(2):
                    neg = csv('negF') if d == 0 else csv('negB')
                    pc3 = h3(pcb[d][:])
                    V("tensor_tensor", [pcb[d], stt], [rlk[d]], out=h3(rl[d]), in0=pc3, in1=Cp[:, d, c, :].unsqueeze(2).to_broadcast([128, 4, 128]), op=ALU.subtract)
                    V("tensor_tensor", [rlk[d], cst], [rlk[d]], out=h3(rl[d]), in0=h3(rl[d]), in1=neg.unsqueeze(1).to_broadcast([128, 4, 128]), op=ALU.add)
                    ACT(Dm[:, d], h3(rl[d]), AF.Exp, [rlk[d]], [('T4', 'D', d)])
                    ACT(h3(ec[d]), pc3, AF.Exp, [pcb[d]], [eck_[d]])
                for d in range(2):
                    eCR_ = h3(ec[d])
                    for base in (0, 64):
                        if ssd:
                            h0 = (base // 64) * 2
                            V("tensor_tensor", [eck_[d], T[5]], [qdz_k[bi]], out=qdz_b[bi][base:base + 64, d, h0:h0 + 2, :],
                              in0=qf[base:base + 64, 0, cs_].unsqueeze(1).to_broadcast([64, 2, 128]), in1=eCR_[base:base + 64, h0:h0 + 2, :], op=ALU.mult)
                        else:
                            o_ = base // 64
                            V("tensor_tensor", [eck_[d], T[5]], [qdz_k[bi]], out=qdz_b[bi][base:base + 64, d, o_::2, :],
                              in0=qf[base:base + 64, :, cs_], in1=eCR_[base:base + 64, o_::2, :], op=ALU.mult)
                V("tensor_tensor", [('T4', 'D', 0), ('T4', 'D', 1)], [Dsum_k[bi]], out=Dsum_b[bi], in0=Dm[:, 0], in1=Dm[:, 1], op=ALU.add)

            def stB(c):
                bi = c % 2
                psc = psc_b[bi]
                if ssd:
                    V("tensor_tensor", [psc, Dsum_k[bi]], [('T4', 'P')], out=Pm.rearrange("p (g e) l -> p g e l", g=2, e=2),
                      in0=psc[:, 0:256].rearrange("p (g l) -> p g l", g=2, l=128).unsqueeze(2).to_broadcast([128, 2, 2, 128]),
                      in1=Dsum_b[bi].rearrange("p (g e) l -> p g e l", g=2, e=2), op=ALU.mult)
                else:
                    V("tensor_tensor", [psc, Dsum_k[bi]], [('T4', 'P')], out=Pm, in0=h3(psc[:]), in1=Dsum_b[bi], op=ALU.mult)
                py = pyb
                for hh_ in range(4):
                    base, slot = hmap(hh_)
                    oap = py[(hh_ % 2) * 64:(hh_ % 2) * 64 + 64, (hh_ // 2) * 128:(hh_ // 2) * 128 + 128]
                    MM(oap, v_tok[:, c, hh_, :], Pm[:, hh_, :], [T[0], ('T4', 'P')], [py], start=True, stop=False)
                    MM(oap, Sent[:, c, 0, slot, :], qdz_b[bi][:, 0, hh_, :], [T[2], qdz_k[bi]], [], start=False, stop=False, noself=True)
                    MM(oap, Sent[:, c, 1, slot, :], qdz_b[bi][:, 1, hh_, :], [T[2], qdz_k[bi]], [py], start=False, stop=True, noself=True)

            def stD(c):
                cs_ = slice(c * 128, (c + 1) * 128)
                py = pyb
                if ssd:
                    sdcol = pkv('sdcol', L, 2)
                    for ti in range(2):
                        V("scalar_tensor_tensor", [T[6], pk, py], [('ymix', ti)], out=ymix[:, ti, cs_], in0=xs[:, ti, cs_], scalar=sdcol[:, l, ti:ti + 1], in1=py[:, ti * 128:(ti + 1) * 128], op0=ALU.mult, op1=ALU.add)
                else:
                    A("copy", [py], [('ymix', 2), ('ymix', 3)], out=ymix[:, 2:4, cs_], in_=py[:, 0:256].rearrange("p (t l) -> p t l", t=2, l=128))

            for c in range(NCH + 1):
                if c < NCH:
                    stA(c)
                if c >= 1:
                    stB(c - 1)
                if c < NCH:
                    stC(c)
                if c >= 1:
                    stD(c - 1)

            if ssd:
                snw = pkv('snw', L, 2)
                for ti in range(2):
                    V("tensor_tensor", [('ymix', ti), T[7]], [('ymix', ti)], out=ymix[:, ti, :], in0=ymix[:, ti, :], in1=sz[:, ti, :], op=ALU.mult)
                sq2 = T[3][:, 0:512].bitcast(BF16).rearrange("p (t l) -> p t l", t=2, l=512)
                rs2 = T[3][:, 512:1024]
                for b in range(3):
                    bs = slice(b * 512, (b + 1) * 512)
                    ACT(sq2, ymix[:, 0:2, bs], AF.Square, [('ymix', 0), ('ymix', 1)], [T[3]])
                    ss = PB()
                    mm_acc(ss[:], [(onesb[:], sq2[:, ti, :]) for ti in range(2)], R=[T[3], onesb], W=[ss])
                    ACT(rs2, ss[:], AF.Ln, [ss, small], [T[3]], bias=small[:, 0:1], scale=4.0)
                    ACT(rs2, rs2, AF.Exp, [T[3]], [T[3]], scale=-0.5)
                    for ti in range(2):
                        V("scalar_tensor_tensor", [('ymix', ti), pk, T[3]], [('ymix', ti)], out=ymix[:, ti, bs], in0=ymix[:, ti, bs], scalar=snw[:, l, ti:ti + 1], in1=rs2, op0=ALU.mult, op1=ALU.mult)
            else:
                rgn = pkv('rgn', L, 2)
                yc = T[3][:, 0:512]
                sqr = T[3][:, 512:768].bitcast(BF16)
                rs2 = T[3][:, 1024:1536]
                for ti in range(2):
                    for b in range(3):
                        bs = slice(b * 512, (b + 1) * 512)
                        pm = PB()
                        MM(pm[:], blk64b[:], ymix[:, 2 + ti, bs], [blk64b, ('ymix', 2 + ti)], [pm])
                        V("tensor_tensor", [('ymix', 2 + ti), pm], [T[3]], out=yc, in0=ymix[:, 2 + ti, bs], in1=pm[:], op=ALU.subtract)
                        ACT(sqr, yc, AF.Square, [T[3]], [T[3]])
                        pv2 = PB()
                        MM(pv2[:], blk64b[:], sqr, [blk64b, T[3]], [pv2])
                        ACT(rs2, pv2[:], AF.Ln, [pv2, small], [T[3]], bias=small[:, 0:1], scale=1.0)
                        ACT(rs2, rs2, AF.Exp, [T[3]], [T[3]], scale=-0.5)
                        V("tensor_tensor", [T[3]], [T[3]], out=yc, in0=yc, in1=rs2, op=ALU.mult)
                        V("scalar_tensor_tensor", [T[3], pk, T[7]], [('ymix', 2 + ti)], out=ymix[:, 2 + ti, bs], in0=yc, scalar=rgn[:, l, ti:ti + 1], in1=sg[:, ti, bs], op0=ALU.mult, op1=ALU.mult)

        fin_s5 = sb("fin_s5", [128, NSEG, 2, 8, 2])
        s5p = sb("s5p", [128, 2, 8, 12])

        def s5_mixer(l):
            TWO_PI = 2.0 * math.pi
            ub = bfv(T[0], 2, NT)
            wre = T[1]; wim = T[2]
            E1r = T[3][:, 0:256]; E1i = T[3][:, 256:512]; E2r = T[3][:, 512:768]; E2i = T[3][:, 768:1024]
            ang = T[3][:, 1024:1280]; kbuf = T[3][:, 1280:1536].bitcast(I32)
            t1 = T[4][:, 0:512]; t2 = T[4][:, 512:1024]
            xb = bfv(T[5], 2, NT)
            acc = [T[6], T[7]]
            lre = pkv('lre', L, 2, 8); lim = pkv('lim', L, 2, 8); lstep = pkv('lstep', L, 2, 8)
            s5d = pkv('s5d', L, 2); s5init = pkv('s5init', L, 2, 8, 2)
            wu_ = load_w(w_in, l * 1024, 8, 1800, 256)
            wbc = ws[wsi[0] % 3]
            wsi[0] += 1
            BT = wbc[:, 0:4, :].rearrange("p a b -> p (a b)").rearrange("p (r j o) -> p r j o", r=2, j=8, o=128)
            CT = wbc[:, 4:8, :].rearrange("p a b -> p (a b)").rearrange("p (r j o) -> p r j o", r=2, j=8, o=128)
            P.dma("pool", W=[wbc], out=wbc[:, 0:4, :].rearrange("p a b -> p (a b)"), in_=s5bt[l * 128:(l + 1) * 128, :])
            P.dma("pool", W=[wbc], out=wbc[:, 4:8, :].rearrange("p a b -> p (a b)"), in_=s5ct[l * 128:(l + 1) * 128, :])
            V("tensor_scalar", [wbc], [wbc], out=CT[:, 1], in0=CT[:, 1], scalar1=-1.0, scalar2=None, op0=ALU.mult)
            for ti in range(2):
                for b in range(3):
                    ps = dense_fm(wu_, ti * 128, b)
                    A("copy", [ps], [T[0]], out=ub[:, ti, b * 512:(b + 1) * 512], in_=ps[:])
            def sp(k):
                return s5p[:, :, :, k]
            ACT(sp(0), lstep[:, l], AF.Exp, [pk], [s5p])
            V("tensor_tensor", [s5p, pk], [s5p], out=sp(1), in0=lre[:, l], in1=sp(0), op=ALU.mult)
            ACT(sp(1), sp(1), AF.Exp, [s5p], [s5p])
            V("tensor_tensor", [s5p, pk], [s5p], out=sp(2), in0=lim[:, l], in1=sp(0), op=ALU.mult)
            def sincos(dst_sin, dst_cos, src, scr_f, scr_i, keyR, keyW):
                for (dst, off) in ((dst_sin, 0.0), (dst_cos, math.pi / 2.0)):
                    V("tensor_scalar", keyR, keyW, out=scr_i, in0=src, scalar1=off, scalar2=1.0 / TWO_PI, op0=ALU.add, op1=ALU.mult)
                    V("tensor_copy", keyW, keyW, out=scr_f, in_=scr_i)
                    V("tensor_scalar", keyW, keyW, out=scr_f, in0=scr_f, scalar1=-TWO_PI, scalar2=off, op0=ALU.mult, op1=ALU.add)
                    V("tensor_tensor", keyR + keyW, keyW, out=scr_f, in0=scr_f, in1=src, op=ALU.add)
                    V("tensor_scalar", keyW, keyW, out=scr_f, in0=scr_f, scalar1=3.141592, scalar2=-3.141592, op0=ALU.min, op1=ALU.max)
                    ACT(dst, scr_f, AF.Sin, keyW, keyW)
            pscr_i = small[:, 64:80].bitcast(I32).rearrange("p (d j) -> p d j", d=2, j=8)
            pscr_f = small[:, 80:96].rearrange("p (d j) -> p d j", d=2, j=8)
            sincos(sp(4), sp(3), sp(2), pscr_f, pscr_i, [s5p, small], [s5p, small])
            V("tensor_tensor", [s5p], [s5p], out=sp(3), in0=sp(3), in1=sp(1), op=ALU.mult)
            V("tensor_tensor", [s5p], [s5p], out=sp(4), in0=sp(4), in1=sp(1), op=ALU.mult)
            V("tensor_tensor", [s5p, pk], [s5p], out=sp(8), in0=lre[:, l], in1=lre[:, l], op=ALU.mult)
            V("tensor_tensor", [s5p, pk], [s5p], out=sp(9), in0=lim[:, l], in1=lim[:, l], op=ALU.mult)
            V("tensor_tensor", [s5p], [s5p], out=sp(8), in0=sp(8), in1=sp(9), op=ALU.add)
            V("reciprocal", [s5p], [s5p], out=sp(8), in_=sp(8))
            V("tensor_scalar", [s5p], [s5p], out=sp(9), in0=sp(3), scalar1=-1.0, scalar2=None, op0=ALU.add)
            V("tensor_tensor", [s5p, pk], [s5p], out=sp(10), in0=sp(9), in1=lre[:, l], op=ALU.mult)
            V("tensor_tensor", [s5p, pk], [s5p], out=sp(11), in0=sp(4), in1=lim[:, l], op=ALU.mult)
            V("tensor_tensor", [s5p], [s5p], out=sp(5), in0=sp(10), in1=sp(11), op=ALU.add)
            V("tensor_tensor", [s5p], [s5p], out=sp(5), in0=sp(5), in1=sp(8), op=ALU.mult)
            V("tensor_tensor", [s5p, pk], [s5p], out=sp(10), in0=sp(4), in1=lre[:, l], op=ALU.mult)
            V("tensor_tensor", [s5p, pk], [s5p], out=sp(11), in0=sp(9), in1=lim[:, l], op=ALU.mult)
            V("tensor_tensor", [s5p], [s5p], out=sp(6), in0=sp(10), in1=sp(11), op=ALU.subtract)
            V("tensor_tensor", [s5p], [s5p], out=sp(6), in0=sp(6), in1=sp(8), op=ALU.mult)
            V("tensor_scalar", [s5p], [s5p], out=sp(7), in0=sp(5), scalar1=-1.0, scalar2=None, op0=ALU.mult)
            wsets = [(T[1], T[2]), (T[6], T[7])]
            E1r = T[3][:, 0:256]; E1i = T[3][:, 256:512]
            E2sets = [(T[3][:, 512:768], T[3][:, 768:1024], ('T3', 1), T[3][:, 512:1024]),
                      (T[3][:, 1024:1280], T[3][:, 1280:1536], ('T3', 2), T[3][:, 1024:1536])]
            rt1 = mws[0][:].rearrange("p a b -> p (a b)").bitcast(F32)
            rt2 = mws[1][:].rearrange("p a b -> p (a b)").bitcast(F32)
            kb_i = mws[2][:].rearrange("p a b -> p (a b)").bitcast(I32)
            kf_ = mws[3][:].rearrange("p a b -> p (a b)").bitcast(F32)
            angs2 = stt[:, 0:6, :].rearrange("p a b -> p (a b)")[:, 0:512]
            pacc = [pb[3], pb[4], pb[5]]
            pb_lim[0] = 3
            items = []
            for ot in range(2):
                for d in range(2):
                    for j in range(4 * ot, 4 * ot + 4):
                        items.append((ot, d, j, len(items)))

            def ctx(it):
                ot, d, j, idx = it
                wre, wim = wsets[idx % 2]
                E2r, E2i, e2k, E2both = E2sets[idx % 2]
                ec0 = 40 + 4 * (idx % 2)
                return dict(ot=ot, d=d, j=j, idx=idx, wre=wre, wim=wim, E2r=E2r, E2i=E2i, e2k=e2k, E2both=E2both,
                            e2c=small[:, ec0:ec0 + 4], eck=('small', 'e2c', idx % 2), ecol=255 if d == 0 else 0,
                            ramp=csv('rampf') if d == 0 else csv('rampb'))

            def pcol_(c, k):
                return s5p[:, c['d'], c['j'], k:k + 1]

            def tables(c):
                ramp, E2r, E2i, e2k, e2c, eck, ecol = c['ramp'], c['E2r'], c['E2i'], c['e2k'], c['e2c'], c['eck'], c['ecol']
                ACT(angs2[:, 256:512], ramp, AF.Identity, [cst, s5p], [stt], scale=pcol_(c, 2))
                ACT(angs2[:, 0:256], ramp, AF.Identity, [cst, s5p], [stt], scale=pcol_(c, 2), bias=math.pi / 2.0)
                V("tensor_scalar", [stt], [mws[2]], out=kb_i, in0=angs2, scalar1=1.0 / TWO_PI, scalar2=None, op0=ALU.mult)
                V("tensor_copy", [mws[2]], [mws[3]], out=kf_, in_=kb_i)
                V("scalar_tensor_tensor", [mws[3], stt], [mws[3]], out=kf_, in0=kf_, scalar=-TWO_PI, in1=angs2, op0=ALU.mult, op1=ALU.add)
                V("tensor_scalar", [mws[3]], [mws[3]], out=kf_, in0=kf_, scalar1=3.141592, scalar2=-3.141592, op0=ALU.min, op1=ALU.max)
                ACT(c['E2both'], kf_, AF.Sin, [mws[3]], [e2k])
                ACT(e2c[:, 0:1], E2r[:, ecol:ecol + 1], AF.Identity, [e2k, pk], [eck], scale=fcol)
                ACT(e2c[:, 1:2], E2i[:, ecol:ecol + 1], AF.Identity, [e2k, pk], [eck], scale=fcol)
                ACT(e2c[:, 2:3], e2c[:, 1:2], AF.Identity, [eck], [eck], scale=-1.0)
                ACT(e2c[:, 3:4], E2i[:, ecol:ecol + 1], AF.Identity, [e2k], [eck], scale=-1.0)
                ACT(E1r, E2r, AF.Identity, [e2k, s5p], [('T3', 0)], scale=pcol_(c, 5))
                ACT(E1i, E2r, AF.Identity, [e2k, s5p], [('T3', 0)], scale=pcol_(c, 6))

            def tables_b(c):
                E2i, e2k = c['E2i'], c['e2k']
                V("scalar_tensor_tensor", [e2k, ('T3', 0), s5p], [('T3', 0)], out=E1r, in0=E2i, scalar=pcol_(c, 6), in1=E1r, op0=ALU.mult, op1=ALU.add)
                V("scalar_tensor_tensor", [e2k, ('T3', 0), s5p], [('T3', 0)], out=E1i, in0=E2i, scalar=pcol_(c, 7), in1=E1i, op0=ALU.mult, op1=ALU.add)

            def rotate(c):
                j, wre, wim = c['j'], c['wre'], c['wim']
                kt_u = j // 4
                E1r3 = E1r.unsqueeze(1).to_broadcast([128, 2, 256]); E1i3 = E1i.unsqueeze(1).to_broadcast([128, 2, 256])
                for b_ in range(3):
                    bs = slice(b_ * 512, (b_ + 1) * 512)
                    pr = PB()
                    MM(pr[:], BT[:, 0, j, :], ub[:, kt_u, bs], [wbc, T[0]], [pr])
                    pi_ = PB()
                    MM(pi_[:], BT[:, 1, j, :], ub[:, kt_u, bs], [wbc, T[0]], [pi_])
                    pr3 = pr[:].rearrange("p (s t) -> p s t", s=2, t=256); pi3 = pi_[:].rearrange("p (s t) -> p s t", s=2, t=256)
                    t13 = rt1.rearrange("p (s t) -> p s t", s=2, t=256); t23 = rt2.rearrange("p (s t) -> p s t", s=2, t=256)
                    wre3 = wre[:, bs].rearrange("p (s t) -> p s t", s=2, t=256); wim3 = wim[:, bs].rearrange("p (s t) -> p s t", s=2, t=256)
                    V("tensor_tensor", [pr, ('T3', 0)], [mws[0]], out=t13, in0=pr3, in1=E1r3, op=ALU.mult)
                    V("tensor_tensor", [pi_, ('T3', 0)], [mws[1]], out=t23, in0=pi3, in1=E1i3, op=ALU.mult)
                    V("tensor_tensor", [mws[0], mws[1]], [wre], out=wre3, in0=t13, in1=t23, op=ALU.subtract)
                    V("tensor_tensor", [pr, ('T3', 0), mws[0]], [mws[0]], out=t13, in0=pr3, in1=E1i3, op=ALU.mult)
                    V("tensor_tensor", [pi_, ('T3', 0), mws[1]], [mws[1]], out=t23, in0=pi3, in1=E1r3, op=ALU.mult)
                    V("tensor_tensor", [mws[0], mws[1]], [wim], out=wim3, in0=t13, in1=t23, op=ALU.add)

            def scans(c):
                d, j, wre, wim, e2c, eck, ecol, e2k, E2r, E2i = c['d'], c['j'], c['wre'], c['wim'], c['e2c'], c['eck'], c['ecol'], c['e2k'], c['E2r'], c['E2i']
                rho3 = pcol_(c, 1).to_broadcast([128, 256])
                order = list(range(NSEG)) if d == 0 else [3, 2, 1, 0, 5, 4]
                ic = small[:, 32:34]
                for sg in order:
                    first = (sg == 0 and d == 0) or (sg == 3 and d == 1)
                    if d == 0:
                        sl_ = slice(sg * 256, (sg + 1) * 256)
                    else:
                        sl_ = slice(sg * 256 + 255, (sg * 256 - 1) if sg > 0 else None, -1)
                    if sg >= 4:
                        ini = (0.0, 0.0); rk = []
                    elif first:
                        ini = (s5init[:, l, d, j, 0:1], s5init[:, l, d, j, 1:2]); rk = [pk]
                    else:
                        prev = sg - 1 if d == 0 else sg + 1
                        pcol = prev * 256 + ecol
                        V("tensor_scalar", [wre, eck], [('small', 'ic')], out=ic[:, 0:1], in0=wre[:, pcol:pcol + 1], scalar1=e2c[:, 0:1], scalar2=None, op0=ALU.mult)
                        V("scalar_tensor_tensor", [wim, eck, ('small', 'ic')], [('small', 'ic')], out=ic[:, 0:1], in0=wim[:, pcol:pcol + 1], scalar=e2c[:, 2:3], in1=ic[:, 0:1], op0=ALU.mult, op1=ALU.add)
                        V("tensor_scalar", [wre, eck], [('small', 'ic')], out=ic[:, 1:2], in0=wre[:, pcol:pcol + 1], scalar1=e2c[:, 1:2], scalar2=None, op0=ALU.mult)
                        V("scalar_tensor_tensor", [wim, eck, ('small', 'ic')], [('small', 'ic')], out=ic[:, 1:2], in0=wim[:, pcol:pcol + 1], scalar=e2c[:, 0:1], in1=ic[:, 1:2], op0=ALU.mult, op1=ALU.add)
                        ini = (ic[:, 0:1], ic[:, 1:2]); rk = [('small', 'ic')]
                    V("tensor_tensor_scan", [wre, s5p] + rk, [wre], out=wre[:, sl_], data0=rho3, data1=wre[:, sl_], initial=ini[0], op0=ALU.mult, op1=ALU.add)
                    V("tensor_tensor_scan", [wim, s5p] + rk, [wim], out=wim[:, sl_], data0=rho3, data1=wim[:, sl_], initial=ini[1], op0=ALU.mult, op1=ALU.add)
                f6 = small[:, 48:60]
                wre_e = v3(wre)[:, :, ecol]; wim_e = v3(wim)[:, :, ecol]
                V("tensor_scalar", [wre, e2k], [('small', 'f6')], out=f6[:, 0:6], in0=wre_e, scalar1=E2r[:, ecol:ecol + 1], scalar2=None, op0=ALU.mult)
                V("scalar_tensor_tensor", [wim, eck, ('small', 'f6')], [fin_s5], out=fin_s5[:, :, d, j, 0], in0=wim_e, scalar=e2c[:, 3:4], in1=f6[:, 0:6], op0=ALU.mult, op1=ALU.add)
                V("tensor_scalar", [wre, e2k], [('small', 'f6')], out=f6[:, 6:12], in0=wre_e, scalar1=E2i[:, ecol:ecol + 1], scalar2=None, op0=ALU.mult)
                V("scalar_tensor_tensor", [wim, e2k, ('small', 'f6')], [fin_s5], out=fin_s5[:, :, d, j, 1], in0=wim_e, scalar=E2r[:, ecol:ecol + 1], in1=f6[:, 6:12], op0=ALU.mult, op1=ALU.add)

            def unrot(c, eng, sp_, ta2, tb2, tkeys):
                wre, wim, e2k, E2r, E2i = c['wre'], c['wim'], c['e2k'], c['E2r'], c['E2i']
                sgs = slice(2 * sp_, 2 * sp_ + 2)
                E2r6 = E2r.unsqueeze(1).to_broadcast([128, 2, 256]); E2i6 = E2i.unsqueeze(1).to_broadcast([128, 2, 256])
                ta = ta2.rearrange("p (s t) -> p s t", s=2, t=256); tb = tb2.rearrange("p (s t) -> p s t", s=2, t=256)
                wr3 = v3(wre)[:, sgs, :]; wi3 = v3(wim)[:, sgs, :]
                xr3 = xb[:, 0, :].rearrange("p (s t) -> p s t", s=NSEG, t=256)[:, sgs, :]
                xi3 = xb[:, 1, :].rearrange("p (s t) -> p s t", s=NSEG, t=256)[:, sgs, :]
                extra = [fin_s5] if eng == "pool" else []
                P.op(eng, "tensor_tensor", [wre, e2k] + extra, [tkeys[0]], out=ta, in0=wr3, in1=E2r6, op=ALU.mult)
                P.op(eng, "tensor_tensor", [wim, e2k], [tkeys[1]], out=tb, in0=wi3, in1=E2i6, op=ALU.mult)
                P.op(eng, "tensor_tensor", [tkeys[0], tkeys[1]], [('S5X', 0, sp_)], out=xr3, in0=ta, in1=tb, op=ALU.subtract)
                P.op(eng, "tensor_tensor", [wre, e2k, tkeys[0]], [tkeys[0]], out=ta, in0=wr3, in1=E2i6, op=ALU.mult)
                P.op(eng, "tensor_tensor", [wim, e2k, tkeys[1]], [tkeys[1]], out=tb, in0=wi3, in1=E2r6, op=ALU.mult)
                P.op(eng, "tensor_tensor", [tkeys[0], tkeys[1]], [('S5X', 1, sp_)], out=xi3, in0=ta, in1=tb, op=ALU.add)

            def cmat(c, n_in_ot):
                j = c['j']
                for b_ in range(3):
                    bs = slice(b_ * 512, (b_ + 1) * 512)
                    MM(pacc[b_][:], CT[:, 0, j, :], xb[:, 0, bs], [wbc, ('S5X', 0, b_)], [pacc[b_]], start=(n_in_ot == 0), stop=False)
                    MM(pacc[b_][:], CT[:, 1, j, :], xb[:, 1, bs], [wbc, ('S5X', 1, b_)], [pacc[b_]], start=False, stop=(n_in_ot == 7))

            cs_ = [ctx(it) for it in items]
            tables(cs_[0])
            tables_b(cs_[0])
            rotate(cs_[0])
            for i, c in enumerate(cs_):
                if i + 1 < len(cs_):
                    tables(cs_[i + 1])
                scans(c)
                unrot(c, "pool", 0, T[4][:, 0:512], T[4][:, 512:1024], [('T4', 'D', 0), ('T4', 'D', 1)])
                unrot(c, "pool", 1, T[4][:, 0:512], T[4][:, 512:1024], [('T4', 'D', 0), ('T4', 'D', 1)])
                if i + 1 < len(cs_):
                    tables_b(cs_[i + 1])
                    rotate(cs_[i + 1])
                unrot(c, "dve", 2, rt1, rt2, [mws[0], mws[1]])
                cmat(c, i % 8)
                if i % 8 == 7:
                    ot = c['ot']
                    for b_ in range(3):
                        bs = slice(b_ * 512, (b_ + 1) * 512)
                        V("scalar_tensor_tensor", [T[0], pk, pacc[b_]], [('ymix', 4 + ot)], out=ymix[:, 4 + ot, bs], in0=ub[:, ot, bs], scalar=s5d[:, l, ot:ot + 1], in1=pacc[b_][:], op0=ALU.mult, op1=ALU.add)
            pb_lim[0] = 6
            for sg in range(NSEG):
                for d in range(2):
                    dst = o_s5[sg, l, d].rearrange("(j p) r -> p j r", p=128)
                    P.dma("sp", R=[fin_s5], is_output=True, out=dst, in_=fin_s5[:, sg, d], allow_slow_non_contiguous=True)
            wg_ = load_w(glu_w, l * 256, 2, 0, 512)
            yk = [('ymix', 4), ('ymix', 5)]
            gl = T[5]
            for b_ in range(3):
                bs = slice(b_ * 512, (b_ + 1) * 512)
                pgs = []; pvs = []
                for ti in range(2):
                    pgs.append(dense_fm(wg_, 256 + ti * 128, b_, nk=2, rhs=ymix[:, 4:6, :], rkeys=yk))
                    pvs.append(dense_fm(wg_, ti * 128, b_, nk=2, rhs=ymix[:, 4:6, :], rkeys=yk))
                for ti in range(2):
                    sgm = gl[:, ti * 512:(ti + 1) * 512]
                    ACT(sgm, pgs[ti][:], AF.Sigmoid, [pgs[ti]], [gl])
                    V("tensor_tensor", [pvs[ti], gl], [('ymix', 4 + ti)], out=ymix[:, 4 + ti, bs], in0=sgm, in1=pvs[ti][:], op=ALU.mult)

        modulation_all(0)
        for l in range(depth):
            mod_finish(l)
            norm_mod(l, 0)
            if 'ssd' in enabled:
                attn_mixer(l, 'ssd')
            if 'ret' in enabled:
                attn_mixer(l, 'ret')
            if 's5' in enabled:
                s5_mixer(l)
            if 'lru' in enabled:
                lru_mixer(l)
            out_proj(l)
            if 'ffn' not in enabled and l + 1 < depth:
                modulation_all(l + 1)
            if 'ffn' in enabled:
                norm_mod(l, 1)
                ffn(l)
                if l + 1 < depth:
                    mod_tail(l + 1)
                if len(enabled & {'ssd', 'ret', 's5', 'lru'}) < 4 and not DEBUG_TAPS:
                    V("memset", [], ALLY, ap=ymix[:], constant=0.0)
        final_out()
        if DEBUG_TAPS:
            d_mod = dout("dbg_mod", [128, 96])
            P.dma("sp", R=[modsb], is_output=True, out=d_mod, in_=modsb[:].rearrange("p a b -> p (a b)"))
            d_h = dout("dbg_h", [128, KT * NT])
            P.dma("pool", R=[('h', 0), ('h', 1), ('h', 2)], is_output=True, out=d_h, in_=h[:].rearrange("p a b -> p (a b)"))
            d_x = dout("dbg_x", [128, KT * NT])
            P.dma("sp", R=[('x', 0), ('x', 1), ('x', 2)], is_output=True, out=d_x, in_=x[:].rearrange("p a b -> p (a b)"))
            d_y = dout("dbg_ymix", [128, KT * NT])
            P.dma("pool", R=ALLY, is_output=True, out=d_y, in_=ymix[:].rearrange("p a b -> p (a b)"))
        P.run()
    return nc


_CACHE = {}


def kernel(**inp):
    inp = {k: np.asarray(v) for k, v in inp.items()}
    enabled = frozenset(ENABLED)
    depth = DEPTH
    cso = _consts()
    cst = cso.build()
    shared = _shared_layouts(inp)
    in_maps = []
    pko = None
    for core in range(8):
        pko = _core_pack(inp, core)
        (ka, ia), ib = _assign(core)
        if ka == 's':
            xa = inp['x_sample'][ia]
            ssd_i = _state_layout(inp['state_ssd'][ia], 'ssd')
            ret_i = _state_layout(inp['state_ret'][ia], 'ret')
        else:
            xa = inp['x_prompt'][ia].reshape(1024, D)
            ssd_i = np.zeros((L * 128, 256), np.float32)
            ret_i = np.zeros((L * 128, 256), np.float32)
        xb = inp['x_prompt'][ib].reshape(512, D)
        cos, sin = _rot_tables(ka == 's')
        m = dict(shared)
        m.update(xin=np.ascontiguousarray(np.concatenate([xa, xb], axis=0), dtype=np.float32), pk=pko.build(), cst=cst,
                 cosT=cos, sinT=sin, ssdinit=ssd_i, retinit=ret_i)
        in_maps.append(m)
    key = (enabled, depth, DEBUG_TAPS)
    if key not in _CACHE:
        _CACHE[key] = build(pko, cso, enabled, depth)
    nc = _CACHE[key]
    res = run_bass_kernel_spmd(nc, in_maps, core_ids=list(range(8)))
    outs = res.results
    LAST['outs'] = outs
    y_p = np.zeros((32, 256, D), np.float32)
    y_s = np.zeros((4, 1024, D), np.float32)
    n_ssd = np.zeros((32, L, 2, 4, 64, 64), np.float32)
    n_ret = np.zeros((32, L, 2, 4, 64, 64), np.float32)
    n_s5 = np.zeros((32, L, 2, 16, 64, 2), np.float32)
    n_lru = np.zeros((32, L, 2, 256), np.float32)
    for core in range(8):
        r = outs[core]
        (ka, ia), ib = _assign(core)
        y = r['y']
        segs = []
        if ka == 's':
            y_s[ia] = y[0:1024]
        else:
            for k, bidx in enumerate(ia):
                segs.append((k, bidx))
        for k, bidx in enumerate(ib):
            segs.append((4 + k, bidx))
        for sgi, bidx in segs:
            y_p[bidx] = y[sgi * 256:(sgi + 1) * 256]
            n_ssd[bidx] = r['o_ssd'][sgi]
            n_ret[bidx] = r['o_ret'][sgi]
            n_s5[bidx] = r['o_s5'][sgi].reshape(L, 2, 16, 64, 2)
            n_lru[bidx] = r['o_lru'][sgi]
    return (y_p, y_s, n_ssd, n_ret, n_s5, n_lru)
```

```python
import math
from contextlib import ExitStack
import numpy as np
import concourse.bass as bass
import concourse.mybir as mybir
from concourse.bass_utils import run_bass_kernel_spmd

F32 = mybir.dt.float32
BF16 = mybir.dt.bfloat16
I32 = mybir.dt.int32
ALU = mybir.AluOpType
AF = mybir.ActivationFunctionType

ENABLED = {'ssd', 'ret', 's5', 'lru', 'ffn'}
DEPTH = 4
DEBUG_TAPS = False
LAST = {}

L = 4; D = 1024; KT = 8; NT = 1536; NSEG = 6; SEG = 256; NCH = 12; CH = 128
DFF = 2816; NJ = 22
EPS = 1e-6
ENGS = ("pe", "act", "dve", "pool", "sp")
N_DMA_SLOTS = 32


class Prog:
    def __init__(self, nc):
        self.nc = nc
        self.ops = {e: [] for e in ENGS}
        self.cnt = {}
        self.last_w = {}
        self.readers = {}
        self.known = {e: {} for e in ENGS}
        self.dma_rr = 0
        self.dma_rr_q = {}
        self.out_dmas = []

    EXPAND = {"T3": [("T3", 0), ("T3", 1), ("T3", 2)],
              "T4": [("T4", "D", 0), ("T4", "D", 1), ("T4", "Ds"), ("T4", "q"), ("T4", "P"), ("T4", "b", 0), ("T4", "b", 1), ("T4", "b", 2)],
              "T7": [("T7", 0), ("T7", 1), ("T7", 2)]}

    @classmethod
    def _keys(cls, lst):
        out = []
        for k in lst:
            if k is None:
                continue
            if not isinstance(k, (str, tuple)):
                k = k.tensor.name if hasattr(k, 'tensor') else k.name
            if k in cls.EXPAND:
                out.extend(cls.EXPAND[k])
            else:
                out.append(k)
        return out

    def _deps(self, eng, reads, writes):
        need = {}

        def add(sv):
            if sv is None:
                return
            s, v = sv
            if need.get(s, 0) < v:
                need[s] = v
        for k in reads:
            add(self.last_w.get(k))
        for k in writes:
            add(self.last_w.get(k))
            for sv in self.readers.get(k, ()):
                add(sv)
        waits = []
        kn = self.known[eng]
        for s, v in need.items():
            if kn.get(s, 0) >= v:
                continue
            kn[s] = v
            waits.append((s, v))
        return waits

    def _record(self, semkey, val, reads, writes):
        for k in reads:
            self.readers.setdefault(k, []).append((semkey, val))
        for k in writes:
            self.last_w[k] = (semkey, val)
            self.readers[k] = []

    def op(self, eng, name, R=(), W=(), noself=False, **kw):
        fn = (name, kw)
        reads = self._keys(R)
        writes = self._keys(W)
        waits = self._deps(eng, reads, writes)
        if noself:
            waits = [(s, v) for (s, v) in waits if s != eng]
        self.cnt[eng] = self.cnt.get(eng, 0) + 1
        val = self.cnt[eng]
        self.ops[eng].append((fn, waits, (eng, 1)))
        self._record(eng, val, reads, writes)

    def dma(self, queue, R=(), W=(), is_output=False, **kw):
        fn = ("dma_start", kw)
        reads = self._keys(R)
        writes = self._keys(W)
        half = N_DMA_SLOTS // 2
        rr = self.dma_rr_q.get(queue, 0)
        self.dma_rr_q[queue] = (rr + 1) % half
        slot = rr + (half if queue == "pool" else 0)
        sk = ("dma", slot)
        waits = self._deps(queue, reads, writes)
        prev = self.cnt.get(sk, 0)
        if prev and self.known[queue].get(sk, 0) < prev:
            self.known[queue][sk] = prev
            waits.append((sk, prev))
        self.cnt[sk] = prev + 1
        val = prev + 1
        self.ops[queue].append((fn, waits, (sk, 16)))
        self._record(sk, val, reads, writes)
        if is_output:
            self.out_dmas.append((sk, val))

    def run(self):
        nc = self.nc
        with ExitStack() as st:
            sems = {}
            for e in ENGS:
                sems[e] = st.enter_context(nc.semaphore("s_" + e))
            for i in range(N_DMA_SLOTS):
                sems[("dma", i)] = st.enter_context(nc.semaphore("s_dma%d" % i))
            block = st.enter_context(nc.Block())
            mult = lambda s: 16 if isinstance(s, tuple) else 1

            def replay(ename, eh, final=False):
                for fn, waits, (isem, iamt) in self.ops[ename]:
                    for s, v in waits:
                        eh.wait_ge(sems[s], v * mult(s))
                    ins = getattr(eh, fn[0])(**fn[1])
                    ins.then_inc(sems[isem], iamt)
                if final:
                    done = {}
                    for s, v in self.out_dmas:
                        done[s] = max(done.get(s, 0), v)
                    for s, v in done.items():
                        eh.wait_ge(sems[s], v * 16)

            @block.tensor
            def _(e):
                replay("pe", e)

            @block.scalar
            def _(e):
                replay("act", e)

            @block.vector
            def _(e):
                replay("dve", e)

            @block.gpsimd
            def _(e):
                replay("pool", e)

            @block.sync
            def _(e):
                replay("sp", e, final=True)


def _cols(v, nt):
    return np.ascontiguousarray(np.asarray(v, np.float32).reshape(nt, 128).T)


class Pack:
    def __init__(self):
        self.off = {}
        self.n = 0
        self.arrs = []

    def add(self, name, arr):
        a = np.ascontiguousarray(arr, np.float32).reshape(128, -1)
        self.off[name] = (self.n, a.shape[1])
        self.n += a.shape[1]
        self.arrs.append(a)

    def build(self):
        return np.ascontiguousarray(np.concatenate(self.arrs, axis=1))


def _consts():
    pk = Pack()
    i = np.arange(128)
    pk.add('ident', np.eye(128))
    pk.add('maskU', (i[:, None] <= i[None, :]).astype(np.float32))
    pk.add('maskL', (i[:, None] >= i[None, :]).astype(np.float32))
    pk.add('negF', np.where(i[:, None] <= i[None, :], 0.0, -30000.0))
    pk.add('negB', np.where(i[:, None] >= i[None, :], 0.0, -30000.0))
    pk.add('ones', np.ones((128, 128)))
    rp = np.zeros((128, 128), np.float32)
    for hb in (0, 64):
        for d in range(64):
            if (d % 32) < 16:
                rp[hb + d + 16, hb + d] = -1.0
            else:
                rp[hb + d - 16, hb + d] = 1.0
    pk.add('rperm', rp)
    hh = np.arange(4, dtype=np.float32)
    lf = np.log1p(-2.0 ** (-5.0 - 0.0 - hh)).astype(np.float32)
    lb = np.log1p(-2.0 ** (-5.0 - 0.5 - hh)).astype(np.float32)
    la = np.stack([lf, lb], axis=0)[None, :, None, :] * np.ones((128, 1, NCH, 1), np.float32)
    pk.add('retla', la)
    b64 = np.zeros((128, 128), np.float32)
    b64[0:64, 0:64] = 1.0 / 64.0
    b64[64:128, 64:128] = 1.0 / 64.0
    pk.add('blk64', b64)
    tt_ = np.arange(256, dtype=np.float32)
    pk.add('rampf', np.broadcast_to(tt_ + 1.0, (128, 256)))
    pk.add('rampb', np.broadcast_to(256.0 - tt_, (128, 256)))
    return pk


def _rot_tables(sample):
    cos = np.ones((128, 1024), np.float32)
    sin = np.zeros((128, 1024), np.float32)
    if sample:
        t = np.arange(1024)
        row = (t // 64).astype(np.float32)
        col = (t % 64).astype(np.float32)
        nf = 16
        inv = (10000.0 ** (-np.arange(nf, dtype=np.float32) / nf)).astype(np.float32)
        for p in range(128):
            d = p % 64
            half = d // 32
            f = d % 16
            ang = (row if half == 0 else col) * inv[f]
            cos[p] = np.cos(ang)
            sin[p] = np.sin(ang)
    return cos, sin


def _shared_layouts(inp):
    sh = {}
    sh['w_in'] = np.ascontiguousarray(inp['w_in'].reshape(L * 1024, 2568))
    sh['w_out'] = np.ascontiguousarray(inp['w_out'].reshape(L * 1024, 1024))
    sh['w_up'] = np.ascontiguousarray(inp['ffn_w_up'].reshape(L * 1024, 2 * DFF))
    sh['w_down'] = np.ascontiguousarray(inp['ffn_w_down'].reshape(L * DFF, 1024))
    sh['mod_w'] = np.ascontiguousarray(inp['mod_w'].reshape(L * 1024, 6144))
    sh['glu_w'] = np.ascontiguousarray(inp['s5_glu_w'].reshape(L * 256, 512))
    lw = np.zeros((L, 128, 2, 2, 2, 128), np.float32)
    for wi, nm in enumerate(('lru_wa', 'lru_wx')):
        w = inp[nm]
        for ti in range(2):
            for bb in range(2):
                blk = ti * 2 + bb
                lw[:, bb * 64:(bb + 1) * 64, wi, :, ti, bb * 64:(bb + 1) * 64] = np.transpose(w[:, :, blk], (0, 2, 1, 3))
    sh['lruw'] = np.ascontiguousarray(lw.reshape(L * 128, 2 * 2 * 2 * 128))
    bt = np.zeros((L, 128, 2, 8, 128), np.float32)
    ct = np.zeros((L, 128, 2, 8, 128), np.float32)
    for ri, (nb, ncn) in enumerate((('s5_b_re', 's5_c_re'), ('s5_b_im', 's5_c_im'))):
        b = inp[nb]
        c = inp[ncn]
        for g in range(16):
            j = g // 2
            gl = g % 8
            co = (g % 2) * 64
            bt[:, gl * 16:(gl + 1) * 16, ri, j, co:co + 64] = np.transpose(b[:, g], (0, 2, 1))
            ct[:, co:co + 64, ri, j, gl * 16:(gl + 1) * 16] = np.transpose(c[:, g], (0, 2, 1))
    sh['s5bt'] = np.ascontiguousarray(bt.reshape(L * 128, 2 * 8 * 128))
    sh['s5ct'] = np.ascontiguousarray(ct.reshape(L * 128, 2 * 8 * 128))
    return sh


def _head_map(kind, h):
    if kind == 'ssd':
        return (h // 2) * 64, h % 2
    return (h % 2) * 64, h // 2


def _state_layout(st, kind):
    o = np.zeros((L, 128, 2, 2, 64), np.float32)
    for h in range(4):
        base, slot = _head_map(kind, h)
        o[:, base:base + 64, :, slot, :] = np.transpose(st[:, :, h], (0, 2, 1, 3))
    return np.ascontiguousarray(o.reshape(L * 128, 256))


def _core_pack(inp, core):
    sample = core < 4
    pk = Pack()
    condA = inp['c'][core] if sample else inp['c_ctx']
    cond = np.stack([_cols(condA, 8), _cols(inp['c_ctx'], 8)], axis=-1)
    pk.add('cond', cond)
    pk.add('f', np.full((128, 1), 1.0 if sample else 0.0, np.float32))
    pk.add('n1w', np.stack([_cols(inp['norm1_w'][l], 8) for l in range(L)], axis=1))
    pk.add('n2w', np.stack([_cols(inp['norm2_w'][l], 8) for l in range(L)], axis=1))
    pk.add('modb', np.stack([_cols(inp['mod_b'][l], 48) for l in range(L)], axis=1))
    pk.add('fnw', _cols(inp['final_norm_w'], 8))
    pk.add('lcw', np.stack([np.stack([_cols(inp['lru_conv_w'][l, j], 2) for j in range(4)], axis=-1) for l in range(L)], axis=1))
    pk.add('lcb', np.stack([_cols(inp['lru_conv_b'][l], 2) for l in range(L)], axis=1))
    pk.add('llam', np.stack([np.stack([_cols(inp['lru_lambda'][l, d], 2) for d in range(2)], axis=1) for l in range(L)], axis=1))
    pk.add('lba', np.stack([np.stack([_cols(inp['lru_ba'][l, d].reshape(-1), 2) for d in range(2)], axis=1) for l in range(L)], axis=1))
    pk.add('lbx', np.stack([np.stack([_cols(inp['lru_bx'][l, d].reshape(-1), 2) for d in range(2)], axis=1) for l in range(L)], axis=1))
    if sample:
        li = inp['state_lru'][core]
    else:
        li = np.zeros((L, 2, 256), np.float32)
    pk.add('linit', np.stack([np.stack([_cols(li[l, d], 2) for d in range(2)], axis=1) for l in range(L)], axis=1))
    pk.add('scw', np.stack([np.stack([_cols(inp['ssd_conv_w'][l, j], 4) for j in range(4)], axis=-1) for l in range(L)], axis=1))
    pk.add('scb', np.stack([_cols(inp['ssd_conv_b'][l], 4) for l in range(L)], axis=1))
    pk.add('sdtb', np.broadcast_to(inp['ssd_dt_bias'].reshape(1, L, 8), (128, L, 8)))
    pk.add('salog', np.broadcast_to(inp['ssd_a_log'].reshape(1, L, 8), (128, L, 8)))
    pk.add('sdcol', np.stack([_cols(np.repeat(inp['ssd_d'][l], 64), 2) for l in range(L)], axis=1))
    pk.add('snw', np.stack([_cols(inp['ssd_norm_w'][l], 2) for l in range(L)], axis=1))
    pk.add('rgn', np.stack([_cols(inp['ret_gn_w'][l], 2) for l in range(L)], axis=1))
    pk.add('fcw', np.stack([np.stack([_cols(inp['ffn_conv_w'][l, j], NJ) for j in range(3)], axis=-1) for l in range(L)], axis=1))
    pk.add('fcb', np.stack([_cols(inp['ffn_conv_b'][l], NJ) for l in range(L)], axis=1))
    pk.add('lre', np.stack([np.stack([_cols(inp['s5_lam_re'][l, d].reshape(-1), 8) for d in range(2)], axis=1) for l in range(L)], axis=1))
    pk.add('lim', np.stack([np.stack([_cols(inp['s5_lam_im'][l, d].reshape(-1), 8) for d in range(2)], axis=1) for l in range(L)], axis=1))
    pk.add('lstep', np.stack([np.stack([_cols(np.repeat(inp['s5_log_step'][l, d], 64), 8) for d in range(2)], axis=1) for l in range(L)], axis=1))
    pk.add('s5d', np.stack([_cols(inp['s5_d'][l], 2) for l in range(L)], axis=1))
    if sample:
        s5i = inp['state_s5'][core]
    else:
        s5i = np.zeros((L, 2, 16, 64, 2), np.float32)
    pk.add('s5init', np.stack([np.stack([np.stack([_cols(s5i[l, d, :, :, ri].reshape(-1), 8) for ri in range(2)], axis=-1)
                                         for d in range(2)], axis=1) for l in range(L)], axis=1))
    return pk


def _assign(core):
    if core < 4:
        return ('s', core), [2 * core, 2 * core + 1]
    base = 8 + (core - 4) * 6
    return ('p', [base, base + 1, base + 2, base + 3]), [base + 4, base + 5]


def build(pko, cso, enabled, depth):
    nc = bass.Bass("TRN2", target_bir_lowering=False)

    def din(name, shape):
        return nc.dram_tensor(name, list(shape), F32, kind="ExternalInput").ap()

    def dout(name, shape):
        return nc.dram_tensor(name, list(shape), F32, kind="ExternalOutput").ap()

    xin = din("xin", [NT, D])
    pk_d = din("pk", [128, pko.n])
    cs_d = din("cst", [128, cso.n])
    cos_d = din("cosT", [128, 1024])
    sin_d = din("sinT", [128, 1024])
    w_in = din("w_in", [L * 1024, 2568])
    w_out = din("w_out", [L * 1024, 1024])
    w_up = din("w_up", [L * 1024, 2 * DFF])
    w_down = din("w_down", [L * DFF, 1024])
    mod_w = din("mod_w", [L * 1024, 6144])
    glu_w = din("glu_w", [L * 256, 512])
    lruw = din("lruw", [L * 128, 1024])
    s5bt = din("s5bt", [L * 128, 2048])
    s5ct = din("s5ct", [L * 128, 2048])
    ssdinit = din("ssdinit", [L * 128, 256])
    retinit = din("retinit", [L * 128, 256])
    y_out = dout("y", [NT, D])
    o_ssd = dout("o_ssd", [NSEG, L, 2, 4, 64, 64])
    o_ret = dout("o_ret", [NSEG, L, 2, 4, 64, 64])
    o_s5 = dout("o_s5", [NSEG, L, 2, 1024, 2])
    o_lru = dout("o_lru", [NSEG, L, 2, 256])

    with ExitStack() as st:
        def sb(name, shape, dt=F32):
            return st.enter_context(nc.sbuf_tensor(name, list(shape), dt))

        def psum(name, shape, dt=F32):
            return st.enter_context(nc.psum_tensor(name, list(shape), dt))

        P = Prog(nc)

        def V(name, R, W, **kw):
            P.op("dve", name, R, W, **kw)

        def A(name, R, W, **kw):
            P.op("act", name, R, W, **kw)

        def ACT(out, in_, func, R, W, **kw):
            P.op("act", "activation", R, W, out=out, in_=in_, func=func, **kw)

        def MM(out, lhsT, rhs, R, W, start=True, stop=True, noself=False):
            P.op("pe", "matmul", R, W, noself=noself, out=out, lhsT=lhsT, rhs=rhs, start=start, stop=stop)

        x = sb("x", [128, KT, NT])
        h = sb("h", [128, KT, NT], BF16)
        ymix = sb("ymix", [128, KT, NT], BF16)
        ws = [sb("ws%d" % i, [128, 8, 512], BF16) for i in range(3)]
        mws = [sb("mws%d" % i, [128, 8, 128], BF16) for i in range(4)]
        cpad = sb("cpad", [128, NSEG, 260])
        pk = sb("pks", [128, pko.n])
        cst = sb("csts", [128, cso.n])
        T = [sb("T%d" % i, [128, NT]) for i in range(8)]
        cs = sb("csil", [128, KT, 2], BF16)
        modsb = sb("modsb", [128, 48, 2])
        modA = sb("modA", [128, 2, KT, 2])
        onesb = sb("onesb", [128, 128], BF16)
        identb = sb("identb", [128, 128], BF16)
        rpermb = sb("rpermb", [128, 128], BF16)
        small = sb("small", [128, 512])
        rs = sb("rs", [128, 256])
        fin_lru = sb("fin_lru", [128, 2, 2, NSEG])
        pb = [psum("pb%d" % i, [128, 512]) for i in range(6)]
        pmod = psum("pmod", [128, 512])
        pb7 = pb + [pmod]
        pbb = psum("pbb", [128, 1024], BF16)
        pbi = [0]
        pb_lim = [6]

        pb_rot = [None]

        def PB():
            lst = pb_rot[0] if pb_rot[0] is not None else pb7
            pbi[0] = (pbi[0] + 1) % len(lst)
            return lst[pbi[0]]

        def pkv(name, *dims):
            o, n = pko.off[name]
            v = pk[:, o:o + n]
            if len(dims) == 2:
                v = v.rearrange("p (a b) -> p a b", a=dims[0], b=dims[1])
            elif len(dims) == 3:
                v = v.rearrange("p (a b c) -> p a b c", a=dims[0], b=dims[1], c=dims[2])
            elif len(dims) == 4:
                v = v.rearrange("p (a b c d) -> p a b c d", a=dims[0], b=dims[1], c=dims[2], d=dims[3])
            return v

        def csv(name):
            o, n = cso.off[name]
            return cst[:, o:o + n]

        fcol = pkv('f')
        ALLY = [('ymix', i) for i in range(8)]

        P.dma("sp", W=[pk], out=pk[:], in_=pk_d)
        P.dma("sp", W=[cst], out=cst[:], in_=cs_d)
        V("memset", [], [onesb], ap=onesb[:], constant=1.0 / 1024.0)
        V("tensor_copy", [cst], [identb], out=identb[:], in_=csv('ident'))
        V("tensor_copy", [cst], [rpermb], out=rpermb[:], in_=csv('rperm'))
        V("memset", [], [cpad], ap=cpad[:], constant=0.0)
        V("memset", [], ALLY, ap=ymix[:], constant=0.0)
        V("memset", [], [small], ap=small[:], constant=0.0)
        V("memset", [small], [small], ap=small[:, 0:1], constant=EPS)
        ACT(cs[:], pkv('cond', 8, 2), AF.Silu, [pk], [cs])

        for c in range(NCH):
            xs_ = T[c % 2]
            P.dma("sp", W=[xs_], out=xs_[:, 0:1024], in_=xin[c * 128:(c + 1) * 128, :])
            for half in range(2):
                ps = PB()
                for q in range(4):
                    kt = half * 4 + q
                    P.op("pe", "transpose", [xs_, cst], [ps], out=ps[:, q * 128:(q + 1) * 128], in_=xs_[:, kt * 128:(kt + 1) * 128], identity=csv('ident'))
                A("copy", [ps], [('x', c // 4)], out=x[:, half * 4:half * 4 + 4, c * 128:(c + 1) * 128], in_=ps[:].rearrange("p (a b) -> p a b", a=4, b=128))

        wsi = [0]

        def load_w(dram2d, row0, nk, col0, ncols):
            s = ws[wsi[0] % 3]
            wsi[0] += 1
            src = dram2d[row0:row0 + nk * 128, col0:col0 + ncols].rearrange("(kt p) c -> p kt c", p=128)
            P.dma("pool", W=[s], out=s[:, 0:nk, 0:ncols], in_=src)
            return s

        def mm_acc(ps_ap, pairs, R, W):
            n = len(pairs)
            for i, (lt, rh) in enumerate(pairs):
                MM(ps_ap, lt, rh, R, W if i in (0, n - 1) else [], start=(i == 0), stop=(i == n - 1), noself=(i > 0))

        def dense_fm(wslot, c0, b, nk=8, rhs=None, rkeys=None):
            ps = PB()
            if rhs is None:
                rhs = h
                rkeys = [('h', b)]
            pairs = [(wslot[:, kt, c0:c0 + 128], rhs[:, kt, b * 512:(b + 1) * 512]) for kt in range(nk)]
            mm_acc(ps[:], pairs, R=[wslot] + rkeys, W=[ps])
            return ps

        def mod_issue(l, sl):
            m = mws[sl % 4]
            src = mod_w[l * 1024:(l + 1) * 1024, sl * 128:(sl + 1) * 128].rearrange("(kt p) c -> p kt c", p=128)
            P.dma("pool", W=[m], out=m[:], in_=src)

        def mod_mm(l, sl):
            m = mws[sl % 4]
            pairs = [(m[:, kt, :], cs[:, kt, :]) for kt in range(KT)]
            mm_acc(pmod[:, sl * 2:sl * 2 + 2], pairs, R=[m, cs], W=[pmod])

        def mod_finish(l):
            mb = pkv('modb', L, 48)
            mp3 = pmod[:, 0:96].rearrange("p (a b) -> p a b", a=48, b=2)
            for c in range(2):
                V("tensor_tensor", [pmod, pk], [modsb], out=modsb[:, :, c], in0=mp3[:, :, c], in1=mb[:, l, :], op=ALU.add)
            for which, (nm, sco) in enumerate((('n1w', 8), ('n2w', 32))):
                nw = pkv(nm, L, 8)
                V("tensor_scalar", [modsb], [modA], out=modA[:, which], in0=modsb[:, sco:sco + 8, :], scalar1=1.0, scalar2=None, op0=ALU.add)
                V("tensor_tensor", [modA, pk], [modA], out=modA[:, which], in0=modA[:, which], in1=nw[:, l, :].unsqueeze(2).to_broadcast([128, 8, 2]), op=ALU.mult)

        def modulation_all(l):
            for sl in range(48):
                mod_issue(l, sl)
                if sl >= 2:
                    mod_mm(l, sl - 2)
            mod_mm(l, 46)
            mod_mm(l, 47)

        sqb = T[7][:, 0:1024].bitcast(BF16).rearrange("p (a b) -> p a b", a=8, b=256)

        rs_alt = [(rs[:], rs), (small[:, 256:512], ('small', 'rs'))]

        def rstd_block(nb):
            b = nb // 2
            tsl = slice(nb * 256, (nb + 1) * 256)
            rs_ap, rs_k = rs_alt[nb % 2]
            ACT(sqb, x[:, :, tsl], AF.Square, [('x', b)], [T[7]])
            ss = PB()
            mm_acc(ss[:, 0:256], [(onesb[:], sqb[:, kt, :]) for kt in range(KT)], R=[T[7], onesb], W=[ss])
            ACT(rs_ap, ss[:, 0:256], AF.Ln, [ss, small], [rs_k], bias=small[:, 0:1], scale=1.0)
            ACT(rs_ap, rs_ap, AF.Exp, [rs_k], [rs_k], scale=-0.5)
            return rs_ap, rs_k

        def norm_mod(l, which):
            sho = 0 if which == 0 else 24
            for nb in range(6):
                c = 0 if nb < 4 else 1
                b = nb // 2
                tsl = slice(nb * 256, (nb + 1) * 256)
                rs_ap, rs_k = rstd_block(nb)
                for kt in range(KT):
                    tmp = T[5 + kt % 2][:, 0:256]
                    tk = T[5 + kt % 2]
                    V("tensor_tensor", [('x', b), rs_k], [tk], out=tmp, in0=x[:, kt, tsl], in1=rs_ap, op=ALU.mult)
                    V("tensor_scalar", [tk, modA, modsb], [('h', b)], out=h[:, kt, tsl], in0=tmp, scalar1=modA[:, which, kt, c:c + 1], scalar2=modsb[:, sho + kt, c:c + 1], op0=ALU.mult, op1=ALU.add)

        def pad_fix(npad_l, npad_r):
            if npad_l:
                V("tensor_scalar", [cpad, pk], [cpad], out=cpad[:, 1:4, 2 - npad_l:2], in0=cpad[:, 0:3, 258 - npad_l:258], scalar1=fcol, scalar2=None, op0=ALU.mult)
            if npad_r:
                V("tensor_scalar", [cpad, pk], [cpad], out=cpad[:, 0:3, 258:258 + npad_r], in0=cpad[:, 1:4, 2:2 + npad_r], scalar1=fcol, scalar2=None, op0=ALU.mult)

        def conv_from_cpad(out3, wcols, bcol, ktaps, okeys):
            o0 = 2 - ktaps // 2
            ACT(out3, cpad[:, :, o0:o0 + 256], AF.Identity, [cpad, pk], okeys, scale=wcols[0], bias=bcol)
            for j in range(1, ktaps):
                V("scalar_tensor_tensor", [cpad, pk] + okeys, okeys, out=out3, in0=cpad[:, :, o0 + j:o0 + j + 256], scalar=wcols[j], in1=out3, op0=ALU.mult, op1=ALU.add)

        def evac_to_cpad(ps, b):
            A("copy", [ps], [cpad], out=cpad[:, 2 * b:2 * b + 2, 2:258], in_=ps[:].rearrange("p (a b) -> p a b", a=2, b=256))

        def v3(t):
            return t[:].rearrange("p (a b) -> p a b", a=NSEG, b=256)

        def lru_mixer(l):
            wsl = load_w(w_in, l * 1024, 8, 2056, 512)
            t5b = T[5][:].bitcast(BF16)
            lw = t5b[:, 0:1024].rearrange("p (w d t o) -> p w d t o", w=2, d=2, t=2, o=128)
            xcb = t5b[:, 1024:1024 + NT]
            P.dma("pool", W=[T[5]], out=t5b[:, 0:1024], in_=lruw[l * 128:(l + 1) * 128, :])
            lcw = pkv('lcw', L, 2, 4); lcb = pkv('lcb', L, 2); llam = pkv('llam', L, 2, 2)
            lba = pkv('lba', L, 2, 2); lbx = pkv('lbx', L, 2, 2); linit = pkv('linit', L, 2, 2)
            cpv = small[:, 8:12]
            ACT(cpv, llam[:, l].rearrange("p a b -> p (a b)"), AF.Exp, [pk], [small], scale=-1.0)
            ACT(cpv, cpv, AF.Ln, [small], [small], bias=1.0, scale=1.0)
            V("tensor_scalar", [small], [small], out=cpv, in0=cpv, scalar1=-8.0, scalar2=None, op0=ALU.mult)
            for ti in range(2):
                for b in range(3):
                    ps = dense_fm(wsl, ti * 128, b)
                    evac_to_cpad(ps, b)
                pad_fix(2, 1)
                xc = T[0]
                conv_from_cpad(v3(xc), [lcw[:, l, ti, j:j + 1] for j in range(4)], lcb[:, l, ti:ti + 1], 4, [xc])
                gg = T[1]
                for b in range(3):
                    ps = dense_fm(wsl, 256 + ti * 128, b)
                    bs = slice(b * 512, (b + 1) * 512)
                    t2 = T[2][:, 0:512]
                    ACT(t2, ps[:], AF.Square, [ps], [T[2]])
                    V("tensor_scalar", [T[2]], [T[2]], out=t2, in0=t2, scalar1=0.044715, scalar2=1.0, op0=ALU.mult, op1=ALU.add)
                    V("tensor_tensor", [T[2], ps], [T[2]], out=t2, in0=t2, in1=ps[:], op=ALU.mult)
                    ACT(t2, t2, AF.Sigmoid, [T[2]], [T[2]], scale=1.5957691216057308)
                    V("tensor_tensor", [T[2], ps], [gg], out=gg[:, bs], in0=t2, in1=ps[:], op=ALU.mult)
                hacc = T[2]
                A("copy", [xc], [T[5]], out=xcb, in_=xc[:])
                for d in range(2):
                    av = T[3]; uv = T[4]
                    for b in range(3):
                        bs = slice(b * 512, (b + 1) * 512)
                        pa = PB()
                        MM(pa[:], lw[:, 0, d, ti, :], xcb[:, bs], [T[5]], [pa])
                        px = PB()
                        MM(px[:], lw[:, 1, d, ti, :], xcb[:, bs], [T[5]], [px])
                        ACT(av[:, bs], pa[:], AF.Sigmoid, [pa, pk], [av], bias=lba[:, l, d, ti:ti + 1], scale=1.0)
                        ACT(uv[:, bs], px[:], AF.Sigmoid, [px, pk], [uv], bias=lbx[:, l, d, ti:ti + 1], scale=1.0)
                    cpc = small[:, 8 + d * 2 + ti:8 + d * 2 + ti + 1]
                    ACT(av[:], av[:], AF.Exp, [av, small], [av], scale=cpc)
                    V("tensor_tensor", [uv, xc], [uv], out=uv[:], in0=uv[:], in1=xc[:], op=ALU.mult)
                    m2 = T[6]
                    ACT(m2[:], av[:], AF.Square, [av], [m2])
                    ACT(m2[:], m2[:], AF.Sqrt, [m2], [m2], scale=-1.0, bias=1.0)
                    V("tensor_tensor", [uv, m2], [uv], out=uv[:], in0=uv[:], in1=m2[:], op=ALU.mult)
                    hd = hacc if d == 0 else T[6]
                    icol = small[:, 16:17]
                    order = list(range(NSEG)) if d == 0 else [3, 2, 1, 0, 5, 4]
                    for sg in order:
                        first = (sg == 0 and d == 0) or (sg == 3 and d == 1)
                        if sg >= 4:
                            init = 0.0
                            rk = []
                        elif first:
                            init = linit[:, l, d, ti:ti + 1]
                            rk = [pk]
                        else:
                            prev = sg - 1 if d == 0 else sg + 1
                            pcol = hd[:, prev * 256 + 255:prev * 256 + 256] if d == 0 else hd[:, prev * 256:prev * 256 + 1]
                            V("tensor_scalar", [hd, pk], [small], out=icol, in0=pcol, scalar1=fcol, scalar2=None, op0=ALU.mult)
                            init = icol
                            rk = [small]
                        if d == 0:
                            sl_ = slice(sg * 256, (sg + 1) * 256)
                        else:
                            sl_ = slice(sg * 256 + 255, (sg * 256 - 1) if sg > 0 else None, -1)
                        V("tensor_tensor_scan", [av, uv] + rk, [hd], out=hd[:, sl_], data0=av[:, sl_], data1=uv[:, sl_], initial=init, op0=ALU.mult, op1=ALU.add)
                    fc = 255 if d == 0 else 0
                    A("copy", [hd], [fin_lru], out=fin_lru[:, ti, d, :], in_=v3(hd)[:, :, fc])
                V("tensor_tensor", [hacc, T[6]], [hacc], out=hacc[:], in0=hacc[:], in1=T[6][:], op=ALU.add)
                V("tensor_tensor", [hacc, gg], [('ymix', 6 + ti)], out=ymix[:, 6 + ti, :], in0=hacc[:], in1=gg[:], op=ALU.mult)
                for d in range(2):
                    dst = o_lru[:, l, d, ti * 128:(ti + 1) * 128].rearrange("s p -> p s")
                    P.dma("sp", R=[fin_lru], is_output=True, out=dst, in_=fin_lru[:, ti, d, :], allow_slow_non_contiguous=True)

        def out_proj(l):
            for half in range(2):
                wsl = load_w(w_out, l * 1024, 8, half * 512, 512)
                for q in range(4):
                    dt_ = half * 4 + q
                    for b in range(3):
                        c = 0 if b < 2 else 1
                        ps = dense_fm(wsl, q * 128, b, rhs=ymix, rkeys=ALLY)
                        bs = slice(b * 512, (b + 1) * 512)
                        V("scalar_tensor_tensor", [ps, modsb, ('x', b)], [('x', b)], out=x[:, dt_, bs], in0=ps[:], scalar=modsb[:, 16 + dt_, c:c + 1], in1=x[:, dt_, bs], op0=ALU.mult, op1=ALU.add)

        def ffn(l):
            fcw = pkv('fcw', L, NJ, 3); fcb = pkv('fcb', L, NJ)
            aff = ymix
            nxt = l + 1 if l + 1 < depth else None
            pending = None

            def u_part(wu_, co_, gc_, jj_):
                for b in range(3):
                    ps = dense_fm(wu_, co_, b)
                    bs = slice(b * 512, (b + 1) * 512)
                    V("tensor_tensor", [gc_, ps], [('ymix', jj_)], out=aff[:, jj_, bs], in0=gc_[:, bs], in1=ps[:], op=ALU.mult)

            for (j0, nj) in ((0, 6), (6, 6), (12, 5), (17, 5)):
                for jj in range(nj):
                    j = j0 + jj
                    if nxt is not None:
                        mod_issue(nxt, 2 * j)
                        mod_issue(nxt, 2 * j + 1)
                    if jj % 4 == 0:
                        ncl = min(4, nj - jj) * 128
                        wg = load_w(w_up, l * 1024, 8, j * 128, ncl)
                        wu = load_w(w_up, l * 1024, 8, DFF + j * 128, ncl)
                    co = (jj % 4) * 128
                    for b in range(3):
                        ps = dense_fm(wg, co, b)
                        evac_to_cpad(ps, b)
                    pad_fix(1, 1)
                    gc = T[jj % 2]
                    conv_from_cpad(v3(gc), [fcw[:, l, j, k:k + 1] for k in range(3)], fcb[:, l, j:j + 1], 3, [gc])
                    ACT(gc[:], gc[:], AF.Silu, [gc], [gc])
                    if pending is not None:
                        u_part(*pending)
                    pending = (wu, co, gc, jj)
                    if nxt is not None and j >= 1:
                        mod_mm(nxt, 2 * (j - 1))
                        mod_mm(nxt, 2 * (j - 1) + 1)
                u_part(*pending)
                pending = None
                for half in range(2):
                    wd = load_w(w_down, l * DFF + j0 * 128, nj, half * 512, 512)
                    for q in range(4):
                        dt_ = half * 4 + q
                        for b in range(3):
                            c = 0 if b < 2 else 1
                            ps = dense_fm(wd, q * 128, b, nk=nj, rhs=aff, rkeys=[('ymix', i) for i in range(nj)])
                            bs = slice(b * 512, (b + 1) * 512)
                            V("scalar_tensor_tensor", [ps, modsb, ('x', b)], [('x', b)], out=x[:, dt_, bs], in0=ps[:], scalar=modsb[:, 40 + dt_, c:c + 1], in1=x[:, dt_, bs], op0=ALU.mult, op1=ALU.add)

        def mod_tail(l):
            mod_mm(l, 42)
            mod_mm(l, 43)
            for sl in range(44, 48):
                mod_issue(l, sl)
            for sl in range(44, 48):
                mod_mm(l, sl)

        def final_out():
            fnw = pkv('fnw')
            for nb in range(6):
                b = nb // 2
                tsl = slice(nb * 256, (nb + 1) * 256)
                rs_ap, rs_k = rstd_block(nb)
                for kt in range(KT):
                    V("scalar_tensor_tensor", [('x', b), rs_k, pk], [T[kt // 4]], out=T[kt // 4][:, (kt % 4) * 256:(kt % 4) * 256 + 256], in0=x[:, kt, tsl], scalar=fnw[:, kt:kt + 1], in1=rs_ap, op0=ALU.mult, op1=ALU.mult)
                for cc in range(2):
                    ot = T[2 + cc]
                    for half in range(2):
                        ps = PB()
                        for q in range(4):
                            kt = half * 4 + q
                            src = T[kt // 4][:, (kt % 4) * 256 + cc * 128:(kt % 4) * 256 + cc * 128 + 128]
                            P.op("pe", "transpose", [T[kt // 4], cst], [ps], out=ps[:, q * 128:(q + 1) * 128], in_=src, identity=csv('ident'))
                        A("copy", [ps], [ot], out=ot[:, half * 512:(half + 1) * 512], in_=ps[:])
                    r0 = nb * 256 + cc * 128
                    P.dma("sp", R=[ot], is_output=True, out=y_out[r0:r0 + 128, :], in_=ot[:, 0:1024])

        stt = sb("stt", [128, 8, 96])
        Sm = sb("Sm", [128, 2, 2, 64])
        Sinit = sb("Sinit", [128, 2, 2, 64])
        kwb = sb("kwb", [128, 2, 4, 64], BF16)
        cdH = sb("cdH", [128, 2, NCH, 2])
        blk64b = sb("blk64b", [128, 128], BF16)
        V("tensor_copy", [cst], [blk64b], out=blk64b[:], in_=csv('blk64'))

        def st4(i):
            return stt[:, i, :].rearrange("p (d c h) -> p d c h", d=2, c=NCH, h=4)

        def bfv(t, *dims):
            v = t[:].bitcast(BF16)
            if len(dims) == 2:
                return v.rearrange("p (a b) -> p a b", a=dims[0], b=dims[1])
            if len(dims) == 3:
                return v.rearrange("p (a b c) -> p a b c", a=dims[0], b=dims[1], c=dims[2])
            if len(dims) == 4:
                return v.rearrange("p (a b c d) -> p a b c d", a=dims[0], b=dims[1], c=dims[2], d=dims[3])
            return v

        def attn_mixer(l, kind):
            ssd = (kind == 'ssd')
            v_tok = bfv(T[0], NCH, 4, 64)
            k_tok = bfv(T[1], NCH, 4, 64) if not ssd else bfv(T[1], NCH, 4, 64)[:, :, 0:2, :]
            Sent = bfv(T[2], NCH, 2, 2, 64)
            rhsla = T[3][:, 0:512].rearrange("p (h l) -> p h l", h=4, l=128)
            tmpD = T[3][:, 512:1024].rearrange("p (h l) -> p h l", h=4, l=128)
            eCR = T[3][:, 1024:1536].rearrange("p (h l) -> p h l", h=4, l=128)
            t4b = T[4][:].bitcast(BF16)
            Dm = t4b[:, 0:1024].rearrange("p (d h l) -> p d h l", d=2, h=4, l=128)
            Dsum = t4b[:, 1024:1536].rearrange("p (h l) -> p h l", h=4, l=128)
            qdz = t4b[:, 1536:2560].rearrange("p (d h l) -> p d h l", d=2, h=4, l=128)
            Pm = t4b[:, 2560:3072].rearrange("p (h l) -> p h l", h=4, l=128)
            la = st4(0); lndt = st4(1); Cp = st4(2); eTot = st4(3); tailw = st4(4); dtv = st4(5)
            finS = cpad[:, :, 2:258].rearrange("p s (d t q) -> p s d t q", d=2, t=2, q=64)
            o_st = o_ssd if ssd else o_ret
            init_d = ssdinit if ssd else retinit
            P.dma("sp", W=[Sinit], out=Sinit[:].rearrange("p d t q -> p (d t q)"), in_=init_d[l * 128:(l + 1) * 128, :])

            def hmap(hh_):
                return _head_map(kind, hh_)

            if ssd:
                sz = bfv(T[7], 2, NT)
                xs = bfv(T[6], 2, NT)
                BC = bfv(T[5], 2, NT)
                kf = BC[:, 0:1, :]
                qf = BC[:, 1:2, :]
                scw = pkv('scw', L, 4, 4); scb = pkv('scb', L, 4)
                w1 = load_w(w_in, l * 1024, 8, 0, 512)
                w2 = load_w(w_in, l * 1024, 8, 512, 264)
                for ti in range(2):
                    for b in range(3):
                        ps = dense_fm(w1, ti * 128, b)
                        ACT(sz[:, ti, b * 512:(b + 1) * 512], ps[:], AF.Silu, [ps], [T[7]])
                for ci in range(4):
                    wsl, co = (w1, 256 + ci * 128) if ci < 2 else (w2, (ci - 2) * 128)
                    for b in range(3):
                        ps = dense_fm(wsl, co, b)
                        evac_to_cpad(ps, b)
                    pad_fix(2, 1)
                    conv_from_cpad(v3(T[3]), [scw[:, l, ci, j:j + 1] for j in range(4)], scb[:, l, ci:ci + 1], 4, [T[3]])
                    dst = xs[:, ci, :] if ci < 2 else BC[:, ci - 2, :]
                    ACT(dst, T[3][:], AF.Silu, [T[3]], [T[6] if ci < 2 else T[5]])
                pdt = PB()
                for c in range(NCH):
                    mm_acc(pdt[:, c * 8:(c + 1) * 8], [(h[:, kt, c * 128:(c + 1) * 128], w2[:, kt, 256:264]) for kt in range(KT)], R=[w2, ('h', c // 4)], W=[pdt])
                pdt4 = pdt[:, 0:96].rearrange("p (c d h) -> p d c h", c=NCH, d=2, h=4)
                sdtb = pkv('sdtb', L, 2, 4); salog = pkv('salog', L, 2, 4)
                V("tensor_tensor", [pdt, pk], [stt], out=dtv, in0=pdt4, in1=sdtb[:, l].unsqueeze(2).to_broadcast([128, 2, NCH, 4]), op=ALU.add)
                ACT(stt[:, 5, :], stt[:, 5, :], AF.Exp, [stt], [stt])
                ACT(stt[:, 5, :], stt[:, 5, :], AF.Ln, [stt], [stt], bias=1.0, scale=1.0)
                ACT(stt[:, 1, :], stt[:, 5, :], AF.Ln, [stt], [stt])
                an = small[:, 24:32]
                ACT(an, salog[:, l].rearrange("p a b -> p (a b)"), AF.Exp, [pk], [small])
                V("tensor_scalar", [small], [small], out=an, in0=an, scalar1=-1.0, scalar2=None, op0=ALU.mult)
                V("tensor_tensor", [stt, small], [stt], out=la, in0=dtv, in1=an.rearrange("p (d h) -> p d h", d=2, h=4).unsqueeze(2).to_broadcast([128, 2, NCH, 4]), op=ALU.mult)
            else:
                sg = bfv(T[7], 2, NT)
                qf = bfv(T[5], 2, NT)
                kf = bfv(T[6], 2, NT)
                wA = load_w(w_in, l * 1024, 8, 776, 512)
                wB = load_w(w_in, l * 1024, 8, 776 + 512, 512)
                cpf = cpad[:].rearrange("p s c -> p (s c)")
                cosb = cpf[:, 0:512].bitcast(BF16)
                sinb = cpf[:, 512:1024].bitcast(BF16)
                P.dma("pool", W=[cpad], out=cosb, in_=cos_d)
                P.dma("pool", W=[cpad], out=sinb, in_=sin_d)
                rq = T[3][:, 0:256].bitcast(BF16)
                t1 = T[3][:, 512:1024]
                t2 = T[3][:, 1024:1536]
                for qi in range(4):
                    dstt, dkey = (qf, T[5]) if qi < 2 else (kf, T[6])
                    sc = 1.0 if qi < 2 else 0.125
                    for b in range(3):
                        ps = dense_fm(wA, qi * 128, b)
                        bs = slice(b * 512, (b + 1) * 512)
                        if b == 2:
                            ACT(dstt[:, qi % 2, bs], ps[:], AF.Identity, [ps], [dkey], scale=sc)
                        else:
                            A("copy", [ps], [T[3]], out=rq, in_=ps[:])
                            pp = PB()
                            MM(pp[:], rpermb[:], rq, [rpermb, T[3]], [pp])
                            V("tensor_tensor", [ps, cpad, T[3]], [T[3]], out=t1, in0=ps[:], in1=cosb[:, bs], op=ALU.mult)
                            V("tensor_tensor", [pp, cpad, T[3]], [T[3]], out=t2, in0=pp[:], in1=sinb[:, bs], op=ALU.mult)
                            V("tensor_tensor", [T[3]], [T[3]], out=t1, in0=t1, in1=t2, op=ALU.add)
                            ACT(dstt[:, qi % 2, bs], t1, AF.Identity, [T[3]], [dkey], scale=sc)
                V("memset", [cpad], [cpad], ap=cpad[:], constant=0.0)
                for c in range(NCH):
                    pv = PB()
                    mm_acc(pv[:, 0:256], [(h[:, kt, c * 128:(c + 1) * 128], wB[:, kt, 0:256]) for kt in range(KT)], R=[wB, ('h', c // 4)], W=[pv])
                    A("copy", [pv], [T[0]], out=v_tok[:, c].rearrange("p a b -> p (a b)"), in_=pv[:, 0:256])
                for ti in range(2):
                    for b in range(3):
                        ps = dense_fm(wB, 256 + ti * 128, b)
                        ACT(sg[:, ti, b * 512:(b + 1) * 512], ps[:], AF.Silu, [ps], [T[7]])
                V("tensor_copy", [cst], [stt], out=stt[:, 0, :], in_=csv('retla'))
                V("memset", [stt], [stt], ap=stt[:, 1, :], constant=0.0)

            for c in range(NCH):
                cs_ = slice(c * 128, (c + 1) * 128)
                if ssd:
                    for ti in range(2):
                        P.op("pe", "transpose", [T[6], identb], [pbb], out=pbb[:, ti * 128:(ti + 1) * 128], in_=xs[:, ti, cs_], identity=identb[:])
                    P.op("pe", "transpose", [T[5], identb], [pbb], out=pbb[:, 256:384], in_=kf[:, 0, cs_], identity=identb[:])
                    A("copy", [pbb], [T[0]], out=v_tok[:, c].rearrange("p a b -> p (a b)"), in_=pbb[:, 0:256])
                    A("copy", [pbb], [T[1]], out=k_tok[:, c].rearrange("p a b -> p (a b)"), in_=pbb[:, 256:384])
                else:
                    for ti in range(2):
                        P.op("pe", "transpose", [T[6], identb], [pbb], out=pbb[:, ti * 128:(ti + 1) * 128], in_=kf[:, ti, cs_], identity=identb[:])
                    A("copy", [pbb], [T[1]], out=k_tok[:, c].rearrange("p a b -> p (a b)"), in_=pbb[:, 0:256])

            pc1 = PB()
            MM(pc1[:, 0:48], csv('maskU'), stt[:, 0, 0:48], [cst, stt], [pc1])
            MM(pc1[:, 48:96], csv('maskL'), stt[:, 0, 48:96], [cst, stt], [pc1])
            V("tensor_tensor", [pc1, stt], [stt], out=stt[:, 2, :], in0=pc1[:, 0:96], in1=stt[:, 1, :], op=ALU.subtract)
            pc2 = PB()
            MM(pc2[:, 0:96], csv('ones'), stt[:, 0, :], [cst, stt], [pc2])
            ACT(stt[:, 3, :], pc2[:, 0:96], AF.Exp, [pc2], [stt])
            V("tensor_tensor", [pc2, stt], [stt], out=stt[:, 4, :], in0=pc2[:, 0:96], in1=stt[:, 2, :], op=ALU.subtract)
            ACT(stt[:, 4, :], stt[:, 4, :], AF.Exp, [stt], [stt])
            for base in (0, 64):
                for slot in range(2):
                    hh_ = (base // 64) * 2 + slot if ssd else slot * 2 + base // 64
                    V("tensor_copy", [stt], [cdH], out=cdH[base:base + 64, :, :, slot], in_=eTot[base:base + 64, :, :, hh_])

            csS = [T[3][:].rearrange("p (c t q) -> p c t q", c=NCH, t=2, q=64), T[4][:].rearrange("p (c t q) -> p c t q", c=NCH, t=2, q=64)]
            csK = [T[3], T[4]]
            for c in range(NCH):
                for d in range(2):
                    if ssd:
                        V("tensor_tensor", [T[1], stt], [('kwb', d)], out=kwb[:, d].rearrange("p (g e) n -> p g e n", g=2, e=2),
                          in0=k_tok[:, c].unsqueeze(2).to_broadcast([128, 2, 2, 64]),
                          in1=tailw[:, d, c, :].rearrange("p (g e) -> p g e", g=2, e=2).unsqueeze(3).to_broadcast([128, 2, 2, 64]), op=ALU.mult)
                    else:
                        V("tensor_tensor", [T[1], stt], [('kwb', d)], out=kwb[:, d], in0=k_tok[:, c], in1=tailw[:, d, c, :].unsqueeze(2).to_broadcast([128, 4, 64]), op=ALU.mult)
                    pcs = PB()
                    for hh_ in range(4):
                        base, slot = hmap(hh_)
                        MM(pcs[base:base + 64, slot * 64:(slot + 1) * 64], kwb[:, d, hh_, :], v_tok[:, c, hh_, :], [('kwb', d), T[0]], [pcs])
                    A("copy", [pcs], [csK[d]], out=csS[d][:, c], in_=pcs[:, 0:128].rearrange("p (t q) -> p t q", t=2, q=64))
            V("tensor_copy", [Sinit], [('Sm', 0)], out=Sm[:, 0], in_=Sinit[:, 0])
            V("memset", [], [('Sm', 1)], ap=Sm[:, 1], constant=0.0)
            for i in range(NCH):
                for d in range(2):
                    c = i if d == 0 else NCH - 1 - i
                    sk = ('Sm', d)
                    V("tensor_copy", [sk], [T[2]], out=Sent[:, c, d], in_=Sm[:, d])
                    for slot in range(2):
                        V("scalar_tensor_tensor", [sk, cdH, csK[d]], [sk], out=Sm[:, d, slot], in0=Sm[:, d, slot], scalar=cdH[:, d, c, slot:slot + 1], in1=csS[d][:, c, slot], op0=ALU.mult, op1=ALU.add)
                    seg_end = (c % 2 == 1) if d == 0 else (c % 2 == 0)
                    if seg_end:
                        sg_ = c // 2
                        V("tensor_copy", [sk], [cpad], out=finS[:, sg_, d], in_=Sm[:, d])
                        if d == 0:
                            if sg_ < 3:
                                V("tensor_scalar", [sk, pk], [sk], out=Sm[:, d], in0=Sm[:, d], scalar1=fcol, scalar2=None, op0=ALU.mult)
                            elif sg_ < 5:
                                V("memset", [sk], [sk], ap=Sm[:, d], constant=0.0)
                        else:
                            if sg_ == 5:
                                V("memset", [sk], [sk], ap=Sm[:, d], constant=0.0)
                            elif sg_ == 4:
                                V("tensor_copy", [Sinit, sk], [sk], out=Sm[:, 1], in_=Sinit[:, 1])
                            elif sg_ > 0:
                                V("tensor_scalar", [sk, pk], [sk], out=Sm[:, d], in0=Sm[:, d], scalar1=fcol, scalar2=None, op0=ALU.mult)
            V("memset", [], [T[4]], ap=T[4][:], constant=0.0)
            for hh_ in range(4):
                base, slot = hmap(hh_)
                for d in range(2):
                    dst = o_st[:, l, d, hh_].rearrange("s n q -> n s q")
                    P.dma("sp", R=[cpad], is_output=True, out=dst, in_=finS[base:base + 64, :, d, slot, :])

            ybase = 0 if ssd else 2
            V("memset", [], [mws[0]], ap=mws[0][:], constant=0.0)
            qdz_b = [qdz, mws[0][:].rearrange("p a b -> p (a b)").rearrange("p (d h l) -> p d h l", d=2, h=4, l=128)]
            qdz_k = [('T4', 'q'), mws[0]]
            Dsum_b = [Dsum, mws[1][:, 0:4, :]]
            Dsum_k = [('T4', 'Ds'), mws[1]]

            rl = [T[3][:, 0:512], T[3][:, 512:1024]]
            rlk = [('T3', 0), ('T3', 1)]
            ec = [T[3][:, 1024:1536], mws[2][:].rearrange("p a b -> p (a b)").bitcast(F32)]
            eck_ = [('T3', 2), mws[2]]
            psc_b = [pb[0], pb[1]]
            pcb = [pb[2], pb[3]]
            pyb = pb[4]

            def h3(ap):
                return ap.rearrange("p (h l) -> p h l", h=4, l=128)

            def stA(c):
                cs_ = slice(c * 128, (c + 1) * 128)
                psc = psc_b[c % 2]
                if ssd:
                    for g in range(2):
                        MM(psc[:, g * 128:(g + 1) * 128], kf[g * 64:(g + 1) * 64, 0, cs_], qf[g * 64:(g + 1) * 64, 0, cs_], [T[5]], [psc])
                else:
                    for hh_ in range(4):
                        base, slot = hmap(hh_)
                        MM(psc[:, hh_ * 128:(hh_ + 1) * 128], kf[base:base + 64, slot, cs_], qf[base:base + 64, slot, cs_], [T[5], T[6]], [psc])
                for d in range(2):
                    msk = csv('maskU') if d == 0 else csv('maskL')
                    V("tensor_tensor", [cst, stt], [rlk[d]], out=h3(rl[d]), in0=msk.unsqueeze(1).to_broadcast([128, 4, 128]), in1=la[:, d, c, :].unsqueeze(2).to_broadcast([128, 4, 128]), op=ALU.mult)
                for d in range(2):
                    MM(pcb[d][:], csv('ones'), rl[d], [cst, rlk[d]], [pcb[d]])

            def stC(c):
                cs_ = slice(c * 128, (c + 1) * 128)
                bi = c % 2
                for d in range(2):
                    neg = csv('negF') if d == 0 else csv('negB')
                    pc3 = h3(pcb[d][:])
                    V("tensor_tensor", [pcb[d], stt], [rlk[d]], out=h3(rl[d]), in0=pc3, in1=Cp[:, d, c, :].unsqueeze(2).to_broadcast([128, 4, 128]), op=ALU.subtract)
                    V("tensor_tensor", [rlk[d], cst], [rlk[d]], out=h3(rl[d]), in0=h3(rl[d]), in1=neg.unsqueeze(1).to_broadcast([128, 4, 128]), op=ALU.add)
                    ACT(Dm[:, d], h3(rl[d]), AF.Exp, [rlk[d]], [('T4', 'D', d)])
                    ACT(h3(ec[d]), pc3, AF.Exp, [pcb[d]], [eck_[d]])
                for d in range(2):
                    eCR_ = h3(ec[d])
                    for base in (0, 64):
                        if ssd:
                            h0 = (base // 64) * 2
                            V("tensor_tensor", [eck_[d], T[5]], [qdz_k[bi]], out=qdz_b[bi][base:base + 64, d, h0:h0 + 2, :],
                              in0=qf[base:base + 64, 0, cs_].unsqueeze(1).to_broadcast([64, 2, 128]), in1=eCR_[base:base + 64, h0:h0 + 2, :], op=ALU.mult)
                        else:
                            o_ = base // 64
                            V("tensor_tensor", [eck_[d], T[5]], [qdz_k[bi]], out=qdz_b[bi][base:base + 64, d, o_::2, :],
                              in0=qf[base:base + 64, :, cs_], in1=eCR_[base:base + 64, o_::2, :], op=ALU.mult)
                V("tensor_tensor", [('T4', 'D', 0), ('T4', 'D', 1)], [Dsum_k[bi]], out=Dsum_b[bi], in0=Dm[:, 0], in1=Dm[:, 1], op=ALU.add)

            def stB(c):
                bi = c % 2
                psc = psc_b[bi]
                if ssd:
                    V("tensor_tensor", [psc, Dsum_k[bi]], [('T4', 'P')], out=Pm.rearrange("p (g e) l -> p g e l", g=2, e=2),
                      in0=psc[:, 0:256].rearrange("p (g l) -> p g l", g=2, l=128).unsqueeze(2).to_broadcast([128, 2, 2, 128]),
                      in1=Dsum_b[bi].rearrange("p (g e) l -> p g e l", g=2, e=2), op=ALU.mult)
                else:
                    V("tensor_tensor", [psc, Dsum_k[bi]], [('T4', 'P')], out=Pm, in0=h3(psc[:]), in1=Dsum_b[bi], op=ALU.mult)
                py = pyb
                for hh_ in range(4):
                    base, slot = hmap(hh_)
                    oap = py[(hh_ % 2) * 64:(hh_ % 2) * 64 + 64, (hh_ // 2) * 128:(hh_ // 2) * 128 + 128]
                    MM(oap, v_tok[:, c, hh_, :], Pm[:, hh_, :], [T[0], ('T4', 'P')], [py], start=True, stop=False)
                    MM(oap, Sent[:, c, 0, slot, :], qdz_b[bi][:, 0, hh_, :], [T[2], qdz_k[bi]], [], start=False, stop=False, noself=True)
                    MM(oap, Sent[:, c, 1, slot, :], qdz_b[bi][:, 1, hh_, :], [T[2], qdz_k[bi]], [py], start=False, stop=True, noself=True)

            def stD(c):
                cs_ = slice(c * 128, (c + 1) * 128)
                py = pyb
                if ssd:
                    sdcol = pkv('sdcol', L, 2)
                    for ti in range(2):
                        V("scalar_tensor_tensor", [T[6], pk, py], [('ymix', ti)], out=ymix[:, ti, cs_], in0=xs[:, ti, cs_], scalar=sdcol[:, l, ti:ti + 1], in1=py[:, ti * 128:(ti + 1) * 128], op0=ALU.mult, op1=ALU.add)
                else:
                    A("copy", [py], [('ymix', 2), ('ymix', 3)], out=ymix[:, 2:4, cs_], in_=py[:, 0:256].rearrange("p (t l) -> p t l", t=2, l=128))

            for c in range(NCH + 1):
                if c < NCH:
                    stA(c)
                if c >= 1:
                    stB(c - 1)
                if c < NCH:
                    stC(c)
                if c >= 1:
                    stD(c - 1)

            if ssd:
                snw = pkv('snw', L, 2)
                for ti in range(2):
                    V("tensor_tensor", [('ymix', ti), T[7]], [('ymix', ti)], out=ymix[:, ti, :], in0=ymix[:, ti, :], in1=sz[:, ti, :], op=ALU.mult)
                sq2 = T[3][:, 0:512].bitcast(BF16).rearrange("p (t l) -> p t l", t=2, l=512)
                rs2 = T[3][:, 512:1024]
                for b in range(3):
                    bs = slice(b * 512, (b + 1) * 512)
                    ACT(sq2, ymix[:, 0:2, bs], AF.Square, [('ymix', 0), ('ymix', 1)], [T[3]])
                    ss = PB()
                    mm_acc(ss[:], [(onesb[:], sq2[:, ti, :]) for ti in range(2)], R=[T[3], onesb], W=[ss])
                    ACT(rs2, ss[:], AF.Ln, [ss, small], [T[3]], bias=small[:, 0:1], scale=4.0)
                    ACT(rs2, rs2, AF.Exp, [T[3]], [T[3]], scale=-0.5)
                    for ti in range(2):
                        V("scalar_tensor_tensor", [('ymix', ti), pk, T[3]], [('ymix', ti)], out=ymix[:, ti, bs], in0=ymix[:, ti, bs], scalar=snw[:, l, ti:ti + 1], in1=rs2, op0=ALU.mult, op1=ALU.mult)
            else:
                rgn = pkv('rgn', L, 2)
                yc = T[3][:, 0:512]
                sqr = T[3][:, 512:768].bitcast(BF16)
                rs2 = T[3][:, 1024:1536]
                for ti in range(2):
                    for b in range(3):
                        bs = slice(b * 512, (b + 1) * 512)
                        pm = PB()
                        MM(pm[:], blk64b[:], ymix[:, 2 + ti, bs], [blk64b, ('ymix', 2 + ti)], [pm])
                        V("tensor_tensor", [('ymix', 2 + ti), pm], [T[3]], out=yc, in0=ymix[:, 2 + ti, bs], in1=pm[:], op=ALU.subtract)
                        ACT(sqr, yc, AF.Square, [T[3]], [T[3]])
                        pv2 = PB()
                        MM(pv2[:], blk64b[:], sqr, [blk64b, T[3]], [pv2])
                        ACT(rs2, pv2[:], AF.Ln, [pv2, small], [T[3]], bias=small[:, 0:1], scale=1.0)
                        ACT(rs2, rs2, AF.Exp, [T[3]], [T[3]], scale=-0.5)
                        V("tensor_tensor", [T[3]], [T[3]], out=yc, in0=yc, in1=rs2, op=ALU.mult)
                        V("scalar_tensor_tensor", [T[3], pk, T[7]], [('ymix', 2 + ti)], out=ymix[:, 2 + ti, bs], in0=yc, scalar=rgn[:, l, ti:ti + 1], in1=sg[:, ti, bs], op0=ALU.mult, op1=ALU.mult)

        fin_s5 = sb("fin_s5", [128, NSEG, 2, 8, 2])
        s5p = sb("s5p", [128, 2, 8, 12])

        def s5_mixer(l):
            TWO_PI = 2.0 * math.pi
            ub = bfv(T[0], 2, NT)
            wre = T[1]; wim = T[2]
            E1r = T[3][:, 0:256]; E1i = T[3][:, 256:512]; E2r = T[3][:, 512:768]; E2i = T[3][:, 768:1024]
            ang = T[3][:, 1024:1280]; kbuf = T[3][:, 1280:1536].bitcast(I32)
            t1 = T[4][:, 0:512]; t2 = T[4][:, 512:1024]
            xb = bfv(T[5], 2, NT)
            acc = [T[6], T[7]]
            lre = pkv('lre', L, 2, 8); lim = pkv('lim', L, 2, 8); lstep = pkv('lstep', L, 2, 8)
            s5d = pkv('s5d', L, 2); s5init = pkv('s5init', L, 2, 8, 2)
            wu_ = load_w(w_in, l * 1024, 8, 1800, 256)
            wbc = ws[wsi[0] % 3]
            wsi[0] += 1
            BT = wbc[:, 0:4, :].rearrange("p a b -> p (a b)").rearrange("p (r j o) -> p r j o", r=2, j=8, o=128)
            CT = wbc[:, 4:8, :].rearrange("p a b -> p (a b)").rearrange("p (r j o) -> p r j o", r=2, j=8, o=128)
            P.dma("pool", W=[wbc], out=wbc[:, 0:4, :].rearrange("p a b -> p (a b)"), in_=s5bt[l * 128:(l + 1) * 128, :])
            P.dma("pool", W=[wbc], out=wbc[:, 4:8, :].rearrange("p a b -> p (a b)"), in_=s5ct[l * 128:(l + 1) * 128, :])
            V("tensor_scalar", [wbc], [wbc], out=CT[:, 1], in0=CT[:, 1], scalar1=-1.0, scalar2=None, op0=ALU.mult)
            for ti in range(2):
                for b in range(3):
                    ps = dense_fm(wu_, ti * 128, b)
                    A("copy", [ps], [T[0]], out=ub[:, ti, b * 512:(b + 1) * 512], in_=ps[:])
            def sp(k):
                return s5p[:, :, :, k]
            ACT(sp(0), lstep[:, l], AF.Exp, [pk], [s5p])
            V("tensor_tensor", [s5p, pk], [s5p], out=sp(1), in0=lre[:, l], in1=sp(0), op=ALU.mult)
            ACT(sp(1), sp(1), AF.Exp, [s5p], [s5p])
            V("tensor_tensor", [s5p, pk], [s5p], out=sp(2), in0=lim[:, l], in1=sp(0), op=ALU.mult)
            def sincos(dst_sin, dst_cos, src, scr_f, scr_i, keyR, keyW):
                for (dst, off) in ((dst_sin, 0.0), (dst_cos, math.pi / 2.0)):
                    V("tensor_scalar", keyR, keyW, out=scr_i, in0=src, scalar1=off, scalar2=1.0 / TWO_PI, op0=ALU.add, op1=ALU.mult)
                    V("tensor_copy", keyW, keyW, out=scr_f, in_=scr_i)
                    V("tensor_scalar", keyW, keyW, out=scr_f, in0=scr_f, scalar1=-TWO_PI, scalar2=off, op0=ALU.mult, op1=ALU.add)
                    V("tensor_tensor", keyR + keyW, keyW, out=scr_f, in0=scr_f, in1=src, op=ALU.add)
                    V("tensor_scalar", keyW, keyW, out=scr_f, in0=scr_f, scalar1=3.141592, scalar2=-3.141592, op0=ALU.min, op1=ALU.max)
                    ACT(dst, scr_f, AF.Sin, keyW, keyW)
            pscr_i = small[:, 64:80].bitcast(I32).rearrange("p (d j) -> p d j", d=2, j=8)
            pscr_f = small[:, 80:96].rearrange("p (d j) -> p d j", d=2, j=8)
            sincos(sp(4), sp(3), sp(2), pscr_f, pscr_i, [s5p, small], [s5p, small])
            V("tensor_tensor", [s5p], [s5p], out=sp(3), in0=sp(3), in1=sp(1), op=ALU.mult)
            V("tensor_tensor", [s5p], [s5p], out=sp(4), in0=sp(4), in1=sp(1), op=ALU.mult)
            V("tensor_tensor", [s5p, pk], [s5p], out=sp(8), in0=lre[:, l], in1=lre[:, l], op=ALU.mult)
            V("tensor_tensor", [s5p, pk], [s5p], out=sp(9), in0=lim[:, l], in1=lim[:, l], op=ALU.mult)
            V("tensor_tensor", [s5p], [s5p], out=sp(8), in0=sp(8), in1=sp(9), op=ALU.add)
            V("reciprocal", [s5p], [s5p], out=sp(8), in_=sp(8))
            V("tensor_scalar", [s5p], [s5p], out=sp(9), in0=sp(3), scalar1=-1.0, scalar2=None, op0=ALU.add)
            V("tensor_tensor", [s5p, pk], [s5p], out=sp(10), in0=sp(9), in1=lre[:, l], op=ALU.mult)
            V("tensor_tensor", [s5p, pk], [s5p], out=sp(11), in0=sp(4), in1=lim[:, l], op=ALU.mult)
            V("tensor_tensor", [s5p], [s5p], out=sp(5), in0=sp(10), in1=sp(11), op=ALU.add)
            V("tensor_tensor", [s5p], [s5p], out=sp(5), in0=sp(5), in1=sp(8), op=ALU.mult)
            V("tensor_tensor", [s5p, pk], [s5p], out=sp(10), in0=sp(4), in1=lre[:, l], op=ALU.mult)
            V("tensor_tensor", [s5p, pk], [s5p], out=sp(11), in0=sp(9), in1=lim[:, l], op=ALU.mult)
            V("tensor_tensor", [s5p], [s5p], out=sp(6), in0=sp(10), in1=sp(11), op=ALU.subtract)
            V("tensor_tensor", [s5p], [s5p], out=sp(6), in0=sp(6), in1=sp(8), op=ALU.mult)
            V("tensor_scalar", [s5p], [s5p], out=sp(7), in0=sp(5), scalar1=-1.0, scalar2=None, op0=ALU.mult)
            wsets = [(T[1], T[2]), (T[6], T[7])]
            E1r = T[3][:, 0:256]; E1i = T[3][:, 256:512]
            E2sets = [(T[3][:, 512:768], T[3][:, 768:1024], ('T3', 1), T[3][:, 512:1024]),
                      (T[3][:, 1024:1280], T[3][:, 1280:1536], ('T3', 2), T[3][:, 1024:1536])]
            rt1 = mws[0][:].rearrange("p a b -> p (a b)").bitcast(F32)
            rt2 = mws[1][:].rearrange("p a b -> p (a b)").bitcast(F32)
            kb_i = mws[2][:].rearrange("p a b -> p (a b)").bitcast(I32)
            kf_ = mws[3][:].rearrange("p a b -> p (a b)").bitcast(F32)
            angs2 = stt[:, 0:6, :].rearrange("p a b -> p (a b)")[:, 0:512]
            pacc = [pb[3], pb[4], pb[5]]
            pb_rot[0] = [pb[0], pb[1], pb[2]]
            items = []
            for ot in range(2):
                for d in range(2):
                    for j in range(4 * ot, 4 * ot + 4):
                        items.append((ot, d, j, len(items)))

            def ctx(it):
                ot, d, j, idx = it
                wre, wim = wsets[idx % 2]
                E2r, E2i, e2k, E2both = E2sets[idx % 2]
                ec0 = 40 + 4 * (idx % 2)
                return dict(ot=ot, d=d, j=j, idx=idx, wre=wre, wim=wim, E2r=E2r, E2i=E2i, e2k=e2k, E2both=E2both,
                            e2c=small[:, ec0:ec0 + 4], eck=('small', 'e2c', idx % 2), ecol=255 if d == 0 else 0,
                            ramp=csv('rampf') if d == 0 else csv('rampb'))

            def pcol_(c, k):
                return s5p[:, c['d'], c['j'], k:k + 1]

            def tables(c):
                ramp, E2r, E2i, e2k, e2c, eck, ecol = c['ramp'], c['E2r'], c['E2i'], c['e2k'], c['e2c'], c['eck'], c['ecol']
                ACT(angs2[:, 256:512], ramp, AF.Identity, [cst, s5p], [stt], scale=pcol_(c, 2))
                ACT(angs2[:, 0:256], ramp, AF.Identity, [cst, s5p], [stt], scale=pcol_(c, 2), bias=math.pi / 2.0)
                V("tensor_scalar", [stt], [mws[2]], out=kb_i, in0=angs2, scalar1=1.0 / TWO_PI, scalar2=None, op0=ALU.mult)
                V("tensor_copy", [mws[2]], [mws[3]], out=kf_, in_=kb_i)
                V("scalar_tensor_tensor", [mws[3], stt], [mws[3]], out=kf_, in0=kf_, scalar=-TWO_PI, in1=angs2, op0=ALU.mult, op1=ALU.add)
                V("tensor_scalar", [mws[3]], [mws[3]], out=kf_, in0=kf_, scalar1=3.141592, scalar2=-3.141592, op0=ALU.min, op1=ALU.max)
                ACT(c['E2both'], kf_, AF.Sin, [mws[3]], [e2k])
                ACT(e2c[:, 0:1], E2r[:, ecol:ecol + 1], AF.Identity, [e2k, pk], [eck], scale=fcol)
                ACT(e2c[:, 1:2], E2i[:, ecol:ecol + 1], AF.Identity, [e2k, pk], [eck], scale=fcol)
                ACT(e2c[:, 2:3], e2c[:, 1:2], AF.Identity, [eck], [eck], scale=-1.0)
                ACT(e2c[:, 3:4], E2i[:, ecol:ecol + 1], AF.Identity, [e2k], [eck], scale=-1.0)
                ACT(E1r, E2r, AF.Identity, [e2k, s5p], [('T3', 0)], scale=pcol_(c, 5))
                ACT(E1i, E2r, AF.Identity, [e2k, s5p], [('T3', 0)], scale=pcol_(c, 6))

            def tables_b(c):
                E2i, e2k = c['E2i'], c['e2k']
                V("scalar_tensor_tensor", [e2k, ('T3', 0), s5p], [('T3', 0)], out=E1r, in0=E2i, scalar=pcol_(c, 6), in1=E1r, op0=ALU.mult, op1=ALU.add)
                V("scalar_tensor_tensor", [e2k, ('T3', 0), s5p], [('T3', 0)], out=E1i, in0=E2i, scalar=pcol_(c, 7), in1=E1i, op0=ALU.mult, op1=ALU.add)

            def rotate(c):
                j, wre, wim = c['j'], c['wre'], c['wim']
                kt_u = j // 4
                E1r3 = E1r.unsqueeze(1).to_broadcast([128, 2, 256]); E1i3 = E1i.unsqueeze(1).to_broadcast([128, 2, 256])
                for b_ in range(3):
                    bs = slice(b_ * 512, (b_ + 1) * 512)
                    pr = PB()
                    MM(pr[:], BT[:, 0, j, :], ub[:, kt_u, bs], [wbc, T[0]], [pr])
                    pi_ = PB()
                    MM(pi_[:], BT[:, 1, j, :], ub[:, kt_u, bs], [wbc, T[0]], [pi_])
                    pr3 = pr[:].rearrange("p (s t) -> p s t", s=2, t=256); pi3 = pi_[:].rearrange("p (s t) -> p s t", s=2, t=256)
                    t13 = rt1.rearrange("p (s t) -> p s t", s=2, t=256); t23 = rt2.rearrange("p (s t) -> p s t", s=2, t=256)
                    wre3 = wre[:, bs].rearrange("p (s t) -> p s t", s=2, t=256); wim3 = wim[:, bs].rearrange("p (s t) -> p s t", s=2, t=256)
                    V("tensor_tensor", [pr, ('T3', 0)], [mws[0]], out=t13, in0=pr3, in1=E1r3, op=ALU.mult)
                    V("tensor_tensor", [pi_, ('T3', 0)], [mws[1]], out=t23, in0=pi3, in1=E1i3, op=ALU.mult)
                    V("tensor_tensor", [mws[0], mws[1]], [wre], out=wre3, in0=t13, in1=t23, op=ALU.subtract)
                    V("tensor_tensor", [pr, ('T3', 0), mws[0]], [mws[0]], out=t13, in0=pr3, in1=E1i3, op=ALU.mult)
                    V("tensor_tensor", [pi_, ('T3', 0), mws[1]], [mws[1]], out=t23, in0=pi3, in1=E1r3, op=ALU.mult)
                    V("tensor_tensor", [mws[0], mws[1]], [wim], out=wim3, in0=t13, in1=t23, op=ALU.add)

            def scans(c):
                d, j, wre, wim, e2c, eck, ecol, e2k, E2r, E2i = c['d'], c['j'], c['wre'], c['wim'], c['e2c'], c['eck'], c['ecol'], c['e2k'], c['E2r'], c['E2i']
                rho3 = pcol_(c, 1).to_broadcast([128, 256])
                order = list(range(NSEG)) if d == 0 else [3, 2, 1, 0, 5, 4]
                ic = small[:, 32:34]
                for sg in order:
                    first = (sg == 0 and d == 0) or (sg == 3 and d == 1)
                    if d == 0:
                        sl_ = slice(sg * 256, (sg + 1) * 256)
                    else:
                        sl_ = slice(sg * 256 + 255, (sg * 256 - 1) if sg > 0 else None, -1)
                    if sg >= 4:
                        ini = (0.0, 0.0); rk = []
                    elif first:
                        ini = (s5init[:, l, d, j, 0:1], s5init[:, l, d, j, 1:2]); rk = [pk]
                    else:
                        prev = sg - 1 if d == 0 else sg + 1
                        pcol = prev * 256 + ecol
                        V("tensor_scalar", [wre, eck], [('small', 'ic')], out=ic[:, 0:1], in0=wre[:, pcol:pcol + 1], scalar1=e2c[:, 0:1], scalar2=None, op0=ALU.mult)
                        V("scalar_tensor_tensor", [wim, eck, ('small', 'ic')], [('small', 'ic')], out=ic[:, 0:1], in0=wim[:, pcol:pcol + 1], scalar=e2c[:, 2:3], in1=ic[:, 0:1], op0=ALU.mult, op1=ALU.add)
                        V("tensor_scalar", [wre, eck], [('small', 'ic')], out=ic[:, 1:2], in0=wre[:, pcol:pcol + 1], scalar1=e2c[:, 1:2], scalar2=None, op0=ALU.mult)
                        V("scalar_tensor_tensor", [wim, eck, ('small', 'ic')], [('small', 'ic')], out=ic[:, 1:2], in0=wim[:, pcol:pcol + 1], scalar=e2c[:, 0:1], in1=ic[:, 1:2], op0=ALU.mult, op1=ALU.add)
                        ini = (ic[:, 0:1], ic[:, 1:2]); rk = [('small', 'ic')]
                    V("tensor_tensor_scan", [wre, s5p] + rk, [wre], out=wre[:, sl_], data0=rho3, data1=wre[:, sl_], initial=ini[0], op0=ALU.mult, op1=ALU.add)
                    V("tensor_tensor_scan", [wim, s5p] + rk, [wim], out=wim[:, sl_], data0=rho3, data1=wim[:, sl_], initial=ini[1], op0=ALU.mult, op1=ALU.add)
                f6 = small[:, 48:60]
                wre_e = v3(wre)[:, :, ecol]; wim_e = v3(wim)[:, :, ecol]
                V("tensor_scalar", [wre, e2k], [('small', 'f6')], out=f6[:, 0:6], in0=wre_e, scalar1=E2r[:, ecol:ecol + 1], scalar2=None, op0=ALU.mult)
                V("scalar_tensor_tensor", [wim, eck, ('small', 'f6')], [fin_s5], out=fin_s5[:, :, d, j, 0], in0=wim_e, scalar=e2c[:, 3:4], in1=f6[:, 0:6], op0=ALU.mult, op1=ALU.add)
                V("tensor_scalar", [wre, e2k], [('small', 'f6')], out=f6[:, 6:12], in0=wre_e, scalar1=E2i[:, ecol:ecol + 1], scalar2=None, op0=ALU.mult)
                V("scalar_tensor_tensor", [wim, e2k, ('small', 'f6')], [fin_s5], out=fin_s5[:, :, d, j, 1], in0=wim_e, scalar=E2r[:, ecol:ecol + 1], in1=f6[:, 6:12], op0=ALU.mult, op1=ALU.add)

            def unrot(c, eng, sp_, ta2, tb2, tkeys):
                wre, wim, e2k, E2r, E2i = c['wre'], c['wim'], c['e2k'], c['E2r'], c['E2i']
                sgs = slice(2 * sp_, 2 * sp_ + 2)
                E2r6 = E2r.unsqueeze(1).to_broadcast([128, 2, 256]); E2i6 = E2i.unsqueeze(1).to_broadcast([128, 2, 256])
                ta = ta2.rearrange("p (s t) -> p s t", s=2, t=256); tb = tb2.rearrange("p (s t) -> p s t", s=2, t=256)
                wr3 = v3(wre)[:, sgs, :]; wi3 = v3(wim)[:, sgs, :]
                xr3 = xb[:, 0, :].rearrange("p (s t) -> p s t", s=NSEG, t=256)[:, sgs, :]
                xi3 = xb[:, 1, :].rearrange("p (s t) -> p s t", s=NSEG, t=256)[:, sgs, :]
                extra = [fin_s5] if eng == "pool" else []
                P.op(eng, "tensor_tensor", [wre, e2k] + extra, [tkeys[0]], out=ta, in0=wr3, in1=E2r6, op=ALU.mult)
                P.op(eng, "tensor_tensor", [wim, e2k], [tkeys[1]], out=tb, in0=wi3, in1=E2i6, op=ALU.mult)
                P.op(eng, "tensor_tensor", [tkeys[0], tkeys[1]], [('S5X', 0, sp_)], out=xr3, in0=ta, in1=tb, op=ALU.subtract)
                P.op(eng, "tensor_tensor", [wre, e2k, tkeys[0]], [tkeys[0]], out=ta, in0=wr3, in1=E2i6, op=ALU.mult)
                P.op(eng, "tensor_tensor", [wim, e2k, tkeys[1]], [tkeys[1]], out=tb, in0=wi3, in1=E2r6, op=ALU.mult)
                P.op(eng, "tensor_tensor", [tkeys[0], tkeys[1]], [('S5X', 1, sp_)], out=xi3, in0=ta, in1=tb, op=ALU.add)

            def cmat(c, n_in_ot):
                j = c['j']
                for b_ in range(3):
                    bs = slice(b_ * 512, (b_ + 1) * 512)
                    MM(pacc[b_][:], CT[:, 0, j, :], xb[:, 0, bs], [wbc, ('S5X', 0, b_)], [pacc[b_]], start=(n_in_ot == 0), stop=False)
                    MM(pacc[b_][:], CT[:, 1, j, :], xb[:, 1, bs], [wbc, ('S5X', 1, b_)], [pacc[b_]], start=False, stop=(n_in_ot == 7))

            cs_ = [ctx(it) for it in items]
            tables(cs_[0])
            tables_b(cs_[0])
            rotate(cs_[0])
            for i, c in enumerate(cs_):
                if i + 1 < len(cs_):
                    tables(cs_[i + 1])
                scans(c)
                unrot(c, "pool", 0, T[4][:, 0:512], T[4][:, 512:1024], [('T4', 'D', 0), ('T4', 'D', 1)])
                unrot(c, "pool", 1, T[4][:, 0:512], T[4][:, 512:1024], [('T4', 'D', 0), ('T4', 'D', 1)])
                if i + 1 < len(cs_):
                    tables_b(cs_[i + 1])
                    rotate(cs_[i + 1])
                unrot(c, "dve", 2, rt1, rt2, [mws[0], mws[1]])
                cmat(c, i % 8)
                if i % 8 == 7:
                    ot = c['ot']
                    for b_ in range(3):
                        bs = slice(b_ * 512, (b_ + 1) * 512)
                        V("scalar_tensor_tensor", [T[0], pk, pacc[b_]], [('ymix', 4 + ot)], out=ymix[:, 4 + ot, bs], in0=ub[:, ot, bs], scalar=s5d[:, l, ot:ot + 1], in1=pacc[b_][:], op0=ALU.mult, op1=ALU.add)
            pb_rot[0] = None
            for sg in range(NSEG):
                for d in range(2):
                    dst = o_s5[sg, l, d].rearrange("(j p) r -> p j r", p=128)
                    P.dma("sp", R=[fin_s5], is_output=True, out=dst, in_=fin_s5[:, sg, d], allow_slow_non_contiguous=True)
            wg_ = load_w(glu_w, l * 256, 2, 0, 512)
            yk = [('ymix', 4), ('ymix', 5)]
            gl = T[5]
            for b_ in range(3):
                bs = slice(b_ * 512, (b_ + 1) * 512)
                pgs = []; pvs = []
                for ti in range(2):
                    pgs.append(dense_fm(wg_, 256 + ti * 128, b_, nk=2, rhs=ymix[:, 4:6, :], rkeys=yk))
                    pvs.append(dense_fm(wg_, ti * 128, b_, nk=2, rhs=ymix[:, 4:6, :], rkeys=yk))
                for ti in range(2):
                    sgm = gl[:, ti * 512:(ti + 1) * 512]
                    ACT(sgm, pgs[ti][:], AF.Sigmoid, [pgs[ti]], [gl])
                    V("tensor_tensor", [pvs[ti], gl], [('ymix', 4 + ti)], out=ymix[:, 4 + ti, bs], in0=sgm, in1=pvs[ti][:], op=ALU.mult)

        modulation_all(0)
        for l in range(depth):
            mod_finish(l)
            norm_mod(l, 0)
            if 'ssd' in enabled:
                attn_mixer(l, 'ssd')
            if 'ret' in enabled:
                attn_mixer(l, 'ret')
            if 's5' in enabled:
                s5_mixer(l)
            if 'lru' in enabled:
                lru_mixer(l)
            out_proj(l)
            if 'ffn' not in enabled and l + 1 < depth:
                modulation_all(l + 1)
            if 'ffn' in enabled:
                norm_mod(l, 1)
                pb_rot[0] = pb
                ffn(l)
                pb_rot[0] = None
                if l + 1 < depth:
                    mod_tail(l + 1)
                if len(enabled & {'ssd', 'ret', 's5', 'lru'}) < 4 and not DEBUG_TAPS:
                    V("memset", [], ALLY, ap=ymix[:], constant=0.0)
        final_out()
        if DEBUG_TAPS:
            d_mod = dout("dbg_mod", [128, 96])
            P.dma("sp", R=[modsb], is_output=True, out=d_mod, in_=modsb[:].rearrange("p a b -> p (a b)"))
            d_h = dout("dbg_h", [128, KT * NT])
            P.dma("pool", R=[('h', 0), ('h', 1), ('h', 2)], is_output=True, out=d_h, in_=h[:].rearrange("p a b -> p (a b)"))
            d_x = dout("dbg_x", [128, KT * NT])
            P.dma("sp", R=[('x', 0), ('x', 1), ('x', 2)], is_output=True, out=d_x, in_=x[:].rearrange("p a b -> p (a b)"))
            d_y = dout("dbg_ymix", [128, KT * NT])
            P.dma("pool", R=ALLY, is_output=True, out=d_y, in_=ymix[:].rearrange("p a b -> p (a b)"))
        P.run()
    return nc


_CACHE = {}


def kernel(**inp):
    inp = {k: np.asarray(v) for k, v in inp.items()}
    enabled = frozenset(ENABLED)
    depth = DEPTH
    cso = _consts()
    cst = cso.build()
    shared = _shared_layouts(inp)
    in_maps = []
    pko = None
    for core in range(8):
        pko = _core_pack(inp, core)
        (ka, ia), ib = _assign(core)
        if ka == 's':
            xa = inp['x_sample'][ia]
            ssd_i = _state_layout(inp['state_ssd'][ia], 'ssd')
            ret_i = _state_layout(inp['state_ret'][ia], 'ret')
        else:
            xa = inp['x_prompt'][ia].reshape(1024, D)
            ssd_i = np.zeros((L * 128, 256), np.float32)
            ret_i = np.zeros((L * 128, 256), np.float32)
        xb = inp['x_prompt'][ib].reshape(512, D)
        cos, sin = _rot_tables(ka == 's')
        m = dict(shared)
        m.update(xin=np.ascontiguousarray(np.concatenate([xa, xb], axis=0), dtype=np.float32), pk=pko.build(), cst=cst,
                 cosT=cos, sinT=sin, ssdinit=ssd_i, retinit=ret_i)
        in_maps.append(m)
    key = (enabled, depth, DEBUG_TAPS)
    if key not in _CACHE:
        _CACHE[key] = build(pko, cso, enabled, depth)
    nc = _CACHE[key]
    res = run_bass_kernel_spmd(nc, in_maps, core_ids=list(range(8)))
    outs = res.results
    LAST['outs'] = outs
    y_p = np.zeros((32, 256, D), np.float32)
    y_s = np.zeros((4, 1024, D), np.float32)
    n_ssd = np.zeros((32, L, 2, 4, 64, 64), np.float32)
    n_ret = np.zeros((32, L, 2, 4, 64, 64), np.float32)
    n_s5 = np.zeros((32, L, 2, 16, 64, 2), np.float32)
    n_lru = np.zeros((32, L, 2, 256), np.float32)
    for core in range(8):
        r = outs[core]
        (ka, ia), ib = _assign(core)
        y = r['y']
        segs = []
        if ka == 's':
            y_s[ia] = y[0:1024]
        else:
            for k, bidx in enumerate(ia):
                segs.append((k, bidx))
        for k, bidx in enumerate(ib):
            segs.append((4 + k, bidx))
        for sgi, bidx in segs:
            y_p[bidx] = y[sgi * 256:(sgi + 1) * 256]
            n_ssd[bidx] = r['o_ssd'][sgi]
            n_ret[bidx] = r['o_ret'][sgi]
            n_s5[bidx] = r['o_s5'][sgi].reshape(L, 2, 16, 64, 2)
            n_lru[bidx] = r['o_lru'][sgi]
    return (y_p, y_s, n_ssd, n_ret, n_s5, n_lru)
```
